# Optimizing a Trainium2 kernel written in Bass

```python
import jax, jax.numpy as jnp
from jax import lax
import numpy as np

D_MODEL = 1024
BATCH = 4
SEQ = 4096
DEPTH = 2

CHUNK = 64
BRANCH_WIDTH = D_MODEL // 2
N_BRANCHES = 3
SB_HEADS = 8
SB_HEAD_DIM = BRANCH_WIDTH // SB_HEADS
SB_BLOCK = 128
SGU_LEN = 128
SGU_GROUPS = 4
SGU_GROUP_DIM = BRANCH_WIDTH // SGU_GROUPS
CONV_WIDTH = 3
MEM_TOKENS = 256
XA_HEADS = 4
XA_HEAD_DIM = D_MODEL // XA_HEADS
FFN_HIDDEN = ((8 * D_MODEL // 3 + 255) // 256) * 256

W_QKV = 3 * BRANCH_WIDTH
W_SGU = 2 * BRANCH_WIDTH
W_CONV = 3 * BRANCH_WIDTH
W_GATES = N_BRANCHES * D_MODEL
IN_COLS = W_QKV + W_SGU + W_CONV + W_GATES
SPLIT_IDX = [BRANCH_WIDTH, 2 * BRANCH_WIDTH, W_QKV,
             W_QKV + W_SGU,
             W_QKV + W_SGU + BRANCH_WIDTH, W_QKV + W_SGU + 2 * BRANCH_WIDTH,
             W_QKV + W_SGU + W_CONV]

kernel_name = "hybrid_stickbreak_gmlp_shortconv_block"


def rms_norm(x, g, eps=1e-6):
    xf = x.astype(jnp.float32)
    y = xf * lax.rsqrt(jnp.mean(xf * xf, axis=-1, keepdims=True) + eps)
    return (y * g.astype(jnp.float32)).astype(x.dtype)


def layer_norm(x, g, b, eps=1e-5):
    xf = x.astype(jnp.float32)
    mu = jnp.mean(xf, axis=-1, keepdims=True)
    xc = xf - mu
    y = xc * lax.rsqrt(jnp.mean(xc * xc, axis=-1, keepdims=True) + eps)
    return (y * g.astype(jnp.float32) + b.astype(jnp.float32)).astype(x.dtype)


def stick_breaking_attention(q, k, v):
    seq = q.shape[2]
    scale = SB_HEAD_DIM ** -0.5
    outs = []
    for i in range(seq // SB_BLOCK):
        q0 = i * SB_BLOCK
        kend = q0 + SB_BLOCK
        qb = q[:, :, q0:kend].astype(jnp.float32)
        kb = k[:, :, :kend].astype(jnp.float32)
        z = jnp.einsum("bhqd,bhkd->bhqk", qb, kb) * scale
        t_pos = q0 + jnp.arange(SB_BLOCK)[:, None]
        s_pos = jnp.arange(kend)[None, :]
        valid = s_pos < t_pos
        log_1m = jnp.where(valid, jax.nn.log_sigmoid(-z), 0.0)
        log_a = jax.nn.log_sigmoid(z) + lax.cumsum(log_1m, axis=3, reverse=True) - log_1m
        a = jnp.where(valid, jnp.exp(log_a), 0.0)
        outs.append(jnp.einsum("bhqk,bhkd->bhqd", a.astype(v.dtype), v[:, :, :kend]))
    return jnp.concatenate(outs, axis=2)


def spatial_gating(z, ln_g, ln_b, w_s, b_s):
    bsz, seq, _ = z.shape
    u, v = jnp.split(z, 2, axis=-1)
    v = layer_norm(v, ln_g, ln_b)
    v = v.reshape(bsz, seq // SGU_LEN, SGU_LEN, SGU_GROUPS, SGU_GROUP_DIM)
    pos = jnp.arange(SGU_LEN)
    allowed = (pos[:, None] // CHUNK) >= (pos[None, :] // CHUNK)
    w = jnp.where(allowed[None], w_s, 0.0).astype(v.dtype)
    vm = jnp.einsum("gts,bnsgc->bntgc", w, v) + b_s.T[None, None, :, :, None].astype(v.dtype)
    return u * vm.reshape(bsz, seq, BRANCH_WIDTH)


def gated_short_conv(gate_b, gate_c, xin, conv_w):
    y = gate_c * xin
    ch = y.shape[-1]
    conv = lax.conv_general_dilated(
        y, conv_w[:, None, :].astype(y.dtype), window_strides=(1,),
        padding=((CONV_WIDTH - 1, 0),), dimension_numbers=("NWC", "WIO", "NWC"),
        feature_group_count=ch)
    return gate_b * conv


def hybrid_mixer(h, w_in, sgu_ln_g, sgu_ln_b, w_spatial, b_spatial, conv_w, w_branch, w_out):
    bsz, seq, _ = h.shape
    p = h @ w_in
    q, k, v, z, cb, cc, cx, gates = jnp.split(p, SPLIT_IDX, axis=-1)

    def heads(t):
        return t.reshape(bsz, seq, SB_HEADS, SB_HEAD_DIM).transpose(0, 2, 1, 3)

    ya = stick_breaking_attention(heads(q), heads(k), heads(v))
    ya = ya.transpose(0, 2, 1, 3).reshape(bsz, seq, BRANCH_WIDTH)
    yb = spatial_gating(jax.nn.gelu(z, approximate=False), sgu_ln_g, sgu_ln_b, w_spatial, b_spatial)
    yc = gated_short_conv(cb, cc, cx, conv_w)

    br = jnp.stack([ya, yb, yc], axis=2)
    br_d = jnp.einsum("bsnc,ncd->bsnd", br, w_branch)
    g = jax.nn.sigmoid(gates.reshape(bsz, seq, N_BRANCHES, D_MODEL))
    merged = jnp.sum(g * br_d, axis=2)
    return merged @ w_out


def memory_cross_attention(h, mem, mem_g, wq, wk, wv, wo):
    bsz, seq, _ = h.shape
    m = rms_norm(mem, mem_g)
    q = (h @ wq).reshape(bsz, seq, XA_HEADS, XA_HEAD_DIM)
    k = (m @ wk).reshape(bsz, MEM_TOKENS, XA_HEADS, XA_HEAD_DIM)
    v = (m @ wv).reshape(bsz, MEM_TOKENS, XA_HEADS, XA_HEAD_DIM)
    s = jnp.einsum("bqhd,bkhd->bhqk", q.astype(jnp.float32), k.astype(jnp.float32)) * (XA_HEAD_DIM ** -0.5)
    pr = jax.nn.softmax(s, axis=-1)
    o = jnp.einsum("bhqk,bkhd->bqhd", pr.astype(v.dtype), v).reshape(bsz, seq, D_MODEL)
    return o @ wo


def swiglu(h, w_gate, w_up, w_down):
    return (jax.nn.silu(h @ w_gate) * (h @ w_up)) @ w_down


def setup_inputs(seed: int = 0) -> dict:
    key = jax.random.key(seed)
    ks = jax.random.split(key, 24)
    f32 = jnp.float32

    def nrm(k, shape, fan_in):
        return jax.random.normal(k, shape, f32) * (fan_in ** -0.5)

    def gain(k, shape):
        return 1.0 + 0.02 * jax.random.normal(k, shape, f32)

    L, D, W = DEPTH, D_MODEL, BRANCH_WIDTH
    return {
        "x": jax.random.normal(ks[0], (BATCH, SEQ, D), f32),
        "mem": jax.random.normal(ks[1], (BATCH, MEM_TOKENS, D), f32),
        "norm_mix_g": gain(ks[2], (L, D)),
        "w_in": nrm(ks[3], (L, D, IN_COLS), D),
        "sgu_ln_g": gain(ks[4], (L, W)),
        "sgu_ln_b": 0.02 * jax.random.normal(ks[5], (L, W), f32),
        "w_spatial": nrm(ks[6], (L, SGU_GROUPS, SGU_LEN, SGU_LEN), SGU_LEN),
        "b_spatial": gain(ks[7], (L, SGU_GROUPS, SGU_LEN)),
        "conv_w": nrm(ks[8], (L, CONV_WIDTH, W), CONV_WIDTH),
        "w_branch": nrm(ks[9], (L, N_BRANCHES, W, D), W),
        "w_out": nrm(ks[10], (L, D, D), D),
        "norm_xa_g": gain(ks[11], (L, D)),
        "mem_norm_g": gain(ks[12], (L, D)),
        "w_q_xa": nrm(ks[13], (L, D, D), D),
        "w_k_xa": nrm(ks[14], (L, D, D), D),
        "w_v_xa": nrm(ks[15], (L, D, D), D),
        "w_o_xa": nrm(ks[16], (L, D, D), D),
        "norm_ffn_g": gain(ks[17], (L, D)),
        "w_gate_ffn": nrm(ks[18], (L, D, FFN_HIDDEN), D),
        "w_up_ffn": nrm(ks[19], (L, D, FFN_HIDDEN), D),
        "w_down_ffn": nrm(ks[20], (L, FFN_HIDDEN, D), FFN_HIDDEN),
        "final_g": gain(ks[21], (D,)),
    }


def reference(x, mem, norm_mix_g, w_in, sgu_ln_g, sgu_ln_b, w_spatial, b_spatial, conv_w,
              w_branch, w_out, norm_xa_g, mem_norm_g, w_q_xa, w_k_xa, w_v_xa, w_o_xa,
              norm_ffn_g, w_gate_ffn, w_up_ffn, w_down_ffn, final_g):
    for l in range(DEPTH):
        x = x + hybrid_mixer(rms_norm(x, norm_mix_g[l]), w_in[l], sgu_ln_g[l], sgu_ln_b[l],
                             w_spatial[l], b_spatial[l], conv_w[l], w_branch[l], w_out[l])
        x = x + memory_cross_attention(rms_norm(x, norm_xa_g[l]), mem, mem_norm_g[l],
                                       w_q_xa[l], w_k_xa[l], w_v_xa[l], w_o_xa[l])
        x = x + swiglu(rms_norm(x, norm_ffn_g[l]), w_gate_ffn[l], w_up_ffn[l], w_down_ffn[l])
    return rms_norm(x, final_g)
```

```python
import numpy as np
import concourse.bass as bass
import concourse.mybir as mybir
from concourse.bass_utils import run_bass_kernel_spmd

F32 = mybir.dt.float32
BF16 = mybir.dt.bfloat16
AF = mybir.ActivationFunctionType
ALU = mybir.AluOpType

L = 2
D = 1024
T = 2048
NB = 16
KC = 8
FH = 2816
NJ = 22
INC = 7168
PAIRS = [[0, 1], [2, 3], [4, 5], [6, 7]]

PV_MIXG = 0
PV_XAG = 16
PV_MEMG = 32
PV_FFNG = 48
PV_FING = 64
PV_CONV = 72
PV_SEL = 96
NPV = 98
CS_NEGTRI = 0
CS_SGUM = 128
CS_AMASK = 256
NCST = 768
BC_LNG = 0
BC_LNB = 512
BC_BSP = 1024
NBC = 3072


def gblock(g):
    j, i = divmod(g, 4)
    return [(0, 2 * j), (1, 2 * j), (1, 2 * j + 1), (0, 2 * j + 1)][i]


class Sched:
    def __init__(self, nc, sems):
        self.nc = nc
        self.sems = sems
        self.engs = ["pe", "act", "dve", "pool", "sp"]
        self.ops = {e: [] for e in self.engs}
        self.count = {e: 0 for e in self.engs}
        self.waited = {e: {} for e in self.engs}
        self.res = {}
        self.dq = {"sp": [f"D_sp{i}" for i in range(8)], "pool": [f"D_pl{i}" for i in range(8)]}
        self.dnext = {"sp": 0, "pool": 0}
        self.dcnt = {}
        self.pending_barrier = {e: {} for e in self.engs}
        self.cc_n = 0

    def _r(self, k):
        if k not in self.res:
            self.res[k] = {"w": None, "r": {}}
        return self.res[k]

    def add(self, eng, fn, reads=(), writes=(), kind="c"):
        deps = {}

        def need(tok):
            if tok is None:
                return
            s, v = tok
            if deps.get(s, 0) < v:
                deps[s] = v

        for k in reads:
            need(self._r(k)["w"])
        for k in writes:
            r = self._r(k)
            need(r["w"])
            for s, v in r["r"].items():
                need((s, v))
        for s, v in self.pending_barrier[eng].items():
            need((s, v))
        self.pending_barrier[eng] = {}
        if kind == "d":
            q = self.dq[eng]
            dsem = q[self.dnext[eng] % len(q)]
            self.dnext[eng] += 1
            prev = self.dcnt.get(dsem, 0)
            if prev:
                need((dsem, prev))
            self.dcnt[dsem] = prev + 16
            tok = (dsem, prev + 16)
            inc = (dsem, 16)
        elif kind == "cc":
            name = f"CC{self.cc_n}"
            self.cc_n += 1
            tok = (name, 1)
            inc = (name, None)
        else:
            self.count[eng] += 1
            tok = (f"S_{eng}", self.count[eng])
            inc = (f"S_{eng}", 1)
        waits = []
        for s, v in deps.items():
            if eng == "pe" and s == "S_pe":
                continue
            if self.waited[eng].get(s, 0) >= v:
                continue
            self.waited[eng][s] = v
            waits.append((s, v))
        self.ops[eng].append((waits, fn, inc))
        for k in reads:
            r = self._r(k)
            if r["r"].get(tok[0], 0) < tok[1]:
                r["r"][tok[0]] = tok[1]
        for k in writes:
            r = self._r(k)
            r["w"] = tok
            r["r"] = {}
        return tok

    def barrier(self):
        snap = {}
        for e in ["pe", "act", "dve", "pool"]:
            if self.count[e]:
                snap[f"S_{e}"] = self.count[e]
        for s, v in self.dcnt.items():
            snap[s] = v
        for i in range(self.cc_n):
            snap[f"CC{i}"] = 1
        for e in self.engs:
            self.pending_barrier[e] = dict(snap)

    def emit(self, eng, handle):
        for waits, fn, inc in self.ops[eng]:
            for s, v in waits:
                handle.wait_ge(self.sems[s], v)
            ins = fn(handle)
            if inc[1] is None:
                ins.then_inc(self.sems[inc[0]])
            else:
                ins.then_inc(self.sems[inc[0]], inc[1])

    def final_wait(self, eng):
        self.barrier()
        deps = self.pending_barrier[eng]
        waits = [(s, v) for s, v in deps.items() if self.waited[eng].get(s, 0) < v]
        return waits


def build_program(stop_after=None, layers=(0, 1), final=True):
    nc = bass.Bass("TRN2", target_bir_lowering=False)

    def din(name, shape):
        return nc.dram_tensor(name, list(shape), F32, kind="ExternalInput").ap()

    xT_d = din("xT", [D, T])
    memT_d = din("memT", [D, 256])
    pvec_d = din("pvec", [128, NPV])
    cst_d = din("cst", [128, NCST])
    bc_d = din("bc", [128, NBC])
    wspT_d = din("wspT", [128, L * 4 * 128])
    w_in_d = din("w_in", [L, D, INC])
    w_br_d = din("w_branch", [L, 3, 512, D])
    w_out_d = din("w_out", [L, D, D])
    wq_d = din("w_q_xa", [L, D, D])
    wk_d = din("w_k_xa", [L, D, D])
    wv_d = din("w_v_xa", [L, D, D])
    wo_d = din("w_o_xa", [L, D, D])
    wg_d = din("w_gate_ffn", [L, D, FH])
    wu_d = din("w_up_ffn", [L, D, FH])
    wd_d = din("w_down_ffn", [L, FH, D])
    yT_d = nc.dram_tensor("yT", [D, T], F32, kind="ExternalOutput").ap()

    scratch = {}
    for l_ in range(L):
        scratch[l_] = dict(
            kt_src=nc.dram_tensor(f"kt_src{l_}", [NB * 128, 512], BF16),
            kt_all=nc.dram_tensor(f"kt_all{l_}", [2 * NB * 128, 512], BF16),
            v_src=nc.dram_tensor(f"v_src{l_}", [NB * 128, 512], BF16),
            v_all=nc.dram_tensor(f"v_all{l_}", [2 * NB * 128, 512], BF16),
            tl_src=nc.dram_tensor(f"tl_src{l_}", [128, 128], BF16),
            tl_all=nc.dram_tensor(f"tl_all{l_}", [256, 128], BF16))

    sem_names = ["S_pe", "S_act", "S_dve", "S_pool"] + [f"D_sp{i}" for i in range(8)] + \
                [f"D_pl{i}" for i in range(8)] + [f"CC{i}" for i in range(3 * L)]

    import contextlib
    with contextlib.ExitStack() as es:
        xres_t = es.enter_context(nc.sbuf_tensor("xres", [128, KC, T], F32))
        ARENA_B = 122 * 1024
        arena = es.enter_context(nc.sbuf_tensor("arena", [128, ARENA_B // 2], BF16))
        pvec = es.enter_context(nc.sbuf_tensor("pvec_sb", [128, NPV], F32))
        cst = es.enter_context(nc.sbuf_tensor("cst_sb", [128, NCST], F32))
        bcp = es.enter_context(nc.sbuf_tensor("bcp", [128, 1536], F32))
        cb16 = es.enter_context(nc.sbuf_tensor("cb16", [128, 1280], BF16))
        wmT = es.enter_context(nc.sbuf_tensor("wmT_sb", [128, L * 4 * 128], BF16))
        small = es.enter_context(nc.sbuf_tensor("small", [128, 64], F32))
        psum = es.enter_context(nc.psum_tensor("ps", [128, 8, 512], F32))
        sems = {n: es.enter_context(nc.semaphore(n)) for n in sem_names}
        block = es.enter_context(nc.Block())

        S = Sched(nc, sems)
        xres = xres_t

        negtri_b = cb16[:, 0:128]
        negones_b = cb16[:, 128:256]
        ones_b = cb16[:, 256:384]
        amask_b = [[cb16[:, 384 + (p * 2 + w) * 128: 384 + (p * 2 + w + 1) * 128] for w in range(2)] for p in range(2)]
        sgum_b = cb16[:, 896:1024]

        def carve(off, shape, dt):
            es_ = 2 if dt == BF16 else 4
            n = int(np.prod(shape[1:]))
            assert off % 4 == 0 and off + n * es_ <= ARENA_B, (off, shape)
            ap = arena[:, off // 2: off // 2 + n * es_ // 2]
            if dt == F32:
                ap = ap.bitcast(F32)
            if len(shape) == 3:
                ap = ap.rearrange("p (a b) -> p a b", b=shape[2])
            elif len(shape) == 4:
                ap = ap.rearrange("p (a b c) -> p a b c", b=shape[2], c=shape[3])
            elif len(shape) == 5:
                ap = ap.rearrange("p (a b c d) -> p a b c d", b=shape[2], c=shape[3], d=shape[4])
            return ap

        PSB = lambda b: psum[:, b, :]
        evac_flip = [0]

        def mm(out, lhsT, rhs, start, stop, reads, writes):
            S.add("pe", lambda e: e.matmul(out, lhsT, rhs, start=start, stop=stop), reads, writes)

        def act(out, in_, func, reads, writes, scale=1.0, bias=0.0, accum=None):
            kw = {"scale": scale}
            if bias != 0.0:
                kw["bias"] = bias
            if accum is not None:
                kw["accum_out"] = accum
            S.add("act", lambda e: e.activation(out, in_, func, **kw), reads, writes)

        def tt(out, in0, in1, op, reads, writes, eng="dve"):
            S.add(eng, lambda e: e.tensor_tensor(out, in0, in1, op), reads, writes)

        def ts(out, in0, s1, s2, op0, op1, reads, writes, eng="dve"):
            if op1 is None:
                S.add(eng, lambda e: e.tensor_scalar(out, in0, s1, None, op0), reads, writes)
            else:
                S.add(eng, lambda e: e.tensor_scalar(out, in0, s1, s2, op0, op1), reads, writes)

        def stt(out, in0, scalar, in1, op0, op1, reads, writes):
            S.add("dve", lambda e: e.scalar_tensor_tensor(out, in0, scalar, in1, op0, op1), reads, writes)

        def cp(out, in_, reads, writes, eng=None):
            if eng is None:
                eng = "act" if evac_flip[0] % 2 == 0 else "dve"
                evac_flip[0] += 1
            if eng == "act":
                S.add("act", lambda e: e.activation(out, in_, AF.Copy), reads, writes)
            else:
                S.add(eng, lambda e: e.tensor_copy(out, in_), reads, writes)

        def dma(q, out, in_, reads, writes):
            S.add(q, lambda e: e.dma_start(out=out, in_=in_), reads, writes, kind="d")

        def allgather(src, dst, reads, writes):
            import os
            if os.environ.get("NO_CC"):
                S.add("pool", lambda e: e.dma_start(out=dst.ap()[0:src.shape[0], :], in_=src.ap()), reads, writes, kind="d")
                return
            S.add("pool", lambda e: e.collective_compute(
                "AllGather", ALU.bypass, replica_groups=PAIRS, ins=[src.ap().opt()], outs=[dst.ap().opt()]),
                reads, writes, kind="cc")

        xT_v = xT_d.rearrange("(c p) t -> p c t", p=128)
        for c in range(KC):
            dma("sp", xres[:, c, :], xT_v[:, c, :], [], [f"x{c}"])
        dma("sp", pvec[:], pvec_d[:, :], [], ["pvec"])
        dma("sp", cst[:], cst_d[:, :], [], ["cst"])
        cp(negtri_b, cst[:, CS_NEGTRI:CS_NEGTRI + 128], ["cst"], ["cb16"], eng="dve")
        S.add("dve", lambda e: e.memset(negones_b, -1.0), [], ["cb16"])
        S.add("dve", lambda e: e.memset(ones_b, 1.0), [], ["cb16"])
        S.add("dve", lambda e: e.memset(small[:, 62:63], -0.5), [], ["smc"])
        cp(cb16[:, 384:896], cst[:, CS_AMASK:CS_AMASK + 512], ["cst"], ["cb16"], eng="dve")
        wsp_stage = carve(0, [128, L * 4 * 128], F32)
        dma("sp", wsp_stage, wspT_d[:, :], [], ["wsp_stage"])
        tt(wmT[:].rearrange("p (a t) -> p a t", t=128), wsp_stage.rearrange("p (a t) -> p a t", t=128),
           cst[:, CS_SGUM:CS_SGUM + 128].unsqueeze(1).broadcast_to([128, L * 4, 128]), ALU.mult,
           ["wsp_stage", "cst"], ["wmT"])
        S.barrier()

        XR = [f"x{c}" for c in range(KC)]

        def rms_norm_tile(src3, ncols, gbase, hT_out3, hres, sq, rstd, lnv, psb, srcres, out_f32=False):
            act(sq, src3, AF.Square, srcres, ["sqA", "sqB"])
            for c in range(KC):
                mm(PSB(psb)[:, 0:ncols], ones_b, sq[:, c, :], c == 0, c == KC - 1, ["sqA", "sqB", "cb16"], [f"ps{psb}"])
            act(lnv, PSB(psb)[:, 0:ncols], AF.Ln, [f"ps{psb}"], ["lnv"], scale=1.0 / D, bias=1e-6)
            act(rstd, lnv, AF.Exp, ["lnv"], ["rstd"], scale=-0.5)
            for c in range(KC):
                stt(hT_out3[:, c, :], src3[:, c, :], pvec[:, gbase + c: gbase + c + 1], rstd, ALU.mult, ALU.mult,
                    srcres + ["rstd", "pvec"], [hres])

        def load_w(q, wb, wres, src_ap):
            dma(q, wb, src_ap, [], [wres])

        for l in layers:
            w_in_v = w_in_d[l].rearrange("(k p) c -> p k c", p=128)
            kt_src, kt_all, v_src, v_all, tl_src, tl_all = [scratch[l][k_] for k_ in
                                                            ("kt_src", "kt_all", "v_src", "v_all", "tl_src", "tl_all")]

            hT = carve(0, [128, KC, T], BF16)
            qT = carve(32768, [128, 4, T], BF16)
            wbuf = [carve(49152 + i * 8192, [128, KC, 512], BF16) for i in range(2)]
            sq = carve(65536, [128, KC, 512], BF16)
            rstd = carve(73728, [128, 512], F32)
            lnv = carve(75776, [128, 512], F32)
            kst = [carve(77824 + i * 4096, [128, T], BF16) for i in range(2)]
            vst = [carve(86016 + i * 4096, [128, 4, 512], BF16) for i in range(2)]
            tl_sb = carve(94208, [128, 128], BF16)
            cc_sb = carve(94464, [128, 128], F32)
            wcnt = [0]

            def next_w(col0, ncol=512, src=None):
                i = wcnt[0] % 2
                wcnt[0] += 1
                srcv = w_in_v if src is None else src
                load_w("pool", wbuf[i][:, :, 0:ncol], f"wbuf{i}", srcv[:, :, col0:col0 + ncol])
                return wbuf[i], f"wbuf{i}"

            wb5 = [wbuf[0], wbuf[1]] + [carve(95232 + i * 8192, [128, KC, 512], BF16) for i in range(3)]
            rb5 = ["wbuf0", "wbuf1", "wb5_2", "wb5_3", "wb5_4"]
            for i_, c0_ in enumerate((3072, 3584, 512, 1024, 0)):
                load_w("pool", wb5[i_], rb5[i_], w_in_v[:, :, c0_:c0_ + 512])
            for tt_ in range(4):
                sl = slice(tt_ * 512, (tt_ + 1) * 512)
                rms_norm_tile(xres[:, :, sl], 512, PV_MIXG + l * 8, hT[:, :, sl], f"hT{tt_}", sq, rstd, lnv, 7, XR)
            HT = [f"hT{i}" for i in range(4)]

            wcc, rcc = wb5[0], rb5[0]
            wcx, rcx = wb5[1], rb5[1]
            wk_, rk_ = wb5[2], rb5[2]
            kt_dst = kt_src.ap().rearrange("(m p) (c k) -> p m c k", p=128, k=128)
            pb = 0
            for c in range(4):
                ks = kst[c % 2]
                for tt_ in range(4):
                    sl = slice(tt_ * 512, (tt_ + 1) * 512)
                    for k in range(KC):
                        mm(PSB(pb), wk_[:, k, c * 128:(c + 1) * 128], hT[:, k, sl], k == 0, k == KC - 1,
                           [f"hT{tt_}", rk_], [f"ps{pb}"])
                    cp(ks[:, sl], PSB(pb), [f"ps{pb}"], [f"kst{c % 2}"])
                    pb = (pb + 1) % 4
                dma("sp", kt_dst[:, :, c, :], ks.rearrange("p (m k) -> p m k", k=128), [f"kst{c % 2}"], ["kt_src"])
            allgather(kt_src, kt_all, ["kt_src"], ["kt_all"])

            wv_, rv_ = wb5[3], rb5[3]
            v_dst = v_src.ap().rearrange("(m p) c -> p m c", p=128)
            for m in range(NB):
                vs = vst[(m // 4) % 2]
                for k in range(KC):
                    mm(PSB(pb), hT[:, k, m * 128:(m + 1) * 128], wv_[:, k, :], k == 0, k == KC - 1,
                       [f"hT{m // 4}", rv_], [f"ps{pb}"])
                cp(vs[:, m % 4, :], PSB(pb), [f"ps{pb}"], [f"vst{(m // 4) % 2}"])
                pb = (pb + 1) % 4
                if m % 4 == 3:
                    dma("sp", v_dst[:, m - 3:m + 1, :], vs, [f"vst{(m // 4) % 2}"], ["v_src"])
            allgather(v_src, v_all, ["v_src"], ["v_all"])

            hT_tail = [hT[:, k, :].rearrange("p (m t) -> p m t", t=128)[:, :, 126:128] for k in range(KC)]
            for (wb_, rb_, half) in ((wcc, rcc, 0), (wcx, rcx, 1)):
                for c in range(4):
                    for k in range(KC):
                        mm(PSB(6)[:, half * 128 + c * 32: half * 128 + (c + 1) * 32], wb_[:, k, c * 128:(c + 1) * 128],
                           hT_tail[k], k == 0, k == KC - 1, HT + [rb_], ["ps6"])
            cp(cc_sb, PSB(6)[:, 0:128], ["ps6"], ["cc_sb"], eng="act")
            tt(tl_sb, cc_sb, PSB(6)[:, 128:256], ALU.mult, ["cc_sb", "ps6"], ["tl_sb"])
            dma("sp", tl_src[:, :], tl_sb, ["tl_sb"], ["tl_src"])
            allgather(tl_src, tl_all, ["tl_src"], ["tl_all"])

            wq_, rq_ = wb5[4], rb5[4]
            for c in range(4):
                for tt_ in range(4):
                    sl = slice(tt_ * 512, (tt_ + 1) * 512)
                    for k in range(KC):
                        mm(PSB(pb), wq_[:, k, c * 128:(c + 1) * 128], hT[:, k, sl], k == 0, k == KC - 1,
                           [f"hT{tt_}", rq_], [f"ps{pb}"])
                    if (c * 4 + tt_) % 2 == 0:
                        act(qT[:, c, sl], PSB(pb), AF.Copy, [f"ps{pb}"], ["qT"], scale=0.125)
                    else:
                        ts(qT[:, c, sl], PSB(pb), 0.125, None, ALU.mult, None, [f"ps{pb}"], ["qT"])
                    pb = (pb + 1) % 4
            S.barrier()

            if stop_after == f"qkv{l}":
                break
            KTr = carve(49152, [128, 2 * NB, 512], BF16)
            Vr = carve(81920, [128, 2 * NB, 512], BF16)
            yaT = carve(0, [128, 4, T], BF16)
            Ebuf = [carve(16384, [128, 1024], F32), carve(118784, [128, 1024], F32)]
            Lsp = [carve(20480 + i * 2048, [128, 1024], BF16) for i in range(3)]
            Abuf = [carve(26624 + i * 2048, [128, 1024], BF16) for i in range(2)]
            Rb = [carve(114688 + i * 2048, [128, 1024], BF16) for i in range(2)]
            yatmp = carve(30720, [128, 4, 128], BF16)
            kt_v = kt_all.ap().rearrange("(b p) x -> p b x", p=128)
            v_v = v_all.ap().rearrange("(b p) x -> p b x", p=128)
            for i in range(4):
                dma("sp", KTr[:, i * 8:(i + 1) * 8, :], kt_v[:, i * 8:(i + 1) * 8, :], ["kt_all"], ["KT"])
            for i in range(4):
                dma("sp", Vr[:, i * 8:(i + 1) * 8, :], v_v[:, i * 8:(i + 1) * 8, :], ["v_all"], ["V"])

            steps = []
            for m in range(NB):
                G = 4 * (m // 2) + (1 if m % 2 == 0 else 3)
                for si, g in enumerate(range(G, -1, -1)):
                    r, ml = gblock(g)
                    steps.append(dict(m=m, si=si, g=g, kb=r * NB + ml, last=(g == 0), par=m % 2))
            NS = len(steps)
            Zb = lambda s: psum[:, (s % 2) * 2:(s % 2) * 2 + 2, :].rearrange("p a b -> p (a b)")
            Zres = lambda s: f"Z{s % 2}"
            Cb = psum[:, 4:6, :].rearrange("p a b -> p (a b)")
            Ob = psum[0:64, 6:8, :].rearrange("p a b -> p (a b)")

            def qk(out1024, st, start_flag, wres):
                m, kb = st["m"], st["kb"]
                for c in range(4):
                    for hh in range(2):
                        hp = hh * 4 + c
                        mm(out1024[:, hp * 128:(hp + 1) * 128],
                           KTr[hh * 64:(hh + 1) * 64, kb, c * 128:(c + 1) * 128],
                           qT[hh * 64:(hh + 1) * 64, c, m * 128:(m + 1) * 128],
                           start_flag, (True if start_flag else c == 3), ["KT", "qT"], [wres])

            def front(s):
                st = steps[s]
                qk(Zb(s), st, True, Zres(s))

            def mid0(s):
                act(Ebuf[s % 2], Zb(s), AF.Exp, [Zres(s)], [f"E{s % 2}"])

            def mid(s):
                st = steps[s]
                si = st["si"]
                lb = Lsp[s % 3]
                act(lb, Ebuf[s % 2], AF.Ln, [f"E{s % 2}"], [f"L{s % 3}"], bias=1.0)
                if si < 2:
                    mk = amask_b[st["par"]][1 - si]
                    tt(lb.rearrange("p (h t) -> p h t", t=128), lb.rearrange("p (h t) -> p h t", t=128),
                       mk.unsqueeze(1).broadcast_to([128, 8, 128]), ALU.mult, [f"L{s % 3}", "cb16"], [f"L{s % 3}"])
                if si == 0:
                    st["carry"] = None
                elif si == 1:
                    st["carry"] = (Lsp[(s - 1) % 3], f"L{(s - 1) % 3}")
                else:
                    prev = steps[s - 1]["carry"]
                    rb = Rb[s % 2]
                    tt(rb, prev[0], Lsp[(s - 1) % 3], ALU.add, [prev[1], f"L{(s - 1) % 3}"], [f"R{s % 2}"])
                    st["carry"] = (rb, f"R{s % 2}")

            def back1(s):
                st = steps[s]
                lb = Lsp[s % 3]
                for hb in range(2):
                    cs = slice(hb * 512, (hb + 1) * 512)
                    mm(Cb[:, cs], negtri_b, lb[:, cs], True, False, [f"L{s % 3}", "cb16"], ["C"])
                    if st["carry"] is not None:
                        mm(Cb[:, cs], negones_b, st["carry"][0][:, cs], False, False, [st["carry"][1], "cb16"], ["C"])
                qk(Cb, st, False, "C")

            def back2(s):
                st = steps[s]
                ab = Abuf[s % 2]
                act(ab, Cb, AF.Exp, ["C"], [f"A{s % 2}"])
                if st["si"] < 2:
                    mk = amask_b[st["par"]][1 - st["si"]]
                    tt(ab.rearrange("p (h t) -> p h t", t=128), ab.rearrange("p (h t) -> p h t", t=128),
                       mk.unsqueeze(1).broadcast_to([128, 8, 128]), ALU.mult, [f"A{s % 2}", "cb16"], [f"A{s % 2}"])

            def back3(s):
                st = steps[s]
                ab = Abuf[s % 2]
                m, kb = st["m"], st["kb"]
                for hp in range(8):
                    hh, c = divmod(hp, 4)
                    h = 2 * c + hh
                    mm(Ob[:, hp * 128:(hp + 1) * 128], Vr[:, kb, h * 64:(h + 1) * 64], ab[:, hp * 128:(hp + 1) * 128],
                       st["si"] == 0 and c == 0, st["last"] and c == 3, [f"A{s % 2}", "V"], ["O"])
                if st["last"]:
                    ov = Ob.rearrange("p (hh c t) -> p hh c t", hh=2, c=4)
                    S.add("dve", lambda e: e.tensor_copy(yaT[0:64, :, m * 128:(m + 1) * 128], ov[:, 0, :, :]),
                          ["O"], ["yaT"])
                    S.add("dve", lambda e: e.tensor_copy(yatmp[0:64], ov[:, 1, :, :]), ["O"], ["yatmp"])
                    dma("sp", yaT[64:128, :, m * 128:(m + 1) * 128], yatmp[0:64], ["yatmp"], ["yaT"])

            ok = lambda i: 0 <= i < NS
            for s in range(-2, NS + 2):
                if ok(s - 1):
                    back1(s - 1)
                if ok(s - 2):
                    back3(s - 2)
                if ok(s + 2):
                    front(s + 2)
                if ok(s + 1):
                    mid0(s + 1)
                if ok(s):
                    mid(s)
                if ok(s - 1):
                    back2(s - 1)
            S.barrier()

            if stop_after == f"attn{l}":
                break

            dma("sp", bcp[:], bc_d[:, l * 1536:(l + 1) * 1536], [], ["bcp"])
            for hf in range(2):
                t0 = hf * 1024
                hTh = carve(16384, [128, KC, 1024], BF16)
                ybT = carve(32768, [128, 4, 1024], BF16)
                ycT = carve(40960, [128, 4, 1024], BF16)
                wbuf = [carve(49152 + i * 8192, [128, KC, 512], BF16) for i in range(2)]
                sq = carve(65536, [128, KC, 512], BF16)
                rstd = carve(73728, [128, 512], F32)
                lnv = carve(75776, [128, 512], F32)
                mrgT = carve(77824, [128, KC, 1024], BF16)
                yext = carve(94208, [128, 4, 8, 130], BF16)
                cbT = carve(102528, [128, 4, 1024], BF16)
                gv4 = carve(110720, [128, 4, 512], F32)
                vn4 = carve(118912, [128, 4, 512], BF16)
                tls = carve(123008, [128, 2, 4, NB, 2], BF16)
                gv, tmpf, ctmp = gv4[:, 0, :], gv4[:, 1, :], gv4[:, 2, :]
                sg = [carve(118912, [128, 512], F32)]
                tfb = [(rstd, "rstd"), (lnv, "lnv")]
                wcnt = [0]
                for tt_ in range(2):
                    sl = slice(tt_ * 512, (tt_ + 1) * 512)
                    gsl = slice(t0 + tt_ * 512, t0 + (tt_ + 1) * 512)
                    rms_norm_tile(xres[:, :, gsl], 512, PV_MIXG + l * 8, hTh[:, :, sl], f"hTh{tt_}", sq, rstd, lnv, 7, XR)
                HTH = ["hTh0", "hTh1"]
                pb = 0
                wz, rz = next_w(1536)
                for c in range(4):
                    for tt_ in range(2):
                        sl = slice(tt_ * 512, (tt_ + 1) * 512)
                        for k in range(KC):
                            mm(PSB(pb), wz[:, k, c * 128:(c + 1) * 128], hTh[:, k, sl], k == 0, k == KC - 1,
                               [f"hTh{tt_}", rz], [f"ps{pb}"])
                        act(ybT[:, c, sl], PSB(pb), AF.Gelu, [f"ps{pb}"], ["ybT"])
                        pb = (pb + 1) % 4
                if hf == 0:
                    dma("sp", tls.rearrange("p r c m t -> p r (c m t)"),
                        tl_all.ap().rearrange("(r p) x -> p r x", p=128), ["tl_all"], ["tls"])
                wcb, rcb = next_w(2560)
                for c in range(4):
                    for tt_ in range(2):
                        sl = slice(tt_ * 512, (tt_ + 1) * 512)
                        for k in range(KC):
                            mm(PSB(pb), wcb[:, k, c * 128:(c + 1) * 128], hTh[:, k, sl], k == 0, k == KC - 1,
                               [f"hTh{tt_}", rcb], [f"ps{pb}"])
                        cp(cbT[:, c, sl], PSB(pb), [f"ps{pb}"], ["cbT"])
                        pb = (pb + 1) % 4
                wcc, rcc = next_w(3072)
                wcx, rcx = next_w(3584)
                for c in range(4):
                    for tt_ in range(2):
                        sl = slice(tt_ * 512, (tt_ + 1) * 512)
                        p1 = pb
                        p2 = (pb + 1) % 4
                        pb = (pb + 2) % 4
                        for k in range(KC):
                            mm(PSB(p1), wcc[:, k, c * 128:(c + 1) * 128], hTh[:, k, sl], k == 0, k == KC - 1,
                               [f"hTh{tt_}", rcc], [f"ps{p1}"])
                        for k in range(KC):
                            mm(PSB(p2), wcx[:, k, c * 128:(c + 1) * 128], hTh[:, k, sl], k == 0, k == KC - 1,
                               [f"hTh{tt_}", rcx], [f"ps{p2}"])
                        cp(ctmp, PSB(p1), [f"ps{p1}"], ["gv4_2"], eng="act")
                        tt(yext[:, c, tt_ * 4:(tt_ + 1) * 4, 2:130], ctmp.rearrange("p (m t) -> p m t", t=128),
                           PSB(p2).rearrange("p (m t) -> p m t", t=128), ALU.mult, ["gv4_2", f"ps{p2}"], ["yext"])
                wz2, rz2 = next_w(2048)
                smc = lambda o, n=4: small[:, o:o + n]
                for grp in range(2):
                    for b_ in range(4):
                        mb = grp * 4 + b_
                        bs = slice(mb * 128, (mb + 1) * 128)
                        for k in range(KC):
                            mm(PSB(b_), hTh[:, k, bs], wz2[:, k, :], k == 0, k == KC - 1, [f"hTh{grp}", rz2], [f"ps{b_}"])
                        act(gv4[:, b_, :], PSB(b_), AF.Gelu, [f"ps{b_}"], [f"gv4_{b_}", "sm_sum"], accum=small[:, b_:b_ + 1])
                    for b_ in range(4):
                        act(vn4[:, b_, :], gv4[:, b_, :], AF.Square, [f"gv4_{b_}"], ["vn4", "sm_sq"], accum=small[:, 4 + b_:5 + b_])
                    ts(smc(8), smc(0), -1.0 / 512, None, ALU.mult, None, ["sm_sum"], ["sm_nm"])
                    tt(smc(12), smc(8), smc(8), ALU.mult, ["sm_nm"], ["sm_m2"])
                    ts(smc(16), smc(4), 1.0 / 512, 1e-5, ALU.mult, ALU.add, ["sm_sq"], ["sm_ve"])
                    tt(smc(20), smc(16), smc(12), ALU.subtract, ["sm_ve", "sm_m2"], ["sm_var"])
                    S.add("pool", lambda e: e.tensor_tensor(smc(24), smc(20), small[:, 62:63].broadcast_to([128, 4]), ALU.pow),
                          ["sm_var", "smc"], ["sm_rstd"])
                    for b_ in range(4):
                        ts(gv4[:, b_, :], gv4[:, b_, :], small[:, 8 + b_:9 + b_], small[:, 24 + b_:25 + b_], ALU.add, ALU.mult,
                           [f"gv4_{b_}", "sm_nm", "sm_rstd"], [f"gv4_{b_}"])
                    G4 = [f"gv4_{i}" for i in range(4)]
                    tt(gv4, gv4, bcp[:, BC_LNG: BC_LNG + 512].unsqueeze(1).broadcast_to([128, 4, 512]), ALU.mult,
                       G4 + ["bcp"], G4)
                    tt(vn4, gv4, bcp[:, BC_LNB: BC_LNB + 512].unsqueeze(1).broadcast_to([128, 4, 512]), ALU.add,
                       G4 + ["bcp"], ["vn4"])
                    for b_ in range(4):
                        for gi in range(4):
                            mm(PSB(4 + b_)[:, gi * 128:(gi + 1) * 128], vn4[:, b_, gi * 128:(gi + 1) * 128],
                               wmT[:, (l * 4 + gi) * 128:(l * 4 + gi + 1) * 128], True, True, ["vn4", "wmT"], [f"ps{4 + b_}"])
                    for b_ in range(4):
                        mb = grp * 4 + b_
                        bs = slice(mb * 128, (mb + 1) * 128)
                        tf_, T_ = tfb[b_ % 2]
                        tt(tf_, PSB(4 + b_), bcp[:, BC_BSP: BC_BSP + 512], ALU.add, [f"ps{4 + b_}", "bcp"], [T_])
                        tt(ybT[:, :, bs], tf_.rearrange("p (g t) -> p g t", t=128), ybT[:, :, bs], ALU.mult,
                           [T_, "ybT"], ["ybT"])
                selA = pvec[:, PV_SEL:PV_SEL + 1]
                selB = pvec[:, PV_SEL + 1:PV_SEL + 2]
                halo_t = small[:, 28:60].bitcast(BF16).rearrange("p (c j t) -> p c j t", c=4, t=2)
                jb = hf * 4
                for par in range(2):
                    if par == 0:
                        candB = tls[:, 0, :, 2 * jb:2 * jb + 8:2, :]
                        if hf == 0:
                            jlo = 1
                            candA = tls[:, 0, :, 1:6:2, :]
                        else:
                            jlo = 0
                            candA = tls[:, 0, :, 2 * jb - 1:2 * jb + 6:2, :]
                    else:
                        jlo = 0
                        candA = tls[:, 1, :, 2 * jb + 1:2 * jb + 8:2, :]
                        candB = tls[:, 1, :, 2 * jb:2 * jb + 8:2, :]
                    for c in range(4):
                        dst = yext[:, c, par:8:2, 0:2]
                        ht = halo_t[:, c, 0:4, :]
                        ts(ht, candB[:, c], selB, None, ALU.mult, None, ["tls", "pvec"], ["halo_t"])
                        if jlo == 1:
                            cp(dst[:, 0:1, :], ht[:, 0:1, :], ["halo_t"], ["yext"], eng="dve")
                            stt(dst[:, 1:4, :], candA[:, c], selA, ht[:, 1:4, :], ALU.mult, ALU.add,
                                ["tls", "pvec", "halo_t"], ["yext"])
                        else:
                            stt(dst, candA[:, c], selA, ht, ALU.mult, ALU.add, ["tls", "pvec", "halo_t"], ["yext"])
                for c in range(4):
                    for tt_ in range(2):
                        sl = slice(tt_ * 512, (tt_ + 1) * 512)
                        ye = yext[:, c, tt_ * 4:(tt_ + 1) * 4, :]
                        acc = ctmp.rearrange("p (m t) -> p m t", t=128)
                        wcol = lambda k: pvec[:, PV_CONV + (l * 3 + k) * 4 + c: PV_CONV + (l * 3 + k) * 4 + c + 1]
                        ts(acc, ye[:, :, 0:128], wcol(0), None, ALU.mult, None, ["yext", "pvec"], ["gv4_2"])
                        stt(acc, ye[:, :, 1:129], wcol(1), acc, ALU.mult, ALU.add, ["yext", "pvec", "gv4_2"], ["gv4_2"])
                        stt(acc, ye[:, :, 2:130], wcol(2), acc, ALU.mult, ALU.add, ["yext", "pvec", "gv4_2"], ["gv4_2"])
                        tt(ycT[:, c, sl], ctmp, cbT[:, c, sl], ALU.mult, ["gv4_2", "cbT"], ["ycT"])
                gw = [carve(49152 + i * 8192, [128, KC, 3, 128], BF16) for i in range(2)]
                bw = [carve(65536 + i * 4096, [128, 3, 4, 128], BF16) for i in range(2)]
                brs = [(yaT, "yaT", t0), (ybT, "ybT", 0), (ycT, "ycT", 0)]
                gcnt = 0
                for dc in range(KC):
                    g_ = gw[dc % 2]
                    b_ = bw[dc % 2]
                    gsrc = bass_gate_ap(w_in_d, l, dc)
                    bsrc = w_br_d[l].rearrange("n (k p) d -> p n k d", p=128)
                    for n in range(3):
                        dma("pool", g_[:, :, n, :], gsrc[:, :, n, :], [], [f"wbuf{dc % 2}"])
                        dma("pool", b_[:, n, :, :], bsrc[:, n, :, dc * 128:(dc + 1) * 128], [], [("sqA" if dc % 2 == 0 else "sqB")])
                    for tt_ in range(2):
                        sl = slice(tt_ * 512, (tt_ + 1) * 512)
                        for n in range(3):
                            pg = (gcnt % 4) * 2
                            pbq = pg + 1
                            gcnt += 1
                            for k in range(KC):
                                mm(PSB(pg), g_[:, k, n, :], hTh[:, k, sl], k == 0, k == KC - 1,
                                   [f"hTh{tt_}", f"wbuf{dc % 2}"], [f"ps{pg}"])
                            src, sres, o0 = brs[n]
                            for k in range(4):
                                mm(PSB(pbq), b_[:, n, k, :], src[:, k, o0 + tt_ * 512: o0 + (tt_ + 1) * 512],
                                   k == 0, k == 3, [sres, ("sqA" if dc % 2 == 0 else "sqB")], [f"ps{pbq}"])
                            act(sg[0], PSB(pg), AF.Sigmoid, [f"ps{pg}"], ["vn4"])
                            if n == 0:
                                tt(gv, sg[0], PSB(pbq), ALU.mult, ["vn4", f"ps{pbq}"], ["gv4_0"])
                            elif n == 1:
                                tt(tmpf, sg[0], PSB(pbq), ALU.mult, ["vn4", f"ps{pbq}"], ["gv4_1"])
                                tt(gv, gv, tmpf, ALU.add, ["gv4_0", "gv4_1"], ["gv4_0"])
                            else:
                                tt(tmpf, sg[0], PSB(pbq), ALU.mult, ["vn4", f"ps{pbq}"], ["gv4_1"])
                                tt(mrgT[:, dc, sl], gv, tmpf, ALU.add, ["gv4_0", "gv4_1"], ["mrgT"])
                w_out_v = w_out_d[l].rearrange("(k p) c -> p k c", p=128)
                pb = 0
                for cg in range(2):
                    wo_, ro_ = next_w(cg * 512, src=w_out_v)
                    for oc4 in range(4):
                        oc = cg * 4 + oc4
                        for tt_ in range(2):
                            sl = slice(tt_ * 512, (tt_ + 1) * 512)
                            gsl = slice(t0 + tt_ * 512, t0 + (tt_ + 1) * 512)
                            for k in range(KC):
                                mm(PSB(pb), wo_[:, k, oc4 * 128:(oc4 + 1) * 128], mrgT[:, k, sl], k == 0, k == KC - 1,
                                   ["mrgT", ro_], [f"ps{pb}"])
                            tt(xres[:, oc, gsl], xres[:, oc, gsl], PSB(pb), ALU.add, [f"x{oc}", f"ps{pb}"], [f"x{oc}"])
                            pb = (pb + 1) % 4

            S.barrier()
            if stop_after == f"mix{l}":
                break

            hT = carve(0, [128, KC, T], BF16)
            qx = carve(32768, [128, KC, T], BF16)
            wbuf = [carve(65536 + i * 8192, [128, KC, 512], BF16) for i in range(2)]
            sq = carve(81920, [128, KC, 512], BF16)
            rstd = carve(90112, [128, 512], F32)
            lnv = carve(92160, [128, 512], F32)
            memf = carve(94208, [128, KC, 256], F32)
            mTn = carve(102400, [128, KC, 256], BF16)
            kTx = carve(106496, [128, KC, 256], BF16)
            vx = carve(110592, [128, 2, D], BF16)
            pT = [carve(114688, [128, 2, 512], BF16), carve(118784, [128, 2, 512], BF16)]
            recb = [carve(116736, [128, 512], F32), carve(120832, [128, 512], F32)]
            wcnt = [0]
            for tt_ in range(4):
                sl = slice(tt_ * 512, (tt_ + 1) * 512)
                rms_norm_tile(xres[:, :, sl], 512, PV_XAG + l * 8, hT[:, :, sl], f"hT{tt_}", sq, rstd, lnv, 7, XR)
            dma("sp", memf, memT_d.rearrange("(c p) t -> p c t", p=128), [], ["memf"])
            rms_norm_tile(memf, 256, PV_MEMG + l * 8, mTn, "mTn", sq[:, :, 0:256], rstd[:, 0:256], lnv[:, 0:256], 6, ["memf"])
            pb = 0
            wk_v = wk_d[l].rearrange("(k p) c -> p k c", p=128)
            wv_v = wv_d[l].rearrange("(k p) c -> p k c", p=128)
            wq_v = wq_d[l].rearrange("(k p) c -> p k c", p=128)
            wo_v = wo_d[l].rearrange("(k p) c -> p k c", p=128)
            for cg in range(2):
                w_, r_ = next_w(cg * 512, src=wk_v)
                for dc4 in range(4):
                    dc = cg * 4 + dc4
                    for k in range(KC):
                        mm(PSB(pb)[:, 0:256], w_[:, k, dc4 * 128:(dc4 + 1) * 128], mTn[:, k, :], k == 0, k == KC - 1,
                           ["mTn", r_], [f"ps{pb}"])
                    cp(kTx[:, dc, :], PSB(pb)[:, 0:256], [f"ps{pb}"], ["kTx"])
                    pb = (pb + 1) % 4
            for cg in range(2):
                w_, r_ = next_w(cg * 512, src=wv_v)
                for kc in range(2):
                    for k in range(KC):
                        mm(PSB(pb), mTn[:, k, kc * 128:(kc + 1) * 128], w_[:, k, :], k == 0, k == KC - 1,
                           ["mTn", r_], [f"ps{pb}"])
                    cp(vx[:, kc, cg * 512:(cg + 1) * 512], PSB(pb), [f"ps{pb}"], ["vx"])
                    pb = (pb + 1) % 4
            for cg in range(2):
                w_, r_ = next_w(cg * 512, src=wq_v)
                for dc4 in range(4):
                    dc = cg * 4 + dc4
                    for tt_ in range(4):
                        sl = slice(tt_ * 512, (tt_ + 1) * 512)
                        for k in range(KC):
                            mm(PSB(pb), w_[:, k, dc4 * 128:(dc4 + 1) * 128], hT[:, k, sl], k == 0, k == KC - 1,
                               [f"hT{tt_}", r_], [f"ps{pb}"])
                        if (dc * 4 + tt_) % 2 == 0:
                            act(qx[:, dc, sl], PSB(pb), AF.Copy, [f"ps{pb}"], ["qx"], scale=0.0625)
                        else:
                            ts(qx[:, dc, sl], PSB(pb), 0.0625, None, ALU.mult, None, [f"ps{pb}"], ["qx"])
                        pb = (pb + 1) % 4
            oT = hT
            it = 0
            for tt_ in range(4):
                sl = slice(tt_ * 512, (tt_ + 1) * 512)
                for hx in range(4):
                    q_ = it % 2
                    it += 1
                    pT_, rec_ = pT[q_], recb[q_]
                    P_, R_ = f"pT{q_}", f"rec{q_}"
                    bd = 2 + q_
                    bo = 4 + 2 * q_
                    for kc in range(2):
                        for dd in range(2):
                            mm(PSB(kc), kTx[:, hx * 2 + dd, kc * 128:(kc + 1) * 128], qx[:, hx * 2 + dd, sl],
                               dd == 0, dd == 1, ["kTx", "qx"], [f"ps{kc}"])
                    act(pT_, psum[:, 0:2, :], AF.Exp, ["ps0", "ps1"], [P_])
                    for kc in range(2):
                        mm(PSB(bd), ones_b, pT_[:, kc, :], kc == 0, kc == 1, [P_, "cb16"], [f"ps{bd}"])
                    for dd in range(2):
                        for kc in range(2):
                            mm(PSB(bo + dd), vx[:, kc, hx * 256 + dd * 128: hx * 256 + (dd + 1) * 128], pT_[:, kc, :],
                               kc == 0, kc == 1, [P_, "vx"], [f"ps{bo + dd}"])
                    act(rec_, PSB(bd), AF.Ln, [f"ps{bd}"], [R_])
                    act(rec_, rec_, AF.Exp, [R_], [R_], scale=-1.0)
                    for dd in range(2):
                        tt(oT[:, hx * 2 + dd, sl], PSB(bo + dd), rec_, ALU.mult, [f"ps{bo + dd}", R_], [f"hT{tt_}"])
            pb = 0
            for cg in range(2):
                w_, r_ = next_w(cg * 512, src=wo_v)
                for oc4 in range(4):
                    oc = cg * 4 + oc4
                    for tt_ in range(4):
                        sl = slice(tt_ * 512, (tt_ + 1) * 512)
                        for k in range(KC):
                            mm(PSB(pb), w_[:, k, oc4 * 128:(oc4 + 1) * 128], oT[:, k, sl], k == 0, k == KC - 1,
                               [f"hT{tt_}", r_], [f"ps{pb}"])
                        tt(xres[:, oc, sl], xres[:, oc, sl], PSB(pb), ALU.add, [f"x{oc}", f"ps{pb}"], [f"x{oc}"])
                        pb = (pb + 1) % 4
            S.barrier()
            if stop_after == f"xa{l}":
                break

            wg_v = wg_d[l].rearrange("(k p) c -> p k c", p=128)
            wu_v = wu_d[l].rearrange("(k p) c -> p k c", p=128)
            wd_v = wd_d[l].rearrange("(k p) c -> p k c", p=128)
            for hf in range(2):
                t0 = hf * 1024
                hTh = carve(0, [128, KC, 1024], BF16)
                h1T = carve(16384, [128, NJ, 1024], BF16)
                wgb = [carve(61440 + i * 8192, [128, KC, 512], BF16) for i in range(2)]
                wub = [carve(77824 + i * 8192, [128, KC, 512], BF16) for i in range(2)]
                sq = carve(94208, [128, KC, 512], BF16)
                rstd = carve(102400, [128, 512], F32)
                lnv = carve(104448, [128, 512], F32)
                sgf = carve(106496, [128, 512], F32)
                tf = carve(108544, [128, 512], F32)
                for tt_ in range(2):
                    sl = slice(tt_ * 512, (tt_ + 1) * 512)
                    gsl = slice(t0 + tt_ * 512, t0 + (tt_ + 1) * 512)
                    rms_norm_tile(xres[:, :, gsl], 512, PV_FFNG + l * 8, hTh[:, :, sl], f"hTh{tt_}", sq, rstd, lnv, 7, XR)
                ngrp = 6
                pcount = 0
                for gi in range(ngrp):
                    c0 = gi * 512
                    ncol = min(512, FH - c0)
                    i = gi % 2
                    dma("pool", wgb[i][:, :, 0:ncol], wg_v[:, :, c0:c0 + ncol], [], [f"wgb{i}"])
                    dma("pool", wub[i][:, :, 0:ncol], wu_v[:, :, c0:c0 + ncol], [], [f"wub{i}"])
                    for jj in range(ncol // 128):
                        j = gi * 4 + jj
                        for tt_ in range(2):
                            sl = slice(tt_ * 512, (tt_ + 1) * 512)
                            pg = (pcount % 4) * 2
                            pu = pg + 1
                            pcount += 1
                            for k in range(KC):
                                mm(PSB(pg), wgb[i][:, k, jj * 128:(jj + 1) * 128], hTh[:, k, sl], k == 0, k == KC - 1,
                                   [f"hTh{tt_}", f"wgb{i}"], [f"ps{pg}"])
                            for k in range(KC):
                                mm(PSB(pu), wub[i][:, k, jj * 128:(jj + 1) * 128], hTh[:, k, sl], k == 0, k == KC - 1,
                                   [f"hTh{tt_}", f"wub{i}"], [f"ps{pu}"])
                            act(sgf, PSB(pg), AF.Sigmoid, [f"ps{pg}"], ["sgf"])
                            tt(tf, sgf, PSB(pg), ALU.mult, ["sgf", f"ps{pg}"], ["tf"])
                            tt(h1T[:, j, sl], tf, PSB(pu), ALU.mult, ["tf", f"ps{pu}"], ["h1T"])
                wdb = [carve(61440 + i * 16384, [128, NJ, 256], BF16) for i in range(2)]
                wdn = [["wgb0", "wgb1"], ["wub0", "wub1"]]
                pb = 0
                for cg in range(4):
                    i = cg % 2
                    dma("pool", wdb[i], wd_v[:, :, cg * 256:(cg + 1) * 256], [], wdn[i])
                    for oc2 in range(2):
                        oc = cg * 2 + oc2
                        for tt_ in range(2):
                            sl = slice(tt_ * 512, (tt_ + 1) * 512)
                            gsl = slice(t0 + tt_ * 512, t0 + (tt_ + 1) * 512)
                            for j in range(NJ):
                                mm(PSB(pb), wdb[i][:, j, oc2 * 128:(oc2 + 1) * 128], h1T[:, j, sl], j == 0, j == NJ - 1,
                                   ["h1T"] + wdn[i], [f"ps{pb}"])
                            tt(xres[:, oc, gsl], xres[:, oc, gsl], PSB(pb), ALU.add, [f"x{oc}", f"ps{pb}"], [f"x{oc}"])
                            pb = (pb + 1) % 4
            S.barrier()
            if stop_after == f"ffn{l}":
                break

        yT_v = yT_d.rearrange("(c p) t -> p c t", p=128)
        if stop_after is None and final:
            sq = carve(0, [128, KC, 512], BF16)
            rstd = carve(8192, [128, 512], F32)
            lnv = carve(10240, [128, 512], F32)
            outb = [carve(16384 + i * 16384, [128, KC, 512], F32) for i in range(2)]
            for tt_ in range(4):
                sl = slice(tt_ * 512, (tt_ + 1) * 512)
                ob = outb[tt_ % 2]
                rms_norm_tile(xres[:, :, sl], 512, PV_FING, ob, f"outb{tt_ % 2}", sq, rstd, lnv, 7, XR)
                dma("sp", yT_v[:, :, sl], ob, [f"outb{tt_ % 2}"], ["yT"])
        else:
            for c in range(KC):
                dma("sp", yT_v[:, c, :], xres[:, c, :], [f"x{c}"], ["yT"])
        fw = S.final_wait("sp")

        @block.tensor
        def _(e):
            S.emit("pe", e)

        @block.scalar
        def _(e):
            S.emit("act", e)

        @block.vector
        def _(e):
            S.emit("dve", e)

        @block.gpsimd
        def _(e):
            S.emit("pool", e)

        @block.sync
        def _(e):
            S.emit("sp", e)
            for s, v in fw:
                e.wait_ge(sems[s], v)

    return nc


def bass_gate_ap(w_in_d, l, dc):
    v = w_in_d[l].rearrange("(k p) c -> p k c", p=128)[:, :, 4096:7168]
    return v.rearrange("p k (n c) -> p k n c", n=3)[:, :, :, dc * 128:(dc + 1) * 128]


_CACHE = {}


def _owned_blocks(role):
    out = []
    for j in range(8):
        out += ([4 * j, 4 * j + 3] if role == 0 else [4 * j + 1, 4 * j + 2])
    return out


def _prep_inputs(inp):
    f = np.float32
    x = np.asarray(inp["x"], f)
    mem = np.asarray(inp["mem"], f)
    gv = lambda k: np.asarray(inp[k], f)

    def pcols(a):
        a = a.reshape(-1, 8, 128)
        return np.ascontiguousarray(a.transpose(2, 0, 1).reshape(128, -1))

    pvec_base = np.zeros((128, NPV), f)
    pvec_base[:, PV_MIXG:PV_MIXG + 16] = pcols(gv("norm_mix_g"))
    pvec_base[:, PV_XAG:PV_XAG + 16] = pcols(gv("norm_xa_g"))
    pvec_base[:, PV_MEMG:PV_MEMG + 16] = pcols(gv("mem_norm_g"))
    pvec_base[:, PV_FFNG:PV_FFNG + 16] = pcols(gv("norm_ffn_g"))
    pvec_base[:, PV_FING:PV_FING + 8] = pcols(gv("final_g")[None])
    cw = gv("conv_w").reshape(L * 3, 4, 128)
    pvec_base[:, PV_CONV:PV_CONV + 24] = cw.transpose(2, 0, 1).reshape(128, 24)

    bc = np.zeros((128, NBC), f)
    for l_ in range(L):
        o = l_ * 1536
        bc[:, o:o + 512] = np.broadcast_to(gv("sgu_ln_g")[l_].reshape(1, -1), (128, 512))
        bc[:, o + 512:o + 1024] = np.broadcast_to(gv("sgu_ln_b")[l_].reshape(1, -1), (128, 512))
        bc[:, o + 1024:o + 1536] = np.broadcast_to(gv("b_spatial")[l_].reshape(1, -1), (128, 512))
    wspT = np.ascontiguousarray(gv("w_spatial").transpose(3, 0, 1, 2).reshape(128, L * 4 * 128))

    j = np.arange(128)
    negtri = -(j[:, None] >= j[None, :]).astype(f)
    sgum = ((j[None, :] // 64) >= (j[:, None] // 64)).astype(f)
    diag = (j[:, None] < j[None, :]).astype(f)
    full = np.ones((128, 128), f)
    none = np.zeros((128, 128), f)
    amasks = {0: [[diag, none], [full, diag]], 1: [[full, diag], [diag, none]]}

    shared = {k: np.ascontiguousarray(gv(k)) for k in
              ["w_in", "w_branch", "w_out", "w_q_xa", "w_k_xa", "w_v_xa", "w_o_xa",
               "w_gate_ffn", "w_up_ffn", "w_down_ffn"]}
    in_maps = []
    for core in range(8):
        b, role = divmod(core, 2)
        blocks = _owned_blocks(role)
        xb = x[b].reshape(32, 128, D)[blocks].reshape(T, D)
        cst = np.zeros((128, NCST), f)
        cst[:, CS_NEGTRI:CS_NEGTRI + 128] = negtri
        cst[:, CS_SGUM:CS_SGUM + 128] = sgum
        for par in range(2):
            for w in range(2):
                o = CS_AMASK + (par * 2 + w) * 128
                cst[:, o:o + 128] = amasks[role][par][w]
        pv = pvec_base.copy()
        pv[:, PV_SEL] = 1.0 if role == 0 else 0.0
        pv[:, PV_SEL + 1] = 0.0 if role == 0 else 1.0
        m = dict(shared)
        m.update({"xT": np.ascontiguousarray(xb.T), "memT": np.ascontiguousarray(mem[b].T),
                  "pvec": pv, "cst": cst, "bc": bc, "wspT": wspT})
        in_maps.append(m)
    return in_maps


def _assemble(results):
    out = np.zeros((4, 4096, D), np.float32)
    for core in range(8):
        b, role = divmod(core, 2)
        blocks = _owned_blocks(role)
        y = np.asarray(results[core]["yT"]).T.reshape(NB, 128, D)
        out[b].reshape(32, 128, D)[blocks] = y
    return out


FUSED = True


def kernel(**inputs):
    in_maps = _prep_inputs(inputs)
    if FUSED:
        if "nc" not in _CACHE:
            _CACHE["nc"] = build_program()
        res = run_bass_kernel_spmd(_CACHE["nc"], in_maps, core_ids=list(range(8)))
        return _assemble(res.results)
    if "nc0" not in _CACHE:
        _CACHE["nc0"] = build_program(layers=(0,), final=False)
        _CACHE["nc1"] = build_program(layers=(1,), final=True)
    res0 = run_bass_kernel_spmd(_CACHE["nc0"], in_maps, core_ids=list(range(8)))
    for c in range(8):
        in_maps[c]["xT"] = np.ascontiguousarray(np.asarray(res0.results[c]["yT"], np.float32))
    res1 = run_bass_kernel_spmd(_CACHE["nc1"], in_maps, core_ids=list(range(8)))
    return _assemble(res1.results)
```

```python
import numpy as np
import concourse.bass as bass
import concourse.mybir as mybir
from concourse.bass_utils import run_bass_kernel_spmd

F32 = mybir.dt.float32
BF16 = mybir.dt.bfloat16
AF = mybir.ActivationFunctionType
ALU = mybir.AluOpType

L = 2
D = 1024
T = 2048
NB = 16
KC = 8
FH = 2816
NJ = 22
INC = 7168
PAIRS = [[0, 1], [2, 3], [4, 5], [6, 7]]

PV_MIXG = 0
PV_XAG = 16
PV_MEMG = 32
PV_FFNG = 48
PV_FING = 64
PV_CONV = 72
PV_SEL = 96
NPV = 98
CS_NEGTRI = 0
CS_SGUM = 128
CS_AMASK = 256
NCST = 768
BC_LNG = 0
BC_LNB = 512
BC_BSP = 1024
NBC = 3072


def gblock(g):
    j, i = divmod(g, 4)
    return [(0, 2 * j), (1, 2 * j), (1, 2 * j + 1), (0, 2 * j + 1)][i]


class Sched:
    def __init__(self, nc, sems):
        self.nc = nc
        self.sems = sems
        self.engs = ["pe", "act", "dve", "pool", "sp"]
        self.ops = {e: [] for e in self.engs}
        self.count = {e: 0 for e in self.engs}
        self.waited = {e: {} for e in self.engs}
        self.res = {}
        self.dq = {"sp": [f"D_sp{i}" for i in range(8)], "pool": [f"D_pl{i}" for i in range(8)]}
        self.dnext = {"sp": 0, "pool": 0}
        self.dcnt = {}
        self.pending_barrier = {e: {} for e in self.engs}
        self.cc_n = 0

    def _r(self, k):
        if k not in self.res:
            self.res[k] = {"w": None, "r": {}}
        return self.res[k]

    def add(self, eng, fn, reads=(), writes=(), kind="c"):
        deps = {}

        def need(tok):
            if tok is None:
                return
            s, v = tok
            if deps.get(s, 0) < v:
                deps[s] = v

        for k in reads:
            need(self._r(k)["w"])
        for k in writes:
            r = self._r(k)
            need(r["w"])
            for s, v in r["r"].items():
                need((s, v))
        for s, v in self.pending_barrier[eng].items():
            need((s, v))
        self.pending_barrier[eng] = {}
        if kind == "d":
            q = self.dq[eng]
            dsem = q[self.dnext[eng] % len(q)]
            self.dnext[eng] += 1
            prev = self.dcnt.get(dsem, 0)
            if prev:
                need((dsem, prev))
            self.dcnt[dsem] = prev + 16
            tok = (dsem, prev + 16)
            inc = (dsem, 16)
        elif kind == "cc":
            name = f"CC{self.cc_n}"
            self.cc_n += 1
            tok = (name, 1)
            inc = (name, None)
        else:
            self.count[eng] += 1
            tok = (f"S_{eng}", self.count[eng])
            inc = (f"S_{eng}", 1)
        waits = []
        for s, v in deps.items():
            if eng == "pe" and s == "S_pe":
                continue
            if self.waited[eng].get(s, 0) >= v:
                continue
            self.waited[eng][s] = v
            waits.append((s, v))
        self.ops[eng].append((waits, fn, inc))
        for k in reads:
            r = self._r(k)
            if r["r"].get(tok[0], 0) < tok[1]:
                r["r"][tok[0]] = tok[1]
        for k in writes:
            r = self._r(k)
            r["w"] = tok
            r["r"] = {}
        return tok

    def barrier(self):
        snap = {}
        for e in ["pe", "act", "dve", "pool"]:
            if self.count[e]:
                snap[f"S_{e}"] = self.count[e]
        for s, v in self.dcnt.items():
            snap[s] = v
        for i in range(self.cc_n):
            snap[f"CC{i}"] = 1
        for e in self.engs:
            self.pending_barrier[e] = dict(snap)

    def emit(self, eng, handle):
        for waits, fn, inc in self.ops[eng]:
            for s, v in waits:
                handle.wait_ge(self.sems[s], v)
            ins = fn(handle)
            if inc[1] is None:
                ins.then_inc(self.sems[inc[0]])
            else:
                ins.then_inc(self.sems[inc[0]], inc[1])

    def final_wait(self, eng):
        self.barrier()
        deps = self.pending_barrier[eng]
        waits = [(s, v) for s, v in deps.items() if self.waited[eng].get(s, 0) < v]
        return waits


def build_program(stop_after=None, layers=(0, 1), final=True):
    nc = bass.Bass("TRN2", target_bir_lowering=False)

    def din(name, shape):
        return nc.dram_tensor(name, list(shape), F32, kind="ExternalInput").ap()

    xT_d = din("xT", [D, T])
    memT_d = din("memT", [D, 256])
    pvec_d = din("pvec", [128, NPV])
    cst_d = din("cst", [128, NCST])
    bc_d = din("bc", [128, NBC])
    wspT_d = din("wspT", [128, L * 4 * 128])
    w_in_d = din("w_in", [L, D, INC])
    w_br_d = din("w_branch", [L, 3, 512, D])
    w_out_d = din("w_out", [L, D, D])
    wq_d = din("w_q_xa", [L, D, D])
    wk_d = din("w_k_xa", [L, D, D])
    wv_d = din("w_v_xa", [L, D, D])
    wo_d = din("w_o_xa", [L, D, D])
    wg_d = din("w_gate_ffn", [L, D, FH])
    wu_d = din("w_up_ffn", [L, D, FH])
    wd_d = din("w_down_ffn", [L, FH, D])
    yT_d = nc.dram_tensor("yT", [D, T], F32, kind="ExternalOutput").ap()

    scratch = {}
    for l_ in range(L):
        scratch[l_] = dict(
            kt_src=nc.dram_tensor(f"kt_src{l_}", [NB * 128, 512], BF16),
            kt_all=nc.dram_tensor(f"kt_all{l_}", [2 * NB * 128, 512], BF16),
            v_src=nc.dram_tensor(f"v_src{l_}", [NB * 128, 512], BF16),
            v_all=nc.dram_tensor(f"v_all{l_}", [2 * NB * 128, 512], BF16),
            tl_src=nc.dram_tensor(f"tl_src{l_}", [128, 128], BF16),
            tl_all=nc.dram_tensor(f"tl_all{l_}", [256, 128], BF16))

    sem_names = ["S_pe", "S_act", "S_dve", "S_pool"] + [f"D_sp{i}" for i in range(8)] + \
                [f"D_pl{i}" for i in range(8)] + [f"CC{i}" for i in range(3 * L)]

    import contextlib
    with contextlib.ExitStack() as es:
        xres_t = es.enter_context(nc.sbuf_tensor("xres", [128, KC, T], F32))
        ARENA_B = 122 * 1024
        arena = es.enter_context(nc.sbuf_tensor("arena", [128, ARENA_B // 2], BF16))
        pvec = es.enter_context(nc.sbuf_tensor("pvec_sb", [128, NPV], F32))
        cst = es.enter_context(nc.sbuf_tensor("cst_sb", [128, NCST], F32))
        bcp = es.enter_context(nc.sbuf_tensor("bcp", [128, 1536], F32))
        cb16 = es.enter_context(nc.sbuf_tensor("cb16", [128, 1280], BF16))
        wmT = es.enter_context(nc.sbuf_tensor("wmT_sb", [128, L * 4 * 128], BF16))
        small = es.enter_context(nc.sbuf_tensor("small", [128, 64], F32))
        psum = es.enter_context(nc.psum_tensor("ps", [128, 8, 512], F32))
        sems = {n: es.enter_context(nc.semaphore(n)) for n in sem_names}
        block = es.enter_context(nc.Block())

        S = Sched(nc, sems)
        xres = xres_t

        negtri_b = cb16[:, 0:128]
        negones_b = cb16[:, 128:256]
        ones_b = cb16[:, 256:384]
        amask_b = [[cb16[:, 384 + (p * 2 + w) * 128: 384 + (p * 2 + w + 1) * 128] for w in range(2)] for p in range(2)]
        sgum_b = cb16[:, 896:1024]

        def carve(off, shape, dt):
            es_ = 2 if dt == BF16 else 4
            n = int(np.prod(shape[1:]))
            assert off % 4 == 0 and off + n * es_ <= ARENA_B, (off, shape)
            ap = arena[:, off // 2: off // 2 + n * es_ // 2]
            if dt == F32:
                ap = ap.bitcast(F32)
            if len(shape) == 3:
                ap = ap.rearrange("p (a b) -> p a b", b=shape[2])
            elif len(shape) == 4:
                ap = ap.rearrange("p (a b c) -> p a b c", b=shape[2], c=shape[3])
            elif len(shape) == 5:
                ap = ap.rearrange("p (a b c d) -> p a b c d", b=shape[2], c=shape[3], d=shape[4])
            return ap

        PSB = lambda b: psum[:, b, :]
        evac_flip = [0]

        def mm(out, lhsT, rhs, start, stop, reads, writes):
            S.add("pe", lambda e: e.matmul(out, lhsT, rhs, start=start, stop=stop), reads, writes)

        def act(out, in_, func, reads, writes, scale=1.0, bias=0.0, accum=None):
            kw = {"scale": scale}
            if bias != 0.0:
                kw["bias"] = bias
            if accum is not None:
                kw["accum_out"] = accum
            S.add("act", lambda e: e.activation(out, in_, func, **kw), reads, writes)

        def tt(out, in0, in1, op, reads, writes, eng="dve"):
            S.add(eng, lambda e: e.tensor_tensor(out, in0, in1, op), reads, writes)

        def ts(out, in0, s1, s2, op0, op1, reads, writes, eng="dve"):
            if op1 is None:
                S.add(eng, lambda e: e.tensor_scalar(out, in0, s1, None, op0), reads, writes)
            else:
                S.add(eng, lambda e: e.tensor_scalar(out, in0, s1, s2, op0, op1), reads, writes)

        def stt(out, in0, scalar, in1, op0, op1, reads, writes):
            S.add("dve", lambda e: e.scalar_tensor_tensor(out, in0, scalar, in1, op0, op1), reads, writes)

        def cp(out, in_, reads, writes, eng=None):
            if eng is None:
                eng = "act" if evac_flip[0] % 2 == 0 else "dve"
                evac_flip[0] += 1
            if eng == "act":
                S.add("act", lambda e: e.activation(out, in_, AF.Copy), reads, writes)
            else:
                S.add(eng, lambda e: e.tensor_copy(out, in_), reads, writes)

        def dma(q, out, in_, reads, writes):
            S.add(q, lambda e: e.dma_start(out=out, in_=in_), reads, writes, kind="d")

        def allgather(src, dst, reads, writes):
            import os
            if os.environ.get("NO_CC"):
                S.add("pool", lambda e: e.dma_start(out=dst.ap()[0:src.shape[0], :], in_=src.ap()), reads, writes, kind="d")
                return
            S.add("pool", lambda e: e.collective_compute(
                "AllGather", ALU.bypass, replica_groups=PAIRS, ins=[src.ap().opt()], outs=[dst.ap().opt()]),
                reads, writes, kind="cc")

        xT_v = xT_d.rearrange("(c p) t -> p c t", p=128)
        for c in range(KC):
            dma("sp", xres[:, c, :], xT_v[:, c, :], [], [f"x{c}"])
        dma("sp", pvec[:], pvec_d[:, :], [], ["pvec"])
        dma("sp", cst[:], cst_d[:, :], [], ["cst"])
        cp(negtri_b, cst[:, CS_NEGTRI:CS_NEGTRI + 128], ["cst"], ["cb16"], eng="dve")
        S.add("dve", lambda e: e.memset(negones_b, -1.0), [], ["cb16"])
        S.add("dve", lambda e: e.memset(ones_b, 1.0), [], ["cb16"])
        S.add("dve", lambda e: e.memset(small[:, 62:63], -0.5), [], ["smc"])
        cp(cb16[:, 384:896], cst[:, CS_AMASK:CS_AMASK + 512], ["cst"], ["cb16"], eng="dve")
        wsp_stage = carve(0, [128, L * 4 * 128], F32)
        dma("sp", wsp_stage, wspT_d[:, :], [], ["wsp_stage"])
        tt(wmT[:].rearrange("p (a t) -> p a t", t=128), wsp_stage.rearrange("p (a t) -> p a t", t=128),
           cst[:, CS_SGUM:CS_SGUM + 128].unsqueeze(1).broadcast_to([128, L * 4, 128]), ALU.mult,
           ["wsp_stage", "cst"], ["wmT"])
        S.barrier()

        XR = [f"x{c}" for c in range(KC)]

        def rms_norm_tile(src3, ncols, gbase, hT_out3, hres, sq, rstd, lnv, psb, srcres, out_f32=False):
            act(sq, src3, AF.Square, srcres, ["sqA", "sqB"])
            for c in range(KC):
                mm(PSB(psb)[:, 0:ncols], ones_b, sq[:, c, :], c == 0, c == KC - 1, ["sqA", "sqB", "cb16"], [f"ps{psb}"])
            act(lnv, PSB(psb)[:, 0:ncols], AF.Ln, [f"ps{psb}"], ["lnv"], scale=1.0 / D, bias=1e-6)
            act(rstd, lnv, AF.Exp, ["lnv"], ["rstd"], scale=-0.5)
            for c in range(KC):
                stt(hT_out3[:, c, :], src3[:, c, :], pvec[:, gbase + c: gbase + c + 1], rstd, ALU.mult, ALU.mult,
                    srcres + ["rstd", "pvec"], [hres])

        def load_w(q, wb, wres, src_ap):
            dma(q, wb, src_ap, [], [wres])

        for l in layers:
            w_in_v = w_in_d[l].rearrange("(k p) c -> p k c", p=128)
            kt_src, kt_all, v_src, v_all, tl_src, tl_all = [scratch[l][k_] for k_ in
                                                            ("kt_src", "kt_all", "v_src", "v_all", "tl_src", "tl_all")]

            hT = carve(0, [128, KC, T], BF16)
            qT = carve(32768, [128, 4, T], BF16)
            wbuf = [carve(49152 + i * 8192, [128, KC, 512], BF16) for i in range(2)]
            sq = carve(65536, [128, KC, 512], BF16)
            rstd = carve(73728, [128, 512], F32)
            lnv = carve(75776, [128, 512], F32)
            kst = [carve(77824 + i * 4096, [128, T], BF16) for i in range(2)]
            vst = [carve(86016 + i * 4096, [128, 4, 512], BF16) for i in range(2)]
            tl_sb = carve(94208, [128, 128], BF16)
            cc_sb = carve(94464, [128, 128], F32)
            wcnt = [0]

            def next_w(col0, ncol=512, src=None):
                i = wcnt[0] % 2
                wcnt[0] += 1
                srcv = w_in_v if src is None else src
                load_w("pool", wbuf[i][:, :, 0:ncol], f"wbuf{i}", srcv[:, :, col0:col0 + ncol])
                return wbuf[i], f"wbuf{i}"

            wb5 = [wbuf[0], wbuf[1]] + [carve(95232 + i * 8192, [128, KC, 512], BF16) for i in range(3)]
            rb5 = ["wbuf0", "wbuf1", "wb5_2", "wb5_3", "wb5_4"]
            for i_, c0_ in enumerate((3072, 3584, 512, 1024, 0)):
                load_w("pool", wb5[i_], rb5[i_], w_in_v[:, :, c0_:c0_ + 512])
            for tt_ in range(4):
                sl = slice(tt_ * 512, (tt_ + 1) * 512)
                rms_norm_tile(xres[:, :, sl], 512, PV_MIXG + l * 8, hT[:, :, sl], f"hT{tt_}", sq, rstd, lnv, 7, XR)
            HT = [f"hT{i}" for i in range(4)]

            wcc, rcc = wb5[0], rb5[0]
            wcx, rcx = wb5[1], rb5[1]
            wk_, rk_ = wb5[2], rb5[2]
            kt_dst = kt_src.ap().rearrange("(m p) (c k) -> p m c k", p=128, k=128)
            pb = 0
            for c in range(4):
                ks = kst[c % 2]
                for tt_ in range(4):
                    sl = slice(tt_ * 512, (tt_ + 1) * 512)
                    for k in range(KC):
                        mm(PSB(pb), wk_[:, k, c * 128:(c + 1) * 128], hT[:, k, sl], k == 0, k == KC - 1,
                           [f"hT{tt_}", rk_], [f"ps{pb}"])
                    cp(ks[:, sl], PSB(pb), [f"ps{pb}"], [f"kst{c % 2}"])
                    pb = (pb + 1) % 4
                dma("sp", kt_dst[:, :, c, :], ks.rearrange("p (m k) -> p m k", k=128), [f"kst{c % 2}"], ["kt_src"])
            allgather(kt_src, kt_all, ["kt_src"], ["kt_all"])

            wv_, rv_ = wb5[3], rb5[3]
            v_dst = v_src.ap().rearrange("(m p) c -> p m c", p=128)
            for m in range(NB):
                vs = vst[(m // 4) % 2]
                for k in range(KC):
                    mm(PSB(pb), hT[:, k, m * 128:(m + 1) * 128], wv_[:, k, :], k == 0, k == KC - 1,
                       [f"hT{m // 4}", rv_], [f"ps{pb}"])
                cp(vs[:, m % 4, :], PSB(pb), [f"ps{pb}"], [f"vst{(m // 4) % 2}"])
                pb = (pb + 1) % 4
                if m % 4 == 3:
                    dma("sp", v_dst[:, m - 3:m + 1, :], vs, [f"vst{(m // 4) % 2}"], ["v_src"])
            allgather(v_src, v_all, ["v_src"], ["v_all"])

            hT_tail = [hT[:, k, :].rearrange("p (m t) -> p m t", t=128)[:, :, 126:128] for k in range(KC)]
            for (wb_, rb_, half) in ((wcc, rcc, 0), (wcx, rcx, 1)):
                for c in range(4):
                    for k in range(KC):
                        mm(PSB(6)[:, half * 128 + c * 32: half * 128 + (c + 1) * 32], wb_[:, k, c * 128:(c + 1) * 128],
                           hT_tail[k], k == 0, k == KC - 1, HT + [rb_], ["ps6"])
            cp(cc_sb, PSB(6)[:, 0:128], ["ps6"], ["cc_sb"], eng="act")
            tt(tl_sb, cc_sb, PSB(6)[:, 128:256], ALU.mult, ["cc_sb", "ps6"], ["tl_sb"])
            dma("sp", tl_src[:, :], tl_sb, ["tl_sb"], ["tl_src"])
            allgather(tl_src, tl_all, ["tl_src"], ["tl_all"])

            wq_, rq_ = wb5[4], rb5[4]
            for c in range(4):
                for tt_ in range(4):
                    sl = slice(tt_ * 512, (tt_ + 1) * 512)
                    for k in range(KC):
                        mm(PSB(pb), wq_[:, k, c * 128:(c + 1) * 128], hT[:, k, sl], k == 0, k == KC - 1,
                           [f"hT{tt_}", rq_], [f"ps{pb}"])
                    if (c * 4 + tt_) % 2 == 0:
                        act(qT[:, c, sl], PSB(pb), AF.Copy, [f"ps{pb}"], ["qT"], scale=0.125)
                    else:
                        ts(qT[:, c, sl], PSB(pb), 0.125, None, ALU.mult, None, [f"ps{pb}"], ["qT"])
                    pb = (pb + 1) % 4
            S.barrier()

            if stop_after == f"qkv{l}":
                break
            KTr = carve(49152, [128, 2 * NB, 512], BF16)
            Vr = carve(81920, [128, 2 * NB, 512], BF16)
            yaT = carve(0, [128, 4, T], BF16)
            Ebuf = [carve(16384, [128, 1024], F32), carve(118784, [128, 1024], F32)]
            Lsp = [carve(20480 + i * 2048, [128, 1024], BF16) for i in range(3)]
            Abuf = [carve(26624 + i * 2048, [128, 1024], BF16) for i in range(2)]
            Rb = [carve(114688 + i * 2048, [128, 1024], BF16) for i in range(2)]
            yatmp = carve(30720, [128, 4, 128], BF16)
            kt_v = kt_all.ap().rearrange("(b p) x -> p b x", p=128)
            v_v = v_all.ap().rearrange("(b p) x -> p b x", p=128)
            for i in range(4):
                dma("sp", KTr[:, i * 8:(i + 1) * 8, :], kt_v[:, i * 8:(i + 1) * 8, :], ["kt_all"], ["KT"])
            for i in range(4):
                dma("sp", Vr[:, i * 8:(i + 1) * 8, :], v_v[:, i * 8:(i + 1) * 8, :], ["v_all"], ["V"])

            steps = []
            for m in range(NB):
                G = 4 * (m // 2) + (1 if m % 2 == 0 else 3)
                for si, g in enumerate(range(G, -1, -1)):
                    r, ml = gblock(g)
                    steps.append(dict(m=m, si=si, g=g, kb=r * NB + ml, last=(g == 0), par=m % 2))
            NS = len(steps)
            Zb = lambda s: psum[:, (s % 2) * 2:(s % 2) * 2 + 2, :].rearrange("p a b -> p (a b)")
            Zres = lambda s: f"Z{s % 2}"
            Cb = psum[:, 4:6, :].rearrange("p a b -> p (a b)")
            Ob = psum[0:64, 6:8, :].rearrange("p a b -> p (a b)")

            def qk(out1024, st, start_flag, wres):
                m, kb = st["m"], st["kb"]
                for c in range(4):
                    for hh in range(2):
                        hp = hh * 4 + c
                        mm(out1024[:, hp * 128:(hp + 1) * 128],
                           KTr[hh * 64:(hh + 1) * 64, kb, c * 128:(c + 1) * 128],
                           qT[hh * 64:(hh + 1) * 64, c, m * 128:(m + 1) * 128],
                           start_flag, (True if start_flag else c == 3), ["KT", "qT"], [wres])

            def front(s):
                st = steps[s]
                qk(Zb(s), st, True, Zres(s))

            def mid0(s):
                act(Ebuf[s % 2], Zb(s), AF.Exp, [Zres(s)], [f"E{s % 2}"])

            def mid(s):
                st = steps[s]
                si = st["si"]
                lb = Lsp[s % 3]
                act(lb, Ebuf[s % 2], AF.Ln, [f"E{s % 2}"], [f"L{s % 3}"], bias=1.0)
                if si < 2:
                    mk = amask_b[st["par"]][1 - si]
                    tt(lb.rearrange("p (h t) -> p h t", t=128), lb.rearrange("p (h t) -> p h t", t=128),
                       mk.unsqueeze(1).broadcast_to([128, 8, 128]), ALU.mult, [f"L{s % 3}", "cb16"], [f"L{s % 3}"])
                if si == 0:
                    st["carry"] = None
                elif si == 1:
                    st["carry"] = (Lsp[(s - 1) % 3], f"L{(s - 1) % 3}")
                else:
                    prev = steps[s - 1]["carry"]
                    rb = Rb[s % 2]
                    tt(rb, prev[0], Lsp[(s - 1) % 3], ALU.add, [prev[1], f"L{(s - 1) % 3}"], [f"R{s % 2}"])
                    st["carry"] = (rb, f"R{s % 2}")

            def back1(s):
                st = steps[s]
                lb = Lsp[s % 3]
                for hb in range(2):
                    cs = slice(hb * 512, (hb + 1) * 512)
                    mm(Cb[:, cs], negtri_b, lb[:, cs], True, False, [f"L{s % 3}", "cb16"], ["C"])
                    if st["carry"] is not None:
                        mm(Cb[:, cs], negones_b, st["carry"][0][:, cs], False, False, [st["carry"][1], "cb16"], ["C"])
                qk(Cb, st, False, "C")

            def back2(s):
                st = steps[s]
                ab = Abuf[s % 2]
                act(ab, Cb, AF.Exp, ["C"], [f"A{s % 2}"])
                if st["si"] < 2:
                    mk = amask_b[st["par"]][1 - st["si"]]
                    tt(ab.rearrange("p (h t) -> p h t", t=128), ab.rearrange("p (h t) -> p h t", t=128),
                       mk.unsqueeze(1).broadcast_to([128, 8, 128]), ALU.mult, [f"A{s % 2}", "cb16"], [f"A{s % 2}"])

            def back3(s):
                st = steps[s]
                ab = Abuf[s % 2]
                m, kb = st["m"], st["kb"]
                for hp in range(8):
                    hh, c = divmod(hp, 4)
                    h = 2 * c + hh
                    mm(Ob[:, hp * 128:(hp + 1) * 128], Vr[:, kb, h * 64:(h + 1) * 64], ab[:, hp * 128:(hp + 1) * 128],
                       st["si"] == 0 and c == 0, st["last"] and c == 3, [f"A{s % 2}", "V"], ["O"])
                if st["last"]:
                    ov = Ob.rearrange("p (hh c t) -> p hh c t", hh=2, c=4)
                    S.add("dve", lambda e: e.tensor_copy(yaT[0:64, :, m * 128:(m + 1) * 128], ov[:, 0, :, :]),
                          ["O"], ["yaT"])
                    S.add("dve", lambda e: e.tensor_copy(yatmp[0:64], ov[:, 1, :, :]), ["O"], ["yatmp"])
                    dma("sp", yaT[64:128, :, m * 128:(m + 1) * 128], yatmp[0:64], ["yatmp"], ["yaT"])

            ok = lambda i: 0 <= i < NS
            for s in range(-2, NS + 2):
                if ok(s - 1):
                    back1(s - 1)
                if ok(s - 2):
                    back3(s - 2)
                if ok(s + 2):
                    front(s + 2)
                if ok(s + 1):
                    mid0(s + 1)
                if ok(s):
                    mid(s)
                if ok(s - 1):
                    back2(s - 1)
            S.barrier()

            if stop_after == f"attn{l}":
                break

            dma("sp", bcp[:], bc_d[:, l * 1536:(l + 1) * 1536], [], ["bcp"])
            for hf in range(2):
                t0 = hf * 1024
                hTh = carve(16384, [128, KC, 1024], BF16)
                ybT = carve(32768, [128, 4, 1024], BF16)
                ycT = carve(40960, [128, 4, 1024], BF16)
                wbuf = [carve(49152 + i * 8192, [128, KC, 512], BF16) for i in range(2)]
                sq = carve(65536, [128, KC, 512], BF16)
                rstd = carve(73728, [128, 512], F32)
                lnv = carve(75776, [128, 512], F32)
                mrgT = carve(77824, [128, KC, 1024], BF16)
                yext = carve(94208, [128, 4, 8, 130], BF16)
                cbT = carve(102528, [128, 4, 1024], BF16)
                gv4 = carve(110720, [128, 4, 512], F32)
                vn4 = carve(118912, [128, 4, 512], BF16)
                tls = carve(123008, [128, 2, 4, NB, 2], BF16)
                gv, tmpf, ctmp = gv4[:, 0, :], gv4[:, 1, :], gv4[:, 2, :]
                sg = [carve(118912, [128, 512], F32)]
                tfb = [(rstd, "rstd"), (lnv, "lnv")]
                wcnt = [0]
                for tt_ in range(2):
                    sl = slice(tt_ * 512, (tt_ + 1) * 512)
                    gsl = slice(t0 + tt_ * 512, t0 + (tt_ + 1) * 512)
                    rms_norm_tile(xres[:, :, gsl], 512, PV_MIXG + l * 8, hTh[:, :, sl], f"hTh{tt_}", sq, rstd, lnv, 7, XR)
                HTH = ["hTh0", "hTh1"]
                pb = 0
                wz, rz = next_w(1536)
                for c in range(4):
                    for tt_ in range(2):
                        sl = slice(tt_ * 512, (tt_ + 1) * 512)
                        for k in range(KC):
                            mm(PSB(pb), wz[:, k, c * 128:(c + 1) * 128], hTh[:, k, sl], k == 0, k == KC - 1,
                               [f"hTh{tt_}", rz], [f"ps{pb}"])
                        act(ybT[:, c, sl], PSB(pb), AF.Gelu, [f"ps{pb}"], ["ybT"])
                        pb = (pb + 1) % 4
                if hf == 0:
                    dma("sp", tls.rearrange("p r c m t -> p r (c m t)"),
                        tl_all.ap().rearrange("(r p) x -> p r x", p=128), ["tl_all"], ["tls"])
                wcb, rcb = next_w(2560)
                for c in range(4):
                    for tt_ in range(2):
                        sl = slice(tt_ * 512, (tt_ + 1) * 512)
                        for k in range(KC):
                            mm(PSB(pb), wcb[:, k, c * 128:(c + 1) * 128], hTh[:, k, sl], k == 0, k == KC - 1,
                               [f"hTh{tt_}", rcb], [f"ps{pb}"])
                        cp(cbT[:, c, sl], PSB(pb), [f"ps{pb}"], ["cbT"])
                        pb = (pb + 1) % 4
                wcc, rcc = next_w(3072)
                wcx, rcx = next_w(3584)
                for c in range(4):
                    for tt_ in range(2):
                        sl = slice(tt_ * 512, (tt_ + 1) * 512)
                        p1 = pb
                        p2 = (pb + 1) % 4
                        pb = (pb + 2) % 4
                        for k in range(KC):
                            mm(PSB(p1), wcc[:, k, c * 128:(c + 1) * 128], hTh[:, k, sl], k == 0, k == KC - 1,
                               [f"hTh{tt_}", rcc], [f"ps{p1}"])
                        for k in range(KC):
                            mm(PSB(p2), wcx[:, k, c * 128:(c + 1) * 128], hTh[:, k, sl], k == 0, k == KC - 1,
                               [f"hTh{tt_}", rcx], [f"ps{p2}"])
                        cp(ctmp, PSB(p1), [f"ps{p1}"], ["gv4_2"], eng="act")
                        tt(yext[:, c, tt_ * 4:(tt_ + 1) * 4, 2:130], ctmp.rearrange("p (m t) -> p m t", t=128),
                           PSB(p2).rearrange("p (m t) -> p m t", t=128), ALU.mult, ["gv4_2", f"ps{p2}"], ["yext"])
                wz2, rz2 = next_w(2048)
                gw = [carve(49152 + i * 8192, [128, KC, 3, 128], BF16) for i in range(2)]
                bw = [carve(65536 + i * 4096, [128, 3, 4, 128], BF16) for i in range(2)]

                def load_gate(dc_):
                    dq_ = (dc_ + 1) % 2
                    gsrc = bass_gate_ap(w_in_d, l, dc_)
                    bsrc = w_br_d[l].rearrange("n (k p) d -> p n k d", p=128)
                    for n_ in range(3):
                        dma("pool", gw[dq_][:, :, n_, :], gsrc[:, :, n_, :], [], [f"wbuf{dq_}"])
                        dma("pool", bw[dq_][:, n_, :, :], bsrc[:, n_, :, dc_ * 128:(dc_ + 1) * 128], [],
                            [("sqA" if dq_ == 0 else "sqB")])

                load_gate(0)
                smc = lambda o, n=4: small[:, o:o + n]
                for grp in range(2):
                    for b_ in range(4):
                        mb = grp * 4 + b_
                        bs = slice(mb * 128, (mb + 1) * 128)
                        for k in range(KC):
                            mm(PSB(b_), hTh[:, k, bs], wz2[:, k, :], k == 0, k == KC - 1, [f"hTh{grp}", rz2], [f"ps{b_}"])
                        act(gv4[:, b_, :], PSB(b_), AF.Gelu, [f"ps{b_}"], [f"gv4_{b_}", "sm_sum"], accum=small[:, b_:b_ + 1])
                    for b_ in range(4):
                        act(vn4[:, b_, :], gv4[:, b_, :], AF.Square, [f"gv4_{b_}"], ["vn4", "sm_sq"], accum=small[:, 4 + b_:5 + b_])
                    ts(smc(8), smc(0), -1.0 / 512, None, ALU.mult, None, ["sm_sum"], ["sm_nm"])
                    tt(smc(12), smc(8), smc(8), ALU.mult, ["sm_nm"], ["sm_m2"])
                    ts(smc(16), smc(4), 1.0 / 512, 1e-5, ALU.mult, ALU.add, ["sm_sq"], ["sm_ve"])
                    tt(smc(20), smc(16), smc(12), ALU.subtract, ["sm_ve", "sm_m2"], ["sm_var"])
                    S.add("pool", lambda e: e.tensor_tensor(smc(24), smc(20), small[:, 62:63].broadcast_to([128, 4]), ALU.pow),
                          ["sm_var", "smc"], ["sm_rstd"])
                    for b_ in range(4):
                        ts(gv4[:, b_, :], gv4[:, b_, :], small[:, 8 + b_:9 + b_], small[:, 24 + b_:25 + b_], ALU.add, ALU.mult,
                           [f"gv4_{b_}", "sm_nm", "sm_rstd"], [f"gv4_{b_}"])
                    G4 = [f"gv4_{i}" for i in range(4)]
                    tt(gv4, gv4, bcp[:, BC_LNG: BC_LNG + 512].unsqueeze(1).broadcast_to([128, 4, 512]), ALU.mult,
                       G4 + ["bcp"], G4)
                    tt(vn4, gv4, bcp[:, BC_LNB: BC_LNB + 512].unsqueeze(1).broadcast_to([128, 4, 512]), ALU.add,
                       G4 + ["bcp"], ["vn4"])
                    for b_ in range(4):
                        for gi in range(4):
                            mm(PSB(4 + b_)[:, gi * 128:(gi + 1) * 128], vn4[:, b_, gi * 128:(gi + 1) * 128],
                               wmT[:, (l * 4 + gi) * 128:(l * 4 + gi + 1) * 128], True, True, ["vn4", "wmT"], [f"ps{4 + b_}"])
                    for b_ in range(4):
                        mb = grp * 4 + b_
                        bs = slice(mb * 128, (mb + 1) * 128)
                        tf_, T_ = tfb[b_ % 2]
                        tt(tf_, PSB(4 + b_), bcp[:, BC_BSP: BC_BSP + 512], ALU.add, [f"ps{4 + b_}", "bcp"], [T_])
                        tt(ybT[:, :, bs], tf_.rearrange("p (g t) -> p g t", t=128), ybT[:, :, bs], ALU.mult,
                           [T_, "ybT"], ["ybT"])
                selA = pvec[:, PV_SEL:PV_SEL + 1]
                selB = pvec[:, PV_SEL + 1:PV_SEL + 2]
                halo_t = small[:, 28:60].bitcast(BF16).rearrange("p (c j t) -> p c j t", c=4, t=2)
                jb = hf * 4
                for par in range(2):
                    if par == 0:
                        candB = tls[:, 0, :, 2 * jb:2 * jb + 8:2, :]
                        if hf == 0:
                            jlo = 1
                            candA = tls[:, 0, :, 1:6:2, :]
                        else:
                            jlo = 0
                            candA = tls[:, 0, :, 2 * jb - 1:2 * jb + 6:2, :]
                    else:
                        jlo = 0
                        candA = tls[:, 1, :, 2 * jb + 1:2 * jb + 8:2, :]
                        candB = tls[:, 1, :, 2 * jb:2 * jb + 8:2, :]
                    for c in range(4):
                        dst = yext[:, c, par:8:2, 0:2]
                        ht = halo_t[:, c, 0:4, :]
                        ts(ht, candB[:, c], selB, None, ALU.mult, None, ["tls", "pvec"], ["halo_t"])
                        if jlo == 1:
                            cp(dst[:, 0:1, :], ht[:, 0:1, :], ["halo_t"], ["yext"], eng="dve")
                            stt(dst[:, 1:4, :], candA[:, c], selA, ht[:, 1:4, :], ALU.mult, ALU.add,
                                ["tls", "pvec", "halo_t"], ["yext"])
                        else:
                            stt(dst, candA[:, c], selA, ht, ALU.mult, ALU.add, ["tls", "pvec", "halo_t"], ["yext"])
                for c in range(4):
                    for tt_ in range(2):
                        sl = slice(tt_ * 512, (tt_ + 1) * 512)
                        ye = yext[:, c, tt_ * 4:(tt_ + 1) * 4, :]
                        acc = ctmp.rearrange("p (m t) -> p m t", t=128)
                        wcol = lambda k: pvec[:, PV_CONV + (l * 3 + k) * 4 + c: PV_CONV + (l * 3 + k) * 4 + c + 1]
                        ts(acc, ye[:, :, 0:128], wcol(0), None, ALU.mult, None, ["yext", "pvec"], ["gv4_2"])
                        stt(acc, ye[:, :, 1:129], wcol(1), acc, ALU.mult, ALU.add, ["yext", "pvec", "gv4_2"], ["gv4_2"])
                        stt(acc, ye[:, :, 2:130], wcol(2), acc, ALU.mult, ALU.add, ["yext", "pvec", "gv4_2"], ["gv4_2"])
                        tt(ycT[:, c, sl], ctmp, cbT[:, c, sl], ALU.mult, ["gv4_2", "cbT"], ["ycT"])
                gw = [carve(49152 + i * 8192, [128, KC, 3, 128], BF16) for i in range(2)]
                bw = [carve(65536 + i * 4096, [128, 3, 4, 128], BF16) for i in range(2)]
                brs = [(yaT, "yaT", t0), (ybT, "ybT", 0), (ycT, "ycT", 0)]
                gcnt = 0
                for dc in range(KC):
                    dq = (dc + 1) % 2
                    g_ = gw[dq]
                    b_ = bw[dq]
                    if dc > 0:
                        load_gate(dc)
                    for tt_ in range(2):
                        sl = slice(tt_ * 512, (tt_ + 1) * 512)
                        for n in range(3):
                            pg = (gcnt % 4) * 2
                            pbq = pg + 1
                            gcnt += 1
                            for k in range(KC):
                                mm(PSB(pg), g_[:, k, n, :], hTh[:, k, sl], k == 0, k == KC - 1,
                                   [f"hTh{tt_}", f"wbuf{dq}"], [f"ps{pg}"])
                            src, sres, o0 = brs[n]
                            for k in range(4):
                                mm(PSB(pbq), b_[:, n, k, :], src[:, k, o0 + tt_ * 512: o0 + (tt_ + 1) * 512],
                                   k == 0, k == 3, [sres, ("sqA" if dq == 0 else "sqB")], [f"ps{pbq}"])
                            act(sg[0], PSB(pg), AF.Sigmoid, [f"ps{pg}"], ["vn4"])
                            if n == 0:
                                tt(gv, sg[0], PSB(pbq), ALU.mult, ["vn4", f"ps{pbq}"], ["gv4_0"])
                            elif n == 1:
                                tt(tmpf, sg[0], PSB(pbq), ALU.mult, ["vn4", f"ps{pbq}"], ["gv4_1"])
                                tt(gv, gv, tmpf, ALU.add, ["gv4_0", "gv4_1"], ["gv4_0"])
                            else:
                                tt(tmpf, sg[0], PSB(pbq), ALU.mult, ["vn4", f"ps{pbq}"], ["gv4_1"])
                                tt(mrgT[:, dc, sl], gv, tmpf, ALU.add, ["gv4_0", "gv4_1"], ["mrgT"])
                w_out_v = w_out_d[l].rearrange("(k p) c -> p k c", p=128)
                pb = 0
                for cg in range(2):
                    wo_, ro_ = next_w(cg * 512, src=w_out_v)
                    for oc4 in range(4):
                        oc = cg * 4 + oc4
                        for tt_ in range(2):
                            sl = slice(tt_ * 512, (tt_ + 1) * 512)
                            gsl = slice(t0 + tt_ * 512, t0 + (tt_ + 1) * 512)
                            for k in range(KC):
                                mm(PSB(pb), wo_[:, k, oc4 * 128:(oc4 + 1) * 128], mrgT[:, k, sl], k == 0, k == KC - 1,
                                   ["mrgT", ro_], [f"ps{pb}"])
                            tt(xres[:, oc, gsl], xres[:, oc, gsl], PSB(pb), ALU.add, [f"x{oc}", f"ps{pb}"], [f"x{oc}"])
                            pb = (pb + 1) % 4

            S.barrier()
            if stop_after == f"mix{l}":
                break

            hT = carve(0, [128, KC, T], BF16)
            qx = carve(32768, [128, KC, T], BF16)
            wbuf = [carve(65536 + i * 8192, [128, KC, 512], BF16) for i in range(2)]
            sq = carve(81920, [128, KC, 512], BF16)
            rstd = carve(90112, [128, 512], F32)
            lnv = carve(92160, [128, 512], F32)
            memf = carve(94208, [128, KC, 256], F32)
            mTn = carve(102400, [128, KC, 256], BF16)
            kTx = carve(106496, [128, KC, 256], BF16)
            vx = carve(110592, [128, 2, D], BF16)
            pT = [carve(114688, [128, 2, 512], BF16), carve(118784, [128, 2, 512], BF16)]
            recb = [carve(116736, [128, 512], F32), carve(120832, [128, 512], F32)]
            wcnt = [0]
            for tt_ in range(4):
                sl = slice(tt_ * 512, (tt_ + 1) * 512)
                rms_norm_tile(xres[:, :, sl], 512, PV_XAG + l * 8, hT[:, :, sl], f"hT{tt_}", sq, rstd, lnv, 7, XR)
            dma("sp", memf, memT_d.rearrange("(c p) t -> p c t", p=128), [], ["memf"])
            rms_norm_tile(memf, 256, PV_MEMG + l * 8, mTn, "mTn", sq[:, :, 0:256], rstd[:, 0:256], lnv[:, 0:256], 6, ["memf"])
            pb = 0
            wk_v = wk_d[l].rearrange("(k p) c -> p k c", p=128)
            wv_v = wv_d[l].rearrange("(k p) c -> p k c", p=128)
            wq_v = wq_d[l].rearrange("(k p) c -> p k c", p=128)
            wo_v = wo_d[l].rearrange("(k p) c -> p k c", p=128)
            for cg in range(2):
                w_, r_ = next_w(cg * 512, src=wk_v)
                for dc4 in range(4):
                    dc = cg * 4 + dc4
                    for k in range(KC):
                        mm(PSB(pb)[:, 0:256], w_[:, k, dc4 * 128:(dc4 + 1) * 128], mTn[:, k, :], k == 0, k == KC - 1,
                           ["mTn", r_], [f"ps{pb}"])
                    cp(kTx[:, dc, :], PSB(pb)[:, 0:256], [f"ps{pb}"], ["kTx"])
                    pb = (pb + 1) % 4
            for cg in range(2):
                w_, r_ = next_w(cg * 512, src=wv_v)
                for kc in range(2):
                    for k in range(KC):
                        mm(PSB(pb), mTn[:, k, kc * 128:(kc + 1) * 128], w_[:, k, :], k == 0, k == KC - 1,
                           ["mTn", r_], [f"ps{pb}"])
                    cp(vx[:, kc, cg * 512:(cg + 1) * 512], PSB(pb), [f"ps{pb}"], ["vx"])
                    pb = (pb + 1) % 4
            for cg in range(2):
                w_, r_ = next_w(cg * 512, src=wq_v)
                for dc4 in range(4):
                    dc = cg * 4 + dc4
                    for tt_ in range(4):
                        sl = slice(tt_ * 512, (tt_ + 1) * 512)
                        for k in range(KC):
                            mm(PSB(pb), w_[:, k, dc4 * 128:(dc4 + 1) * 128], hT[:, k, sl], k == 0, k == KC - 1,
                               [f"hT{tt_}", r_], [f"ps{pb}"])
                        if (dc * 4 + tt_) % 2 == 0:
                            act(qx[:, dc, sl], PSB(pb), AF.Copy, [f"ps{pb}"], ["qx"], scale=0.0625)
                        else:
                            ts(qx[:, dc, sl], PSB(pb), 0.0625, None, ALU.mult, None, [f"ps{pb}"], ["qx"])
                        pb = (pb + 1) % 4
            oT = hT
            it = 0
            for tt_ in range(4):
                sl = slice(tt_ * 512, (tt_ + 1) * 512)
                for hx in range(4):
                    q_ = it % 2
                    it += 1
                    pT_, rec_ = pT[q_], recb[q_]
                    P_, R_ = f"pT{q_}", f"rec{q_}"
                    bd = 2 + q_
                    bo = 4 + 2 * q_
                    for kc in range(2):
                        for dd in range(2):
                            mm(PSB(kc), kTx[:, hx * 2 + dd, kc * 128:(kc + 1) * 128], qx[:, hx * 2 + dd, sl],
                               dd == 0, dd == 1, ["kTx", "qx"], [f"ps{kc}"])
                    act(pT_, psum[:, 0:2, :], AF.Exp, ["ps0", "ps1"], [P_])
                    for kc in range(2):
                        mm(PSB(bd), ones_b, pT_[:, kc, :], kc == 0, kc == 1, [P_, "cb16"], [f"ps{bd}"])
                    for dd in range(2):
                        for kc in range(2):
                            mm(PSB(bo + dd), vx[:, kc, hx * 256 + dd * 128: hx * 256 + (dd + 1) * 128], pT_[:, kc, :],
                               kc == 0, kc == 1, [P_, "vx"], [f"ps{bo + dd}"])
                    act(rec_, PSB(bd), AF.Ln, [f"ps{bd}"], [R_])
                    act(rec_, rec_, AF.Exp, [R_], [R_], scale=-1.0)
                    for dd in range(2):
                        tt(oT[:, hx * 2 + dd, sl], PSB(bo + dd), rec_, ALU.mult, [f"ps{bo + dd}", R_], [f"hT{tt_}"])
            pb = 0
            for cg in range(2):
                w_, r_ = next_w(cg * 512, src=wo_v)
                for oc4 in range(4):
                    oc = cg * 4 + oc4
                    for tt_ in range(4):
                        sl = slice(tt_ * 512, (tt_ + 1) * 512)
                        for k in range(KC):
                            mm(PSB(pb), w_[:, k, oc4 * 128:(oc4 + 1) * 128], oT[:, k, sl], k == 0, k == KC - 1,
                               [f"hT{tt_}", r_], [f"ps{pb}"])
                        tt(xres[:, oc, sl], xres[:, oc, sl], PSB(pb), ALU.add, [f"x{oc}", f"ps{pb}"], [f"x{oc}"])
                        pb = (pb + 1) % 4
            S.barrier()
            if stop_after == f"xa{l}":
                break

            wg_v = wg_d[l].rearrange("(k p) c -> p k c", p=128)
            wu_v = wu_d[l].rearrange("(k p) c -> p k c", p=128)
            wd_v = wd_d[l].rearrange("(k p) c -> p k c", p=128)
            for hf in range(2):
                t0 = hf * 1024
                hTh = carve(0, [128, KC, 1024], BF16)
                h1T = carve(16384, [128, NJ, 1024], BF16)
                wgb = [carve(61440 + i * 8192, [128, KC, 512], BF16) for i in range(2)]
                wub = [carve(77824 + i * 8192, [128, KC, 512], BF16) for i in range(2)]
                sq = carve(94208, [128, KC, 512], BF16)
                rstd = carve(102400, [128, 512], F32)
                lnv = carve(104448, [128, 512], F32)
                sgf = carve(106496, [128, 512], F32)
                tf = carve(108544, [128, 512], F32)
                for tt_ in range(2):
                    sl = slice(tt_ * 512, (tt_ + 1) * 512)
                    gsl = slice(t0 + tt_ * 512, t0 + (tt_ + 1) * 512)
                    rms_norm_tile(xres[:, :, gsl], 512, PV_FFNG + l * 8, hTh[:, :, sl], f"hTh{tt_}", sq, rstd, lnv, 7, XR)
                ngrp = 6
                pcount = 0
                for gi in range(ngrp):
                    c0 = gi * 512
                    ncol = min(512, FH - c0)
                    i = gi % 2
                    dma("pool", wgb[i][:, :, 0:ncol], wg_v[:, :, c0:c0 + ncol], [], [f"wgb{i}"])
                    dma("pool", wub[i][:, :, 0:ncol], wu_v[:, :, c0:c0 + ncol], [], [f"wub{i}"])
                    for jj in range(ncol // 128):
                        j = gi * 4 + jj
                        for tt_ in range(2):
                            sl = slice(tt_ * 512, (tt_ + 1) * 512)
                            pg = (pcount % 4) * 2
                            pu = pg + 1
                            pcount += 1
                            for k in range(KC):
                                mm(PSB(pg), wgb[i][:, k, jj * 128:(jj + 1) * 128], hTh[:, k, sl], k == 0, k == KC - 1,
                                   [f"hTh{tt_}", f"wgb{i}"], [f"ps{pg}"])
                            for k in range(KC):
                                mm(PSB(pu), wub[i][:, k, jj * 128:(jj + 1) * 128], hTh[:, k, sl], k == 0, k == KC - 1,
                                   [f"hTh{tt_}", f"wub{i}"], [f"ps{pu}"])
                            act(sgf, PSB(pg), AF.Sigmoid, [f"ps{pg}"], ["sgf"])
                            tt(tf, sgf, PSB(pg), ALU.mult, ["sgf", f"ps{pg}"], ["tf"])
                            tt(h1T[:, j, sl], tf, PSB(pu), ALU.mult, ["tf", f"ps{pu}"], ["h1T"])
                wdb = [carve(110592, [128, NJ, 256], BF16), carve(61440, [128, NJ, 256], BF16)]
                wdn = [["wdb0"], ["wgb0", "wgb1"]]
                pb = 0
                for cg in range(4):
                    i = cg % 2
                    dma("pool", wdb[i], wd_v[:, :, cg * 256:(cg + 1) * 256], [], wdn[i])
                    for oc2 in range(2):
                        oc = cg * 2 + oc2
                        for tt_ in range(2):
                            sl = slice(tt_ * 512, (tt_ + 1) * 512)
                            gsl = slice(t0 + tt_ * 512, t0 + (tt_ + 1) * 512)
                            for j in range(NJ):
                                mm(PSB(pb), wdb[i][:, j, oc2 * 128:(oc2 + 1) * 128], h1T[:, j, sl], j == 0, j == NJ - 1,
                                   ["h1T"] + wdn[i], [f"ps{pb}"])
                            tt(xres[:, oc, gsl], xres[:, oc, gsl], PSB(pb), ALU.add, [f"x{oc}", f"ps{pb}"], [f"x{oc}"])
                            pb = (pb + 1) % 4
            S.barrier()
            if stop_after == f"ffn{l}":
                break

        yT_v = yT_d.rearrange("(c p) t -> p c t", p=128)
        if stop_after is None and final:
            sq = carve(0, [128, KC, 512], BF16)
            rstd = carve(8192, [128, 512], F32)
            lnv = carve(10240, [128, 512], F32)
            outb = [carve(16384 + i * 16384, [128, KC, 512], F32) for i in range(2)]
            for tt_ in range(4):
                sl = slice(tt_ * 512, (tt_ + 1) * 512)
                ob = outb[tt_ % 2]
                rms_norm_tile(xres[:, :, sl], 512, PV_FING, ob, f"outb{tt_ % 2}", sq, rstd, lnv, 7, XR)
                dma("sp", yT_v[:, :, sl], ob, [f"outb{tt_ % 2}"], ["yT"])
        else:
            for c in range(KC):
                dma("sp", yT_v[:, c, :], xres[:, c, :], [f"x{c}"], ["yT"])
        fw = S.final_wait("sp")

        @block.tensor
        def _(e):
            S.emit("pe", e)

        @block.scalar
        def _(e):
            S.emit("act", e)

        @block.vector
        def _(e):
            S.emit("dve", e)

        @block.gpsimd
        def _(e):
            S.emit("pool", e)

        @block.sync
        def _(e):
            S.emit("sp", e)
            for s, v in fw:
                e.wait_ge(sems[s], v)

    return nc


def bass_gate_ap(w_in_d, l, dc):
    v = w_in_d[l].rearrange("(k p) c -> p k c", p=128)[:, :, 4096:7168]
    return v.rearrange("p k (n c) -> p k n c", n=3)[:, :, :, dc * 128:(dc + 1) * 128]


_CACHE = {}


def _owned_blocks(role):
    out = []
    for j in range(8):
        out += ([4 * j, 4 * j + 3] if role == 0 else [4 * j + 1, 4 * j + 2])
    return out


def _prep_inputs(inp):
    f = np.float32
    x = np.asarray(inp["x"], f)
    mem = np.asarray(inp["mem"], f)
    gv = lambda k: np.asarray(inp[k], f)

    def pcols(a):
        a = a.reshape(-1, 8, 128)
        return np.ascontiguousarray(a.transpose(2, 0, 1).reshape(128, -1))

    pvec_base = np.zeros((128, NPV), f)
    pvec_base[:, PV_MIXG:PV_MIXG + 16] = pcols(gv("norm_mix_g"))
    pvec_base[:, PV_XAG:PV_XAG + 16] = pcols(gv("norm_xa_g"))
    pvec_base[:, PV_MEMG:PV_MEMG + 16] = pcols(gv("mem_norm_g"))
    pvec_base[:, PV_FFNG:PV_FFNG + 16] = pcols(gv("norm_ffn_g"))
    pvec_base[:, PV_FING:PV_FING + 8] = pcols(gv("final_g")[None])
    cw = gv("conv_w").reshape(L * 3, 4, 128)
    pvec_base[:, PV_CONV:PV_CONV + 24] = cw.transpose(2, 0, 1).reshape(128, 24)

    bc = np.zeros((128, NBC), f)
    for l_ in range(L):
        o = l_ * 1536
        bc[:, o:o + 512] = np.broadcast_to(gv("sgu_ln_g")[l_].reshape(1, -1), (128, 512))
        bc[:, o + 512:o + 1024] = np.broadcast_to(gv("sgu_ln_b")[l_].reshape(1, -1), (128, 512))
        bc[:, o + 1024:o + 1536] = np.broadcast_to(gv("b_spatial")[l_].reshape(1, -1), (128, 512))
    wspT = np.ascontiguousarray(gv("w_spatial").transpose(3, 0, 1, 2).reshape(128, L * 4 * 128))

    j = np.arange(128)
    negtri = -(j[:, None] >= j[None, :]).astype(f)
    sgum = ((j[None, :] // 64) >= (j[:, None] // 64)).astype(f)
    diag = (j[:, None] < j[None, :]).astype(f)
    full = np.ones((128, 128), f)
    none = np.zeros((128, 128), f)
    amasks = {0: [[diag, none], [full, diag]], 1: [[full, diag], [diag, none]]}

    shared = {k: np.ascontiguousarray(gv(k)) for k in
              ["w_in", "w_branch", "w_out", "w_q_xa", "w_k_xa", "w_v_xa", "w_o_xa",
               "w_gate_ffn", "w_up_ffn", "w_down_ffn"]}
    in_maps = []
    for core in range(8):
        b, role = divmod(core, 2)
        blocks = _owned_blocks(role)
        xb = x[b].reshape(32, 128, D)[blocks].reshape(T, D)
        cst = np.zeros((128, NCST), f)
        cst[:, CS_NEGTRI:CS_NEGTRI + 128] = negtri
        cst[:, CS_SGUM:CS_SGUM + 128] = sgum
        for par in range(2):
            for w in range(2):
                o = CS_AMASK + (par * 2 + w) * 128
                cst[:, o:o + 128] = amasks[role][par][w]
        pv = pvec_base.copy()
        pv[:, PV_SEL] = 1.0 if role == 0 else 0.0
        pv[:, PV_SEL + 1] = 0.0 if role == 0 else 1.0
        m = dict(shared)
        m.update({"xT": np.ascontiguousarray(xb.T), "memT": np.ascontiguousarray(mem[b].T),
                  "pvec": pv, "cst": cst, "bc": bc, "wspT": wspT})
        in_maps.append(m)
    return in_maps


def _assemble(results):
    out = np.zeros((4, 4096, D), np.float32)
    for core in range(8):
        b, role = divmod(core, 2)
        blocks = _owned_blocks(role)
        y = np.asarray(results[core]["yT"]).T.reshape(NB, 128, D)
        out[b].reshape(32, 128, D)[blocks] = y
    return out


FUSED = True


def kernel(**inputs):
    in_maps = _prep_inputs(inputs)
    if FUSED:
        if "nc" not in _CACHE:
            _CACHE["nc"] = build_program()
        res = run_bass_kernel_spmd(_CACHE["nc"], in_maps, core_ids=list(range(8)))
        return _assemble(res.results)
    if "nc0" not in _CACHE:
        _CACHE["nc0"] = build_program(layers=(0,), final=False)
        _CACHE["nc1"] = build_program(layers=(1,), final=True)
    res0 = run_bass_kernel_spmd(_CACHE["nc0"], in_maps, core_ids=list(range(8)))
    for c in range(8):
        in_maps[c]["xT"] = np.ascontiguousarray(np.asarray(res0.results[c]["yT"], np.float32))
    res1 = run_bass_kernel_spmd(_CACHE["nc1"], in_maps, core_ids=list(range(8)))
    return _assemble(res1.results)
```

```python
import numpy as np
import concourse.bass as bass
import concourse.mybir as mybir
from concourse.bass_utils import run_bass_kernel_spmd

F32 = mybir.dt.float32
BF16 = mybir.dt.bfloat16
AF = mybir.ActivationFunctionType
ALU = mybir.AluOpType

L = 2
D = 1024
T = 2048
NB = 16
KC = 8
FH = 2816
NJ = 22
INC = 7168
PAIRS = [[0, 1], [2, 3], [4, 5], [6, 7]]

PV_MIXG = 0
PV_XAG = 16
PV_MEMG = 32
PV_FFNG = 48
PV_FING = 64
PV_CONV = 72
PV_SEL = 96
NPV = 98
CS_NEGTRI = 0
CS_SGUM = 128
CS_AMASK = 256
NCST = 768
BC_LNG = 0
BC_LNB = 512
BC_BSP = 1024
NBC = 3072


def gblock(g):
    j, i = divmod(g, 4)
    return [(0, 2 * j), (1, 2 * j), (1, 2 * j + 1), (0, 2 * j + 1)][i]


class Sched:
    def __init__(self, nc, sems):
        self.nc = nc
        self.sems = sems
        self.engs = ["pe", "act", "dve", "pool", "sp"]
        self.ops = {e: [] for e in self.engs}
        self.count = {e: 0 for e in self.engs}
        self.waited = {e: {} for e in self.engs}
        self.res = {}
        self.dq = {"sp": [f"D_sp{i}" for i in range(8)], "pool": [f"D_pl{i}" for i in range(8)]}
        self.dnext = {"sp": 0, "pool": 0}
        self.dcnt = {}
        self.pending_barrier = {e: {} for e in self.engs}
        self.cc_n = 0

    def _r(self, k):
        if k not in self.res:
            self.res[k] = {"w": None, "r": {}}
        return self.res[k]

    def add(self, eng, fn, reads=(), writes=(), kind="c"):
        deps = {}

        def need(tok):
            if tok is None:
                return
            s, v = tok
            if deps.get(s, 0) < v:
                deps[s] = v

        for k in reads:
            need(self._r(k)["w"])
        for k in writes:
            r = self._r(k)
            need(r["w"])
            for s, v in r["r"].items():
                need((s, v))
        for s, v in self.pending_barrier[eng].items():
            need((s, v))
        self.pending_barrier[eng] = {}
        if kind == "d":
            q = self.dq[eng]
            dsem = q[self.dnext[eng] % len(q)]
            self.dnext[eng] += 1
            prev = self.dcnt.get(dsem, 0)
            if prev:
                need((dsem, prev))
            self.dcnt[dsem] = prev + 16
            tok = (dsem, prev + 16)
            inc = (dsem, 16)
        elif kind == "cc":
            name = f"CC{self.cc_n}"
            self.cc_n += 1
            tok = (name, 1)
            inc = (name, None)
        else:
            self.count[eng] += 1
            tok = (f"S_{eng}", self.count[eng])
            inc = (f"S_{eng}", 1)
        waits = []
        for s, v in deps.items():
            if eng == "pe" and s == "S_pe":
                continue
            if self.waited[eng].get(s, 0) >= v:
                continue
            self.waited[eng][s] = v
            waits.append((s, v))
        self.ops[eng].append((waits, fn, inc))
        for k in reads:
            r = self._r(k)
            if r["r"].get(tok[0], 0) < tok[1]:
                r["r"][tok[0]] = tok[1]
        for k in writes:
            r = self._r(k)
            r["w"] = tok
            r["r"] = {}
        return tok

    def barrier(self):
        snap = {}
        for e in ["pe", "act", "dve", "pool"]:
            if self.count[e]:
                snap[f"S_{e}"] = self.count[e]
        for s, v in self.dcnt.items():
            snap[s] = v
        for i in range(self.cc_n):
            snap[f"CC{i}"] = 1
        for e in self.engs:
            self.pending_barrier[e] = dict(snap)

    def emit(self, eng, handle):
        for waits, fn, inc in self.ops[eng]:
            for s, v in waits:
                handle.wait_ge(self.sems[s], v)
            ins = fn(handle)
            if inc[1] is None:
                ins.then_inc(self.sems[inc[0]])
            else:
                ins.then_inc(self.sems[inc[0]], inc[1])

    def final_wait(self, eng):
        self.barrier()
        deps = self.pending_barrier[eng]
        waits = [(s, v) for s, v in deps.items() if self.waited[eng].get(s, 0) < v]
        return waits


def build_program(stop_after=None, layers=(0, 1), final=True):
    nc = bass.Bass("TRN2", target_bir_lowering=False)

    def din(name, shape):
        return nc.dram_tensor(name, list(shape), F32, kind="ExternalInput").ap()

    xT_d = din("xT", [D, T])
    memT_d = din("memT", [D, 256])
    pvec_d = din("pvec", [128, NPV])
    cst_d = din("cst", [128, NCST])
    bc_d = din("bc", [128, NBC])
    wspT_d = din("wspT", [128, L * 4 * 128])
    w_in_d = din("w_in", [L, D, INC])
    w_br_d = din("w_branch", [L, 3, 512, D])
    w_out_d = din("w_out", [L, D, D])
    wq_d = din("w_q_xa", [L, D, D])
    wk_d = din("w_k_xa", [L, D, D])
    wv_d = din("w_v_xa", [L, D, D])
    wo_d = din("w_o_xa", [L, D, D])
    wg_d = din("w_gate_ffn", [L, D, FH])
    wu_d = din("w_up_ffn", [L, D, FH])
    wd_d = din("w_down_ffn", [L, FH, D])
    yT_d = nc.dram_tensor("yT", [D, T], F32, kind="ExternalOutput").ap()

    scratch = {}
    for l_ in range(L):
        scratch[l_] = dict(
            kt_src=nc.dram_tensor(f"kt_src{l_}", [NB * 128, 512], BF16),
            kt_all=nc.dram_tensor(f"kt_all{l_}", [2 * NB * 128, 512], BF16),
            v_src=nc.dram_tensor(f"v_src{l_}", [NB * 128, 512], BF16),
            v_all=nc.dram_tensor(f"v_all{l_}", [2 * NB * 128, 512], BF16),
            tl_src=nc.dram_tensor(f"tl_src{l_}", [128, 128], BF16),
            tl_all=nc.dram_tensor(f"tl_all{l_}", [256, 128], BF16))

    sem_names = ["S_pe", "S_act", "S_dve", "S_pool"] + [f"D_sp{i}" for i in range(8)] + \
                [f"D_pl{i}" for i in range(8)] + [f"CC{i}" for i in range(3 * L)]

    import contextlib
    with contextlib.ExitStack() as es:
        xres_t = es.enter_context(nc.sbuf_tensor("xres", [128, KC, T], F32))
        ARENA_B = 122 * 1024
        arena = es.enter_context(nc.sbuf_tensor("arena", [128, ARENA_B // 2], BF16))
        pvec = es.enter_context(nc.sbuf_tensor("pvec_sb", [128, NPV], F32))
        cst = es.enter_context(nc.sbuf_tensor("cst_sb", [128, NCST], F32))
        bcp = es.enter_context(nc.sbuf_tensor("bcp", [128, 1536], F32))
        cb16 = es.enter_context(nc.sbuf_tensor("cb16", [128, 1280], BF16))
        wmT = es.enter_context(nc.sbuf_tensor("wmT_sb", [128, L * 4 * 128], BF16))
        small = es.enter_context(nc.sbuf_tensor("small", [128, 64], F32))
        psum = es.enter_context(nc.psum_tensor("ps", [128, 8, 512], F32))
        sems = {n: es.enter_context(nc.semaphore(n)) for n in sem_names}
        block = es.enter_context(nc.Block())

        S = Sched(nc, sems)
        xres = xres_t

        negtri_b = cb16[:, 0:128]
        negones_b = cb16[:, 128:256]
        ones_b = cb16[:, 256:384]
        amask_b = [[cb16[:, 384 + (p * 2 + w) * 128: 384 + (p * 2 + w + 1) * 128] for w in range(2)] for p in range(2)]
        sgum_b = cb16[:, 896:1024]

        def carve(off, shape, dt):
            es_ = 2 if dt == BF16 else 4
            n = int(np.prod(shape[1:]))
            assert off % 4 == 0 and off + n * es_ <= ARENA_B, (off, shape)
            ap = arena[:, off // 2: off // 2 + n * es_ // 2]
            if dt == F32:
                ap = ap.bitcast(F32)
            if len(shape) == 3:
                ap = ap.rearrange("p (a b) -> p a b", b=shape[2])
            elif len(shape) == 4:
                ap = ap.rearrange("p (a b c) -> p a b c", b=shape[2], c=shape[3])
            elif len(shape) == 5:
                ap = ap.rearrange("p (a b c d) -> p a b c d", b=shape[2], c=shape[3], d=shape[4])
            return ap

        PSB = lambda b: psum[:, b, :]
        evac_flip = [0]

        def mm(out, lhsT, rhs, start, stop, reads, writes):
            S.add("pe", lambda e: e.matmul(out, lhsT, rhs, start=start, stop=stop), reads, writes)

        def act(out, in_, func, reads, writes, scale=1.0, bias=0.0, accum=None):
            kw = {"scale": scale}
            if bias != 0.0:
                kw["bias"] = bias
            if accum is not None:
                kw["accum_out"] = accum
            S.add("act", lambda e: e.activation(out, in_, func, **kw), reads, writes)

        def tt(out, in0, in1, op, reads, writes, eng="dve"):
            S.add(eng, lambda e: e.tensor_tensor(out, in0, in1, op), reads, writes)

        def ts(out, in0, s1, s2, op0, op1, reads, writes, eng="dve"):
            if op1 is None:
                S.add(eng, lambda e: e.tensor_scalar(out, in0, s1, None, op0), reads, writes)
            else:
                S.add(eng, lambda e: e.tensor_scalar(out, in0, s1, s2, op0, op1), reads, writes)

        def stt(out, in0, scalar, in1, op0, op1, reads, writes):
            S.add("dve", lambda e: e.scalar_tensor_tensor(out, in0, scalar, in1, op0, op1), reads, writes)

        def cp(out, in_, reads, writes, eng=None):
            if eng is None:
                eng = "act" if evac_flip[0] % 2 == 0 else "dve"
                evac_flip[0] += 1
            if eng == "act":
                S.add("act", lambda e: e.activation(out, in_, AF.Copy), reads, writes)
            else:
                S.add(eng, lambda e: e.tensor_copy(out, in_), reads, writes)

        def dma(q, out, in_, reads, writes):
            S.add(q, lambda e: e.dma_start(out=out, in_=in_), reads, writes, kind="d")

        def allgather(src, dst, reads, writes):
            import os
            if os.environ.get("NO_CC"):
                S.add("pool", lambda e: e.dma_start(out=dst.ap()[0:src.shape[0], :], in_=src.ap()), reads, writes, kind="d")
                return
            S.add("pool", lambda e: e.collective_compute(
                "AllGather", ALU.bypass, replica_groups=PAIRS, ins=[src.ap().opt()], outs=[dst.ap().opt()]),
                reads, writes, kind="cc")

        xT_v = xT_d.rearrange("(c p) t -> p c t", p=128)
        for c in range(KC):
            dma("sp", xres[:, c, :], xT_v[:, c, :], [], [f"x{c}"])
        dma("sp", pvec[:], pvec_d[:, :], [], ["pvec"])
        dma("sp", cst[:], cst_d[:, :], [], ["cst"])
        cp(negtri_b, cst[:, CS_NEGTRI:CS_NEGTRI + 128], ["cst"], ["cb16"], eng="dve")
        S.add("dve", lambda e: e.memset(negones_b, -1.0), [], ["cb16"])
        S.add("dve", lambda e: e.memset(ones_b, 1.0), [], ["cb16"])
        S.add("dve", lambda e: e.memset(small[:, 62:63], -0.5), [], ["smc"])
        cp(cb16[:, 384:896], cst[:, CS_AMASK:CS_AMASK + 512], ["cst"], ["cb16"], eng="dve")
        wsp_stage = carve(0, [128, L * 4 * 128], F32)
        dma("sp", wsp_stage, wspT_d[:, :], [], ["wsp_stage"])
        tt(wmT[:].rearrange("p (a t) -> p a t", t=128), wsp_stage.rearrange("p (a t) -> p a t", t=128),
           cst[:, CS_SGUM:CS_SGUM + 128].unsqueeze(1).broadcast_to([128, L * 4, 128]), ALU.mult,
           ["wsp_stage", "cst"], ["wmT"])
        S.barrier()

        XR = [f"x{c}" for c in range(KC)]

        def rms_norm_tile(src3, ncols, gbase, hT_out3, hres, sq, rstd, lnv, psb, srcres, out_f32=False):
            act(sq, src3, AF.Square, srcres, ["sqA", "sqB"])
            for c in range(KC):
                mm(PSB(psb)[:, 0:ncols], ones_b, sq[:, c, :], c == 0, c == KC - 1, ["sqA", "sqB", "cb16"], [f"ps{psb}"])
            act(lnv, PSB(psb)[:, 0:ncols], AF.Ln, [f"ps{psb}"], ["lnv"], scale=1.0 / D, bias=1e-6)
            act(rstd, lnv, AF.Exp, ["lnv"], ["rstd"], scale=-0.5)
            for c in range(KC):
                stt(hT_out3[:, c, :], src3[:, c, :], pvec[:, gbase + c: gbase + c + 1], rstd, ALU.mult, ALU.mult,
                    srcres + ["rstd", "pvec"], [hres])

        def load_w(q, wb, wres, src_ap):
            dma(q, wb, src_ap, [], [wres])

        for l in layers:
            w_in_v = w_in_d[l].rearrange("(k p) c -> p k c", p=128)
            kt_src, kt_all, v_src, v_all, tl_src, tl_all = [scratch[l][k_] for k_ in
                                                            ("kt_src", "kt_all", "v_src", "v_all", "tl_src", "tl_all")]

            hT = carve(0, [128, KC, T], BF16)
            qT = carve(32768, [128, 4, T], BF16)
            wbuf = [carve(49152 + i * 8192, [128, KC, 512], BF16) for i in range(2)]
            sq = carve(65536, [128, KC, 512], BF16)
            rstd = carve(73728, [128, 512], F32)
            lnv = carve(75776, [128, 512], F32)
            kst = [carve(77824 + i * 4096, [128, T], BF16) for i in range(2)]
            vst = [carve(86016 + i * 4096, [128, 4, 512], BF16) for i in range(2)]
            tl_sb = carve(94208, [128, 128], BF16)
            cc_sb = carve(94464, [128, 128], F32)
            wcnt = [0]

            def next_w(col0, ncol=512, src=None):
                i = wcnt[0] % 2
                wcnt[0] += 1
                srcv = w_in_v if src is None else src
                load_w("pool", wbuf[i][:, :, 0:ncol], f"wbuf{i}", srcv[:, :, col0:col0 + ncol])
                return wbuf[i], f"wbuf{i}"

            wb5 = [wbuf[0], wbuf[1]] + [carve(95232 + i * 8192, [128, KC, 512], BF16) for i in range(3)]
            rb5 = ["wbuf0", "wbuf1", "wb5_2", "wb5_3", "wb5_4"]
            for i_, c0_ in enumerate((3072, 3584, 512, 1024, 0)):
                load_w("pool", wb5[i_], rb5[i_], w_in_v[:, :, c0_:c0_ + 512])
            for tt_ in range(4):
                sl = slice(tt_ * 512, (tt_ + 1) * 512)
                rms_norm_tile(xres[:, :, sl], 512, PV_MIXG + l * 8, hT[:, :, sl], f"hT{tt_}", sq, rstd, lnv, 7, XR)
            HT = [f"hT{i}" for i in range(4)]

            wcc, rcc = wb5[0], rb5[0]
            wcx, rcx = wb5[1], rb5[1]
            wk_, rk_ = wb5[2], rb5[2]
            kt_dst = kt_src.ap().rearrange("(m p) (c k) -> p m c k", p=128, k=128)
            pb = 0
            for tt_ in range(4):
                sl = slice(tt_ * 512, (tt_ + 1) * 512)
                ks = kst[tt_ % 2].rearrange("p (c t) -> p c t", t=512)
                for c in range(4):
                    for k in range(KC):
                        mm(PSB(pb), wk_[:, k, c * 128:(c + 1) * 128], hT[:, k, sl], k == 0, k == KC - 1,
                           [f"hT{tt_}", rk_], [f"ps{pb}"])
                    cp(ks[:, c, :], PSB(pb), [f"ps{pb}"], [f"kst{tt_ % 2}"])
                    pb = (pb + 1) % 4
                for c in range(4):
                    dma("sp", kt_dst[:, tt_ * 4:(tt_ + 1) * 4, c, :], ks[:, c, :].rearrange("p (m k) -> p m k", k=128),
                        [f"kst{tt_ % 2}"], ["kt_src"])
            allgather(kt_src, kt_all, ["kt_src"], ["kt_all"])

            wv_, rv_ = wb5[3], rb5[3]
            v_dst = v_src.ap().rearrange("(m p) c -> p m c", p=128)
            for m in range(NB):
                vs = vst[(m // 4) % 2]
                for k in range(KC):
                    mm(PSB(pb), hT[:, k, m * 128:(m + 1) * 128], wv_[:, k, :], k == 0, k == KC - 1,
                       [f"hT{m // 4}", rv_], [f"ps{pb}"])
                cp(vs[:, m % 4, :], PSB(pb), [f"ps{pb}"], [f"vst{(m // 4) % 2}"])
                pb = (pb + 1) % 4
                if m % 4 == 3:
                    dma("sp", v_dst[:, m - 3:m + 1, :], vs, [f"vst{(m // 4) % 2}"], ["v_src"])
            allgather(v_src, v_all, ["v_src"], ["v_all"])

            hT_tail = [hT[:, k, :].rearrange("p (m t) -> p m t", t=128)[:, :, 126:128] for k in range(KC)]
            for (wb_, rb_, half) in ((wcc, rcc, 0), (wcx, rcx, 1)):
                for c in range(4):
                    for k in range(KC):
                        mm(PSB(6)[:, half * 128 + c * 32: half * 128 + (c + 1) * 32], wb_[:, k, c * 128:(c + 1) * 128],
                           hT_tail[k], k == 0, k == KC - 1, HT + [rb_], ["ps6"])
            cp(cc_sb, PSB(6)[:, 0:128], ["ps6"], ["cc_sb"], eng="act")
            tt(tl_sb, cc_sb, PSB(6)[:, 128:256], ALU.mult, ["cc_sb", "ps6"], ["tl_sb"])
            dma("sp", tl_src[:, :], tl_sb, ["tl_sb"], ["tl_src"])
            allgather(tl_src, tl_all, ["tl_src"], ["tl_all"])

            wq_, rq_ = wb5[4], rb5[4]
            for tt_ in range(4):
                sl = slice(tt_ * 512, (tt_ + 1) * 512)
                for c in range(4):
                    for k in range(KC):
                        mm(PSB(pb), wq_[:, k, c * 128:(c + 1) * 128], hT[:, k, sl], k == 0, k == KC - 1,
                           [f"hT{tt_}", rq_], [f"ps{pb}"])
                    if (c * 4 + tt_) % 2 == 0:
                        act(qT[:, c, sl], PSB(pb), AF.Copy, [f"ps{pb}"], ["qT"], scale=0.125)
                    else:
                        ts(qT[:, c, sl], PSB(pb), 0.125, None, ALU.mult, None, [f"ps{pb}"], ["qT"])
                    pb = (pb + 1) % 4
            S.barrier()

            if stop_after == f"qkv{l}":
                break
            KTr = carve(49152, [128, 2 * NB, 512], BF16)
            Vr = carve(81920, [128, 2 * NB, 512], BF16)
            yaT = carve(0, [128, 4, T], BF16)
            Ebuf = [carve(16384, [128, 1024], F32), carve(118784, [128, 1024], F32)]
            Lsp = [carve(20480 + i * 2048, [128, 1024], BF16) for i in range(3)]
            Abuf = [carve(26624 + i * 2048, [128, 1024], BF16) for i in range(2)]
            Rb = [carve(114688 + i * 2048, [128, 1024], BF16) for i in range(2)]
            yatmp = carve(30720, [128, 4, 128], BF16)
            kt_v = kt_all.ap().rearrange("(b p) x -> p b x", p=128)
            v_v = v_all.ap().rearrange("(b p) x -> p b x", p=128)
            for i in range(4):
                dma("sp", KTr[:, i * 8:(i + 1) * 8, :], kt_v[:, i * 8:(i + 1) * 8, :], ["kt_all"], ["KT"])
            for i in range(4):
                dma("sp", Vr[:, i * 8:(i + 1) * 8, :], v_v[:, i * 8:(i + 1) * 8, :], ["v_all"], ["V"])

            steps = []
            for m in range(NB):
                G = 4 * (m // 2) + (1 if m % 2 == 0 else 3)
                for si, g in enumerate(range(G, -1, -1)):
                    r, ml = gblock(g)
                    steps.append(dict(m=m, si=si, g=g, kb=r * NB + ml, last=(g == 0), par=m % 2))
            NS = len(steps)
            Zb = lambda s: psum[:, (s % 2) * 2:(s % 2) * 2 + 2, :].rearrange("p a b -> p (a b)")
            Zres = lambda s: f"Z{s % 2}"
            Cb = psum[:, 4:6, :].rearrange("p a b -> p (a b)")
            Ob = psum[0:64, 6:8, :].rearrange("p a b -> p (a b)")

            def qk(out1024, st, start_flag, wres):
                m, kb = st["m"], st["kb"]
                for c in range(4):
                    for hh in range(2):
                        hp = hh * 4 + c
                        mm(out1024[:, hp * 128:(hp + 1) * 128],
                           KTr[hh * 64:(hh + 1) * 64, kb, c * 128:(c + 1) * 128],
                           qT[hh * 64:(hh + 1) * 64, c, m * 128:(m + 1) * 128],
                           start_flag, (True if start_flag else c == 3), ["KT", "qT"], [wres])

            def front(s):
                st = steps[s]
                qk(Zb(s), st, True, Zres(s))

            def mid0(s):
                act(Ebuf[s % 2], Zb(s), AF.Exp, [Zres(s)], [f"E{s % 2}"])

            def mid(s):
                st = steps[s]
                si = st["si"]
                lb = Lsp[s % 3]
                act(lb, Ebuf[s % 2], AF.Ln, [f"E{s % 2}"], [f"L{s % 3}"], bias=1.0)
                if si < 2:
                    mk = amask_b[st["par"]][1 - si]
                    tt(lb.rearrange("p (h t) -> p h t", t=128), lb.rearrange("p (h t) -> p h t", t=128),
                       mk.unsqueeze(1).broadcast_to([128, 8, 128]), ALU.mult, [f"L{s % 3}", "cb16"], [f"L{s % 3}"])
                if si == 0:
                    st["carry"] = None
                elif si == 1:
                    st["carry"] = (Lsp[(s - 1) % 3], f"L{(s - 1) % 3}")
                else:
                    prev = steps[s - 1]["carry"]
                    rb = Rb[s % 2]
                    tt(rb, prev[0], Lsp[(s - 1) % 3], ALU.add, [prev[1], f"L{(s - 1) % 3}"], [f"R{s % 2}"])
                    st["carry"] = (rb, f"R{s % 2}")

            def back1(s):
                st = steps[s]
                lb = Lsp[s % 3]
                for hb in range(2):
                    cs = slice(hb * 512, (hb + 1) * 512)
                    mm(Cb[:, cs], negtri_b, lb[:, cs], True, False, [f"L{s % 3}", "cb16"], ["C"])
                    if st["carry"] is not None:
                        mm(Cb[:, cs], negones_b, st["carry"][0][:, cs], False, False, [st["carry"][1], "cb16"], ["C"])
                qk(Cb, st, False, "C")

            def back2(s):
                st = steps[s]
                ab = Abuf[s % 2]
                act(ab, Cb, AF.Exp, ["C"], [f"A{s % 2}"])
                if st["si"] < 2:
                    mk = amask_b[st["par"]][1 - st["si"]]
                    tt(ab.rearrange("p (h t) -> p h t", t=128), ab.rearrange("p (h t) -> p h t", t=128),
                       mk.unsqueeze(1).broadcast_to([128, 8, 128]), ALU.mult, [f"A{s % 2}", "cb16"], [f"A{s % 2}"])

            def back3(s):
                st = steps[s]
                ab = Abuf[s % 2]
                m, kb = st["m"], st["kb"]
                for hp in range(8):
                    hh, c = divmod(hp, 4)
                    h = 2 * c + hh
                    mm(Ob[:, hp * 128:(hp + 1) * 128], Vr[:, kb, h * 64:(h + 1) * 64], ab[:, hp * 128:(hp + 1) * 128],
                       st["si"] == 0 and c == 0, st["last"] and c == 3, [f"A{s % 2}", "V"], ["O"])
                if st["last"]:
                    ov = Ob.rearrange("p (hh c t) -> p hh c t", hh=2, c=4)
                    S.add("dve", lambda e: e.tensor_copy(yaT[0:64, :, m * 128:(m + 1) * 128], ov[:, 0, :, :]),
                          ["O"], ["yaT"])
                    S.add("dve", lambda e: e.tensor_copy(yatmp[0:64], ov[:, 1, :, :]), ["O"], ["yatmp"])
                    dma("sp", yaT[64:128, :, m * 128:(m + 1) * 128], yatmp[0:64], ["yatmp"], ["yaT"])

            ok = lambda i: 0 <= i < NS
            for s in range(-2, NS + 2):
                if ok(s - 1):
                    back1(s - 1)
                if ok(s - 2):
                    back3(s - 2)
                if ok(s + 2):
                    front(s + 2)
                if ok(s + 1):
                    mid0(s + 1)
                if ok(s):
                    mid(s)
                if ok(s - 1):
                    back2(s - 1)
            S.barrier()

            if stop_after == f"attn{l}":
                break

            dma("sp", bcp[:], bc_d[:, l * 1536:(l + 1) * 1536], [], ["bcp"])
            for hf in range(2):
                t0 = hf * 1024
                hTh = carve(16384, [128, KC, 1024], BF16)
                ybT = carve(32768, [128, 4, 1024], BF16)
                ycT = carve(40960, [128, 4, 1024], BF16)
                wbuf = [carve(49152 + i * 8192, [128, KC, 512], BF16) for i in range(2)]
                sq = carve(65536, [128, KC, 512], BF16)
                rstd = carve(73728, [128, 512], F32)
                lnv = carve(75776, [128, 512], F32)
                mrgT = carve(77824, [128, KC, 1024], BF16)
                yext = carve(94208, [128, 4, 8, 130], BF16)
                cbT = carve(102528, [128, 4, 1024], BF16)
                gv4 = carve(110720, [128, 4, 512], F32)
                vn4 = carve(118912, [128, 4, 512], BF16)
                tls = carve(123008, [128, 2, 4, NB, 2], BF16)
                gv, tmpf, ctmp = gv4[:, 0, :], gv4[:, 1, :], gv4[:, 2, :]
                sg = [carve(118912, [128, 512], F32)]
                tfb = [(rstd, "rstd"), (lnv, "lnv")]
                wcnt = [0]
                for tt_ in range(2):
                    sl = slice(tt_ * 512, (tt_ + 1) * 512)
                    gsl = slice(t0 + tt_ * 512, t0 + (tt_ + 1) * 512)
                    rms_norm_tile(xres[:, :, gsl], 512, PV_MIXG + l * 8, hTh[:, :, sl], f"hTh{tt_}", sq, rstd, lnv, 7, XR)
                HTH = ["hTh0", "hTh1"]
                pb = 0
                wz, rz = next_w(1536)
                for c in range(4):
                    for tt_ in range(2):
                        sl = slice(tt_ * 512, (tt_ + 1) * 512)
                        for k in range(KC):
                            mm(PSB(pb), wz[:, k, c * 128:(c + 1) * 128], hTh[:, k, sl], k == 0, k == KC - 1,
                               [f"hTh{tt_}", rz], [f"ps{pb}"])
                        act(ybT[:, c, sl], PSB(pb), AF.Gelu, [f"ps{pb}"], ["ybT"])
                        pb = (pb + 1) % 4
                if hf == 0:
                    dma("sp", tls.rearrange("p r c m t -> p r (c m t)"),
                        tl_all.ap().rearrange("(r p) x -> p r x", p=128), ["tl_all"], ["tls"])
                wcb, rcb = next_w(2560)
                for c in range(4):
                    for tt_ in range(2):
                        sl = slice(tt_ * 512, (tt_ + 1) * 512)
                        for k in range(KC):
                            mm(PSB(pb), wcb[:, k, c * 128:(c + 1) * 128], hTh[:, k, sl], k == 0, k == KC - 1,
                               [f"hTh{tt_}", rcb], [f"ps{pb}"])
                        cp(cbT[:, c, sl], PSB(pb), [f"ps{pb}"], ["cbT"])
                        pb = (pb + 1) % 4
                wcc, rcc = next_w(3072)
                wcx, rcx = next_w(3584)
                for c in range(4):
                    for tt_ in range(2):
                        sl = slice(tt_ * 512, (tt_ + 1) * 512)
                        p1 = pb
                        p2 = (pb + 1) % 4
                        pb = (pb + 2) % 4
                        for k in range(KC):
                            mm(PSB(p1), wcc[:, k, c * 128:(c + 1) * 128], hTh[:, k, sl], k == 0, k == KC - 1,
                               [f"hTh{tt_}", rcc], [f"ps{p1}"])
                        for k in range(KC):
                            mm(PSB(p2), wcx[:, k, c * 128:(c + 1) * 128], hTh[:, k, sl], k == 0, k == KC - 1,
                               [f"hTh{tt_}", rcx], [f"ps{p2}"])
                        cp(ctmp, PSB(p1), [f"ps{p1}"], ["gv4_2"], eng="act")
                        tt(yext[:, c, tt_ * 4:(tt_ + 1) * 4, 2:130], ctmp.rearrange("p (m t) -> p m t", t=128),
                           PSB(p2).rearrange("p (m t) -> p m t", t=128), ALU.mult, ["gv4_2", f"ps{p2}"], ["yext"])
                wz2, rz2 = next_w(2048)
                gw = [carve(49152 + i * 8192, [128, KC, 3, 128], BF16) for i in range(2)]
                bw = [carve(65536 + i * 4096, [128, 3, 4, 128], BF16) for i in range(2)]

                def load_gate(dc_):
                    dq_ = (dc_ + 1) % 2
                    gsrc = bass_gate_ap(w_in_d, l, dc_)
                    bsrc = w_br_d[l].rearrange("n (k p) d -> p n k d", p=128)
                    for n_ in range(3):
                        dma("pool", gw[dq_][:, :, n_, :], gsrc[:, :, n_, :], [], [f"wbuf{dq_}"])
                        dma("pool", bw[dq_][:, n_, :, :], bsrc[:, n_, :, dc_ * 128:(dc_ + 1) * 128], [],
                            [("sqA" if dq_ == 0 else "sqB")])

                load_gate(0)
                smc = lambda o, n=4: small[:, o:o + n]
                for grp in range(2):
                    for b_ in range(4):
                        mb = grp * 4 + b_
                        bs = slice(mb * 128, (mb + 1) * 128)
                        for k in range(KC):
                            mm(PSB(b_), hTh[:, k, bs], wz2[:, k, :], k == 0, k == KC - 1, [f"hTh{grp}", rz2], [f"ps{b_}"])
                        act(gv4[:, b_, :], PSB(b_), AF.Gelu, [f"ps{b_}"], [f"gv4_{b_}", "sm_sum"], accum=small[:, b_:b_ + 1])
                    for b_ in range(4):
                        act(vn4[:, b_, :], gv4[:, b_, :], AF.Square, [f"gv4_{b_}"], ["vn4", "sm_sq"], accum=small[:, 4 + b_:5 + b_])
                    ts(smc(8), smc(0), -1.0 / 512, None, ALU.mult, None, ["sm_sum"], ["sm_nm"])
                    tt(smc(12), smc(8), smc(8), ALU.mult, ["sm_nm"], ["sm_m2"])
                    ts(smc(16), smc(4), 1.0 / 512, 1e-5, ALU.mult, ALU.add, ["sm_sq"], ["sm_ve"])
                    tt(smc(20), smc(16), smc(12), ALU.subtract, ["sm_ve", "sm_m2"], ["sm_var"])
                    S.add("pool", lambda e: e.tensor_tensor(smc(24), smc(20), small[:, 62:63].broadcast_to([128, 4]), ALU.pow),
                          ["sm_var", "smc"], ["sm_rstd"])
                    for b_ in range(4):
                        ts(gv4[:, b_, :], gv4[:, b_, :], small[:, 8 + b_:9 + b_], small[:, 24 + b_:25 + b_], ALU.add, ALU.mult,
                           [f"gv4_{b_}", "sm_nm", "sm_rstd"], [f"gv4_{b_}"])
                    G4 = [f"gv4_{i}" for i in range(4)]
                    tt(gv4, gv4, bcp[:, BC_LNG: BC_LNG + 512].unsqueeze(1).broadcast_to([128, 4, 512]), ALU.mult,
                       G4 + ["bcp"], G4)
                    tt(vn4, gv4, bcp[:, BC_LNB: BC_LNB + 512].unsqueeze(1).broadcast_to([128, 4, 512]), ALU.add,
                       G4 + ["bcp"], ["vn4"])
                    for b_ in range(4):
                        for gi in range(4):
                            mm(PSB(4 + b_)[:, gi * 128:(gi + 1) * 128], vn4[:, b_, gi * 128:(gi + 1) * 128],
                               wmT[:, (l * 4 + gi) * 128:(l * 4 + gi + 1) * 128], True, True, ["vn4", "wmT"], [f"ps{4 + b_}"])
                    for b_ in range(4):
                        mb = grp * 4 + b_
                        bs = slice(mb * 128, (mb + 1) * 128)
                        tf_, T_ = tfb[b_ % 2]
                        tt(tf_, PSB(4 + b_), bcp[:, BC_BSP: BC_BSP + 512], ALU.add, [f"ps{4 + b_}", "bcp"], [T_])
                        tt(ybT[:, :, bs], tf_.rearrange("p (g t) -> p g t", t=128), ybT[:, :, bs], ALU.mult,
                           [T_, "ybT"], ["ybT"])
                selA = pvec[:, PV_SEL:PV_SEL + 1]
                selB = pvec[:, PV_SEL + 1:PV_SEL + 2]
                halo_t = small[:, 28:60].bitcast(BF16).rearrange("p (c j t) -> p c j t", c=4, t=2)
                jb = hf * 4
                for par in range(2):
                    if par == 0:
                        candB = tls[:, 0, :, 2 * jb:2 * jb + 8:2, :]
                        if hf == 0:
                            jlo = 1
                            candA = tls[:, 0, :, 1:6:2, :]
                        else:
                            jlo = 0
                            candA = tls[:, 0, :, 2 * jb - 1:2 * jb + 6:2, :]
                    else:
                        jlo = 0
                        candA = tls[:, 1, :, 2 * jb + 1:2 * jb + 8:2, :]
                        candB = tls[:, 1, :, 2 * jb:2 * jb + 8:2, :]
                    for c in range(4):
                        dst = yext[:, c, par:8:2, 0:2]
                        ht = halo_t[:, c, 0:4, :]
                        ts(ht, candB[:, c], selB, None, ALU.mult, None, ["tls", "pvec"], ["halo_t"])
                        if jlo == 1:
                            cp(dst[:, 0:1, :], ht[:, 0:1, :], ["halo_t"], ["yext"], eng="dve")
                            stt(dst[:, 1:4, :], candA[:, c], selA, ht[:, 1:4, :], ALU.mult, ALU.add,
                                ["tls", "pvec", "halo_t"], ["yext"])
                        else:
                            stt(dst, candA[:, c], selA, ht, ALU.mult, ALU.add, ["tls", "pvec", "halo_t"], ["yext"])
                for c in range(4):
                    for tt_ in range(2):
                        sl = slice(tt_ * 512, (tt_ + 1) * 512)
                        ye = yext[:, c, tt_ * 4:(tt_ + 1) * 4, :]
                        acc = ctmp.rearrange("p (m t) -> p m t", t=128)
                        wcol = lambda k: pvec[:, PV_CONV + (l * 3 + k) * 4 + c: PV_CONV + (l * 3 + k) * 4 + c + 1]
                        ts(acc, ye[:, :, 0:128], wcol(0), None, ALU.mult, None, ["yext", "pvec"], ["gv4_2"])
                        stt(acc, ye[:, :, 1:129], wcol(1), acc, ALU.mult, ALU.add, ["yext", "pvec", "gv4_2"], ["gv4_2"])
                        stt(acc, ye[:, :, 2:130], wcol(2), acc, ALU.mult, ALU.add, ["yext", "pvec", "gv4_2"], ["gv4_2"])
                        tt(ycT[:, c, sl], ctmp, cbT[:, c, sl], ALU.mult, ["gv4_2", "cbT"], ["ycT"])
                gw = [carve(49152 + i * 8192, [128, KC, 3, 128], BF16) for i in range(2)]
                bw = [carve(65536 + i * 4096, [128, 3, 4, 128], BF16) for i in range(2)]
                brs = [(yaT, "yaT", t0), (ybT, "ybT", 0), (ycT, "ycT", 0)]
                gcnt = 0
                for dc in range(KC):
                    dq = (dc + 1) % 2
                    g_ = gw[dq]
                    b_ = bw[dq]
                    if dc > 0:
                        load_gate(dc)
                    for tt_ in range(2):
                        sl = slice(tt_ * 512, (tt_ + 1) * 512)
                        for n in range(3):
                            pg = (gcnt % 4) * 2
                            pbq = pg + 1
                            gcnt += 1
                            for k in range(KC):
                                mm(PSB(pg), g_[:, k, n, :], hTh[:, k, sl], k == 0, k == KC - 1,
                                   [f"hTh{tt_}", f"wbuf{dq}"], [f"ps{pg}"])
                            src, sres, o0 = brs[n]
                            for k in range(4):
                                mm(PSB(pbq), b_[:, n, k, :], src[:, k, o0 + tt_ * 512: o0 + (tt_ + 1) * 512],
                                   k == 0, k == 3, [sres, ("sqA" if dq == 0 else "sqB")], [f"ps{pbq}"])
                            act(sg[0], PSB(pg), AF.Sigmoid, [f"ps{pg}"], ["vn4"])
                            if n == 0:
                                tt(gv, sg[0], PSB(pbq), ALU.mult, ["vn4", f"ps{pbq}"], ["gv4_0"])
                            elif n == 1:
                                tt(tmpf, sg[0], PSB(pbq), ALU.mult, ["vn4", f"ps{pbq}"], ["gv4_1"])
                                tt(gv, gv, tmpf, ALU.add, ["gv4_0", "gv4_1"], ["gv4_0"])
                            else:
                                tt(tmpf, sg[0], PSB(pbq), ALU.mult, ["vn4", f"ps{pbq}"], ["gv4_1"])
                                tt(mrgT[:, dc, sl], gv, tmpf, ALU.add, ["gv4_0", "gv4_1"], ["mrgT"])
                w_out_v = w_out_d[l].rearrange("(k p) c -> p k c", p=128)
                pb = 0
                for cg in range(2):
                    wo_, ro_ = next_w(cg * 512, src=w_out_v)
                    for oc4 in range(4):
                        oc = cg * 4 + oc4
                        for tt_ in range(2):
                            sl = slice(tt_ * 512, (tt_ + 1) * 512)
                            gsl = slice(t0 + tt_ * 512, t0 + (tt_ + 1) * 512)
                            for k in range(KC):
                                mm(PSB(pb), wo_[:, k, oc4 * 128:(oc4 + 1) * 128], mrgT[:, k, sl], k == 0, k == KC - 1,
                                   ["mrgT", ro_], [f"ps{pb}"])
                            tt(xres[:, oc, gsl], xres[:, oc, gsl], PSB(pb), ALU.add, [f"x{oc}", f"ps{pb}"], [f"x{oc}"])
                            pb = (pb + 1) % 4

            S.barrier()
            if stop_after == f"mix{l}":
                break

            hT = carve(0, [128, KC, T], BF16)
            qx = carve(32768, [128, KC, T], BF16)
            wbuf = [carve(65536 + i * 8192, [128, KC, 512], BF16) for i in range(2)]
            sq = carve(81920, [128, KC, 512], BF16)
            rstd = carve(90112, [128, 512], F32)
            lnv = carve(92160, [128, 512], F32)
            memf = carve(94208, [128, KC, 256], F32)
            mTn = carve(102400, [128, KC, 256], BF16)
            kTx = carve(106496, [128, KC, 256], BF16)
            vx = carve(110592, [128, 2, D], BF16)
            pT = [carve(114688, [128, 2, 512], BF16), carve(118784, [128, 2, 512], BF16)]
            recb = [carve(116736, [128, 512], F32), carve(120832, [128, 512], F32)]
            wcnt = [0]
            wk_v = wk_d[l].rearrange("(k p) c -> p k c", p=128)
            wv_v = wv_d[l].rearrange("(k p) c -> p k c", p=128)
            wq_v = wq_d[l].rearrange("(k p) c -> p k c", p=128)
            wo_v = wo_d[l].rearrange("(k p) c -> p k c", p=128)
            dma("sp", memf, memT_d.rearrange("(c p) t -> p c t", p=128), [], ["memf"])
            rms_norm_tile(memf, 256, PV_MEMG + l * 8, mTn, "mTn", sq[:, :, 0:256], rstd[:, 0:256], lnv[:, 0:256], 6, ["memf"])
            pbx = [0]

            def xnorm(tt_):
                sl = slice(tt_ * 512, (tt_ + 1) * 512)
                rms_norm_tile(xres[:, :, sl], 512, PV_XAG + l * 8, hT[:, :, sl], f"hT{tt_}", sq, rstd, lnv, 7, XR)

            def memk(cg):
                w_, r_ = next_w(cg * 512, src=wk_v)
                for dc4 in range(4):
                    dc = cg * 4 + dc4
                    pb = pbx[0]
                    for k in range(KC):
                        mm(PSB(pb)[:, 0:256], w_[:, k, dc4 * 128:(dc4 + 1) * 128], mTn[:, k, :], k == 0, k == KC - 1,
                           ["mTn", r_], [f"ps{pb}"])
                    cp(kTx[:, dc, :], PSB(pb)[:, 0:256], [f"ps{pb}"], ["kTx"])
                    pbx[0] = (pb + 1) % 4

            def memv(cg):
                w_, r_ = next_w(cg * 512, src=wv_v)
                for kc in range(2):
                    pb = pbx[0]
                    for k in range(KC):
                        mm(PSB(pb), mTn[:, k, kc * 128:(kc + 1) * 128], w_[:, k, :], k == 0, k == KC - 1,
                           ["mTn", r_], [f"ps{pb}"])
                    cp(vx[:, kc, cg * 512:(cg + 1) * 512], PSB(pb), [f"ps{pb}"], ["vx"])
                    pbx[0] = (pb + 1) % 4

            xnorm(0)
            memk(0)
            xnorm(1)
            memk(1)
            xnorm(2)
            memv(0)
            xnorm(3)
            memv(1)
            pb = pbx[0]
            for cg in range(2):
                w_, r_ = next_w(cg * 512, src=wq_v)
                for tt_ in range(4):
                    sl = slice(tt_ * 512, (tt_ + 1) * 512)
                    for dc4 in range(4):
                        dc = cg * 4 + dc4
                        for k in range(KC):
                            mm(PSB(pb), w_[:, k, dc4 * 128:(dc4 + 1) * 128], hT[:, k, sl], k == 0, k == KC - 1,
                               [f"hT{tt_}", r_], [f"ps{pb}"])
                        if (dc * 4 + tt_) % 2 == 0:
                            act(qx[:, dc, sl], PSB(pb), AF.Copy, [f"ps{pb}"], ["qx"], scale=0.0625)
                        else:
                            ts(qx[:, dc, sl], PSB(pb), 0.0625, None, ALU.mult, None, [f"ps{pb}"], ["qx"])
                        pb = (pb + 1) % 4
            oT = hT
            it = 0
            for tt_ in range(4):
                sl = slice(tt_ * 512, (tt_ + 1) * 512)
                for hx in range(4):
                    q_ = it % 2
                    it += 1
                    pT_, rec_ = pT[q_], recb[q_]
                    P_, R_ = f"pT{q_}", f"rec{q_}"
                    bd = 2 + q_
                    bo = 4 + 2 * q_
                    for kc in range(2):
                        for dd in range(2):
                            mm(PSB(kc), kTx[:, hx * 2 + dd, kc * 128:(kc + 1) * 128], qx[:, hx * 2 + dd, sl],
                               dd == 0, dd == 1, ["kTx", "qx"], [f"ps{kc}"])
                    act(pT_, psum[:, 0:2, :], AF.Exp, ["ps0", "ps1"], [P_])
                    for kc in range(2):
                        mm(PSB(bd), ones_b, pT_[:, kc, :], kc == 0, kc == 1, [P_, "cb16"], [f"ps{bd}"])
                    for dd in range(2):
                        for kc in range(2):
                            mm(PSB(bo + dd), vx[:, kc, hx * 256 + dd * 128: hx * 256 + (dd + 1) * 128], pT_[:, kc, :],
                               kc == 0, kc == 1, [P_, "vx"], [f"ps{bo + dd}"])
                    act(rec_, PSB(bd), AF.Ln, [f"ps{bd}"], [R_])
                    act(rec_, rec_, AF.Exp, [R_], [R_], scale=-1.0)
                    for dd in range(2):
                        tt(oT[:, hx * 2 + dd, sl], PSB(bo + dd), rec_, ALU.mult, [f"ps{bo + dd}", R_], [f"hT{tt_}"])
            pb = 0
            for cg in range(2):
                w_, r_ = next_w(cg * 512, src=wo_v)
                for oc4 in range(4):
                    oc = cg * 4 + oc4
                    for tt_ in range(4):
                        sl = slice(tt_ * 512, (tt_ + 1) * 512)
                        for k in range(KC):
                            mm(PSB(pb), w_[:, k, oc4 * 128:(oc4 + 1) * 128], oT[:, k, sl], k == 0, k == KC - 1,
                               [f"hT{tt_}", r_], [f"ps{pb}"])
                        tt(xres[:, oc, sl], xres[:, oc, sl], PSB(pb), ALU.add, [f"x{oc}", f"ps{pb}"], [f"x{oc}"])
                        pb = (pb + 1) % 4
            S.barrier()
            if stop_after == f"xa{l}":
                break

            wg_v = wg_d[l].rearrange("(k p) c -> p k c", p=128)
            wu_v = wu_d[l].rearrange("(k p) c -> p k c", p=128)
            wd_v = wd_d[l].rearrange("(k p) c -> p k c", p=128)
            for hf in range(2):
                t0 = hf * 1024
                hTh = carve(0, [128, KC, 1024], BF16)
                h1T = carve(16384, [128, NJ, 1024], BF16)
                wgb = [carve(61440 + i * 8192, [128, KC, 512], BF16) for i in range(2)]
                wub = [carve(77824 + i * 8192, [128, KC, 512], BF16) for i in range(2)]
                sq = carve(94208, [128, KC, 512], BF16)
                rstd = carve(102400, [128, 512], F32)
                lnv = carve(104448, [128, 512], F32)
                sgf = carve(106496, [128, 512], F32)
                tf = carve(108544, [128, 512], F32)
                for tt_ in range(2):
                    sl = slice(tt_ * 512, (tt_ + 1) * 512)
                    gsl = slice(t0 + tt_ * 512, t0 + (tt_ + 1) * 512)
                    rms_norm_tile(xres[:, :, gsl], 512, PV_FFNG + l * 8, hTh[:, :, sl], f"hTh{tt_}", sq, rstd, lnv, 7, XR)
                ngrp = 6
                pcount = 0
                for gi in range(ngrp):
                    c0 = gi * 512
                    ncol = min(512, FH - c0)
                    i = gi % 2
                    dma("pool", wgb[i][:, :, 0:ncol], wg_v[:, :, c0:c0 + ncol], [], [f"wgb{i}"])
                    dma("pool", wub[i][:, :, 0:ncol], wu_v[:, :, c0:c0 + ncol], [], [f"wub{i}"])
                    for jj in range(ncol // 128):
                        j = gi * 4 + jj
                        for tt_ in range(2):
                            sl = slice(tt_ * 512, (tt_ + 1) * 512)
                            pg = (pcount % 4) * 2
                            pu = pg + 1
                            pcount += 1
                            for k in range(KC):
                                mm(PSB(pg), wgb[i][:, k, jj * 128:(jj + 1) * 128], hTh[:, k, sl], k == 0, k == KC - 1,
                                   [f"hTh{tt_}", f"wgb{i}"], [f"ps{pg}"])
                            for k in range(KC):
                                mm(PSB(pu), wub[i][:, k, jj * 128:(jj + 1) * 128], hTh[:, k, sl], k == 0, k == KC - 1,
                                   [f"hTh{tt_}", f"wub{i}"], [f"ps{pu}"])
                            act(sgf, PSB(pg), AF.Sigmoid, [f"ps{pg}"], ["sgf"])
                            tt(tf, sgf, PSB(pg), ALU.mult, ["sgf", f"ps{pg}"], ["tf"])
                            tt(h1T[:, j, sl], tf, PSB(pu), ALU.mult, ["tf", f"ps{pu}"], ["h1T"])
                wdb = [carve(110592, [128, NJ, 256], BF16), carve(61440, [128, NJ, 256], BF16)]
                wdn = [["wdb0"], ["wgb0", "wgb1"]]
                pb = 0
                for cg in range(4):
                    i = cg % 2
                    dma("pool", wdb[i], wd_v[:, :, cg * 256:(cg + 1) * 256], [], wdn[i])
                    for oc2 in range(2):
                        oc = cg * 2 + oc2
                        for tt_ in range(2):
                            sl = slice(tt_ * 512, (tt_ + 1) * 512)
                            gsl = slice(t0 + tt_ * 512, t0 + (tt_ + 1) * 512)
                            for j in range(NJ):
                                mm(PSB(pb), wdb[i][:, j, oc2 * 128:(oc2 + 1) * 128], h1T[:, j, sl], j == 0, j == NJ - 1,
                                   ["h1T"] + wdn[i], [f"ps{pb}"])
                            tt(xres[:, oc, gsl], xres[:, oc, gsl], PSB(pb), ALU.add, [f"x{oc}", f"ps{pb}"], [f"x{oc}"])
                            pb = (pb + 1) % 4
            S.barrier()
            if stop_after == f"ffn{l}":
                break

        yT_v = yT_d.rearrange("(c p) t -> p c t", p=128)
        if stop_after is None and final:
            sq = carve(0, [128, KC, 512], BF16)
            rstd = carve(8192, [128, 512], F32)
            lnv = carve(10240, [128, 512], F32)
            outb = [carve(16384 + i * 16384, [128, KC, 512], F32) for i in range(2)]
            for tt_ in range(4):
                sl = slice(tt_ * 512, (tt_ + 1) * 512)
                ob = outb[tt_ % 2]
                rms_norm_tile(xres[:, :, sl], 512, PV_FING, ob, f"outb{tt_ % 2}", sq, rstd, lnv, 7, XR)
                dma("sp", yT_v[:, :, sl], ob, [f"outb{tt_ % 2}"], ["yT"])
        else:
            for c in range(KC):
                dma("sp", yT_v[:, c, :], xres[:, c, :], [f"x{c}"], ["yT"])
        fw = S.final_wait("sp")

        @block.tensor
        def _(e):
            S.emit("pe", e)

        @block.scalar
        def _(e):
            S.emit("act", e)

        @block.vector
        def _(e):
            S.emit("dve", e)

        @block.gpsimd
        def _(e):
            S.emit("pool", e)

        @block.sync
        def _(e):
            S.emit("sp", e)
            for s, v in fw:
                e.wait_ge(sems[s], v)

    return nc


def bass_gate_ap(w_in_d, l, dc):
    v = w_in_d[l].rearrange("(k p) c -> p k c", p=128)[:, :, 4096:7168]
    return v.rearrange("p k (n c) -> p k n c", n=3)[:, :, :, dc * 128:(dc + 1) * 128]


_CACHE = {}


def _owned_blocks(role):
    out = []
    for j in range(8):
        out += ([4 * j, 4 * j + 3] if role == 0 else [4 * j + 1, 4 * j + 2])
    return out


def _prep_inputs(inp):
    f = np.float32
    x = np.asarray(inp["x"], f)
    mem = np.asarray(inp["mem"], f)
    gv = lambda k: np.asarray(inp[k], f)

    def pcols(a):
        a = a.reshape(-1, 8, 128)
        return np.ascontiguousarray(a.transpose(2, 0, 1).reshape(128, -1))

    pvec_base = np.zeros((128, NPV), f)
    pvec_base[:, PV_MIXG:PV_MIXG + 16] = pcols(gv("norm_mix_g"))
    pvec_base[:, PV_XAG:PV_XAG + 16] = pcols(gv("norm_xa_g"))
    pvec_base[:, PV_MEMG:PV_MEMG + 16] = pcols(gv("mem_norm_g"))
    pvec_base[:, PV_FFNG:PV_FFNG + 16] = pcols(gv("norm_ffn_g"))
    pvec_base[:, PV_FING:PV_FING + 8] = pcols(gv("final_g")[None])
    cw = gv("conv_w").reshape(L * 3, 4, 128)
    pvec_base[:, PV_CONV:PV_CONV + 24] = cw.transpose(2, 0, 1).reshape(128, 24)

    bc = np.zeros((128, NBC), f)
    for l_ in range(L):
        o = l_ * 1536
        bc[:, o:o + 512] = np.broadcast_to(gv("sgu_ln_g")[l_].reshape(1, -1), (128, 512))
        bc[:, o + 512:o + 1024] = np.broadcast_to(gv("sgu_ln_b")[l_].reshape(1, -1), (128, 512))
        bc[:, o + 1024:o + 1536] = np.broadcast_to(gv("b_spatial")[l_].reshape(1, -1), (128, 512))
    wspT = np.ascontiguousarray(gv("w_spatial").transpose(3, 0, 1, 2).reshape(128, L * 4 * 128))

    j = np.arange(128)
    negtri = -(j[:, None] >= j[None, :]).astype(f)
    sgum = ((j[None, :] // 64) >= (j[:, None] // 64)).astype(f)
    diag = (j[:, None] < j[None, :]).astype(f)
    full = np.ones((128, 128), f)
    none = np.zeros((128, 128), f)
    amasks = {0: [[diag, none], [full, diag]], 1: [[full, diag], [diag, none]]}

    shared = {k: np.ascontiguousarray(gv(k)) for k in
              ["w_in", "w_branch", "w_out", "w_q_xa", "w_k_xa", "w_v_xa", "w_o_xa",
               "w_gate_ffn", "w_up_ffn", "w_down_ffn"]}
    in_maps = []
    for core in range(8):
        b, role = divmod(core, 2)
        blocks = _owned_blocks(role)
        xb = x[b].reshape(32, 128, D)[blocks].reshape(T, D)
        cst = np.zeros((128, NCST), f)
        cst[:, CS_NEGTRI:CS_NEGTRI + 128] = negtri
        cst[:, CS_SGUM:CS_SGUM + 128] = sgum
        for par in range(2):
            for w in range(2):
                o = CS_AMASK + (par * 2 + w) * 128
                cst[:, o:o + 128] = amasks[role][par][w]
        pv = pvec_base.copy()
        pv[:, PV_SEL] = 1.0 if role == 0 else 0.0
        pv[:, PV_SEL + 1] = 0.0 if role == 0 else 1.0
        m = dict(shared)
        m.update({"xT": np.ascontiguousarray(xb.T), "memT": np.ascontiguousarray(mem[b].T),
                  "pvec": pv, "cst": cst, "bc": bc, "wspT": wspT})
        in_maps.append(m)
    return in_maps


def _assemble(results):
    out = np.zeros((4, 4096, D), np.float32)
    for core in range(8):
        b, role = divmod(core, 2)
        blocks = _owned_blocks(role)
        y = np.asarray(results[core]["yT"]).T.reshape(NB, 128, D)
        out[b].reshape(32, 128, D)[blocks] = y
    return out


FUSED = True


def kernel(**inputs):
    in_maps = _prep_inputs(inputs)
    if FUSED:
        if "nc" not in _CACHE:
            _CACHE["nc"] = build_program()
        res = run_bass_kernel_spmd(_CACHE["nc"], in_maps, core_ids=list(range(8)))
        return _assemble(res.results)
    if "nc0" not in _CACHE:
        _CACHE["nc0"] = build_program(layers=(0,), final=False)
        _CACHE["nc1"] = build_program(layers=(1,), final=True)
    res0 = run_bass_kernel_spmd(_CACHE["nc0"], in_maps, core_ids=list(range(8)))
    for c in range(8):
        in_maps[c]["xT"] = np.ascontiguousarray(np.asarray(res0.results[c]["yT"], np.float32))
    res1 = run_bass_kernel_spmd(_CACHE["nc1"], in_maps, core_ids=list(range(8)))
    return _assemble(res1.results)
```

```python
import numpy as np
import concourse.bass as bass
import concourse.mybir as mybir
from concourse.bass_utils import run_bass_kernel_spmd

F32 = mybir.dt.float32
BF16 = mybir.dt.bfloat16
AF = mybir.ActivationFunctionType
ALU = mybir.AluOpType

L = 2
D = 1024
T = 2048
NB = 16
KC = 8
FH = 2816
NJ = 22
INC = 7168
PAIRS = [[0, 1], [2, 3], [4, 5], [6, 7]]

PV_MIXG = 0
PV_XAG = 16
PV_MEMG = 32
PV_FFNG = 48
PV_FING = 64
PV_CONV = 72
PV_SEL = 96
NPV = 98
CS_NEGTRI = 0
CS_SGUM = 128
CS_AMASK = 256
NCST = 768
BC_LNG = 0
BC_LNB = 512
BC_BSP = 1024
NBC = 3072


def gblock(g):
    j, i = divmod(g, 4)
    return [(0, 2 * j), (1, 2 * j), (1, 2 * j + 1), (0, 2 * j + 1)][i]


class Sched:
    def __init__(self, nc, sems):
        self.nc = nc
        self.sems = sems
        self.engs = ["pe", "act", "dve", "pool", "sp"]
        self.ops = {e: [] for e in self.engs}
        self.count = {e: 0 for e in self.engs}
        self.waited = {e: {} for e in self.engs}
        self.res = {}
        self.dq = {"sp": [f"D_sp{i}" for i in range(8)], "pool": [f"D_pl{i}" for i in range(8)]}
        self.dnext = {"sp": 0, "pool": 0}
        self.dcnt = {}
        self.pending_barrier = {e: {} for e in self.engs}
        self.cc_n = 0

    def _r(self, k):
        if k not in self.res:
            self.res[k] = {"w": None, "r": {}}
        return self.res[k]

    def add(self, eng, fn, reads=(), writes=(), kind="c"):
        deps = {}

        def need(tok):
            if tok is None:
                return
            s, v = tok
            if deps.get(s, 0) < v:
                deps[s] = v

        for k in reads:
            need(self._r(k)["w"])
        for k in writes:
            r = self._r(k)
            need(r["w"])
            for s, v in r["r"].items():
                need((s, v))
        for s, v in self.pending_barrier[eng].items():
            need((s, v))
        self.pending_barrier[eng] = {}
        if kind == "d":
            q = self.dq[eng]
            dsem = q[self.dnext[eng] % len(q)]
            self.dnext[eng] += 1
            prev = self.dcnt.get(dsem, 0)
            if prev:
                need((dsem, prev))
            self.dcnt[dsem] = prev + 16
            tok = (dsem, prev + 16)
            inc = (dsem, 16)
        elif kind == "cc":
            name = f"CC{self.cc_n}"
            self.cc_n += 1
            tok = (name, 1)
            inc = (name, None)
        else:
            self.count[eng] += 1
            tok = (f"S_{eng}", self.count[eng])
            inc = (f"S_{eng}", 1)
        waits = []
        for s, v in deps.items():
            if eng == "pe" and s == "S_pe":
                continue
            if self.waited[eng].get(s, 0) >= v:
                continue
            self.waited[eng][s] = v
            waits.append((s, v))
        self.ops[eng].append((waits, fn, inc))
        for k in reads:
            r = self._r(k)
            if r["r"].get(tok[0], 0) < tok[1]:
                r["r"][tok[0]] = tok[1]
        for k in writes:
            r = self._r(k)
            r["w"] = tok
            r["r"] = {}
        return tok

    def barrier(self):
        snap = {}
        for e in ["pe", "act", "dve", "pool"]:
            if self.count[e]:
                snap[f"S_{e}"] = self.count[e]
        for s, v in self.dcnt.items():
            snap[s] = v
        for i in range(self.cc_n):
            snap[f"CC{i}"] = 1
        for e in self.engs:
            self.pending_barrier[e] = dict(snap)

    def emit(self, eng, handle):
        for waits, fn, inc in self.ops[eng]:
            for s, v in waits:
                handle.wait_ge(self.sems[s], v)
            ins = fn(handle)
            if inc[1] is None:
                ins.then_inc(self.sems[inc[0]])
            else:
                ins.then_inc(self.sems[inc[0]], inc[1])

    def final_wait(self, eng):
        self.barrier()
        deps = self.pending_barrier[eng]
        waits = [(s, v) for s, v in deps.items() if self.waited[eng].get(s, 0) < v]
        return waits


def build_program(stop_after=None, layers=(0, 1), final=True):
    nc = bass.Bass("TRN2", target_bir_lowering=False)

    def din(name, shape):
        return nc.dram_tensor(name, list(shape), F32, kind="ExternalInput").ap()

    xT_d = din("xT", [D, T])
    memT_d = din("memT", [D, 256])
    pvec_d = din("pvec", [128, NPV])
    cst_d = din("cst", [128, NCST])
    bc_d = din("bc", [128, NBC])
    wspT_d = din("wspT", [128, L * 4 * 128])
    w_in_d = din("w_in", [L, D, INC])
    w_br_d = din("w_branch", [L, 3, 512, D])
    w_out_d = din("w_out", [L, D, D])
    wq_d = din("w_q_xa", [L, D, D])
    wk_d = din("w_k_xa", [L, D, D])
    wv_d = din("w_v_xa", [L, D, D])
    wo_d = din("w_o_xa", [L, D, D])
    wg_d = din("w_gate_ffn", [L, D, FH])
    wu_d = din("w_up_ffn", [L, D, FH])
    wd_d = din("w_down_ffn", [L, FH, D])
    yT_d = nc.dram_tensor("yT", [D, T], F32, kind="ExternalOutput").ap()

    scratch = {}
    for l_ in range(L):
        scratch[l_] = dict(
            kt_src=nc.dram_tensor(f"kt_src{l_}", [NB * 128, 512], BF16),
            kt_all=nc.dram_tensor(f"kt_all{l_}", [2 * NB * 128, 512], BF16),
            v_src=nc.dram_tensor(f"v_src{l_}", [NB * 128, 512], BF16),
            v_all=nc.dram_tensor(f"v_all{l_}", [2 * NB * 128, 512], BF16),
            tl_src=nc.dram_tensor(f"tl_src{l_}", [128, 128], BF16),
            tl_all=nc.dram_tensor(f"tl_all{l_}", [256, 128], BF16))

    sem_names = ["S_pe", "S_act", "S_dve", "S_pool"] + [f"D_sp{i}" for i in range(8)] + \
                [f"D_pl{i}" for i in range(8)] + [f"CC{i}" for i in range(3 * L)]

    import contextlib
    with contextlib.ExitStack() as es:
        xres_t = es.enter_context(nc.sbuf_tensor("xres", [128, KC, T], F32))
        ARENA_B = 122 * 1024
        arena = es.enter_context(nc.sbuf_tensor("arena", [128, ARENA_B // 2], BF16))
        pvec = es.enter_context(nc.sbuf_tensor("pvec_sb", [128, NPV], F32))
        cst = es.enter_context(nc.sbuf_tensor("cst_sb", [128, NCST], F32))
        bcp = es.enter_context(nc.sbuf_tensor("bcp", [128, 1536], F32))
        cb16 = es.enter_context(nc.sbuf_tensor("cb16", [128, 1280], BF16))
        wmT = es.enter_context(nc.sbuf_tensor("wmT_sb", [128, L * 4 * 128], BF16))
        small = es.enter_context(nc.sbuf_tensor("small", [128, 64], F32))
        psum = es.enter_context(nc.psum_tensor("ps", [128, 8, 512], F32))
        sems = {n: es.enter_context(nc.semaphore(n)) for n in sem_names}
        block = es.enter_context(nc.Block())

        S = Sched(nc, sems)
        xres = xres_t

        negtri_b = cb16[:, 0:128]
        negones_b = cb16[:, 128:256]
        ones_b = cb16[:, 256:384]
        amask_b = [[cb16[:, 384 + (p * 2 + w) * 128: 384 + (p * 2 + w + 1) * 128] for w in range(2)] for p in range(2)]
        sgum_b = cb16[:, 896:1024]

        def carve(off, shape, dt):
            es_ = 2 if dt == BF16 else 4
            n = int(np.prod(shape[1:]))
            assert off % 4 == 0 and off + n * es_ <= ARENA_B, (off, shape)
            ap = arena[:, off // 2: off // 2 + n * es_ // 2]
            if dt == F32:
                ap = ap.bitcast(F32)
            if len(shape) == 3:
                ap = ap.rearrange("p (a b) -> p a b", b=shape[2])
            elif len(shape) == 4:
                ap = ap.rearrange("p (a b c) -> p a b c", b=shape[2], c=shape[3])
            elif len(shape) == 5:
                ap = ap.rearrange("p (a b c d) -> p a b c d", b=shape[2], c=shape[3], d=shape[4])
            return ap

        PSB = lambda b: psum[:, b, :]
        evac_flip = [0]

        def mm(out, lhsT, rhs, start, stop, reads, writes):
            S.add("pe", lambda e: e.matmul(out, lhsT, rhs, start=start, stop=stop), reads, writes)

        def act(out, in_, func, reads, writes, scale=1.0, bias=0.0, accum=None):
            kw = {"scale": scale}
            if bias != 0.0:
                kw["bias"] = bias
            if accum is not None:
                kw["accum_out"] = accum
            S.add("act", lambda e: e.activation(out, in_, func, **kw), reads, writes)

        def tt(out, in0, in1, op, reads, writes, eng="dve"):
            S.add(eng, lambda e: e.tensor_tensor(out, in0, in1, op), reads, writes)

        def ts(out, in0, s1, s2, op0, op1, reads, writes, eng="dve"):
            if op1 is None:
                S.add(eng, lambda e: e.tensor_scalar(out, in0, s1, None, op0), reads, writes)
            else:
                S.add(eng, lambda e: e.tensor_scalar(out, in0, s1, s2, op0, op1), reads, writes)

        def stt(out, in0, scalar, in1, op0, op1, reads, writes):
            S.add("dve", lambda e: e.scalar_tensor_tensor(out, in0, scalar, in1, op0, op1), reads, writes)

        def cp(out, in_, reads, writes, eng=None):
            if eng is None:
                eng = "act" if evac_flip[0] % 2 == 0 else "dve"
                evac_flip[0] += 1
            if eng == "act":
                S.add("act", lambda e: e.activation(out, in_, AF.Copy), reads, writes)
            else:
                S.add(eng, lambda e: e.tensor_copy(out, in_), reads, writes)

        def dma(q, out, in_, reads, writes):
            S.add(q, lambda e: e.dma_start(out=out, in_=in_), reads, writes, kind="d")

        def allgather(src, dst, reads, writes):
            import os
            if os.environ.get("NO_CC"):
                S.add("pool", lambda e: e.dma_start(out=dst.ap()[0:src.shape[0], :], in_=src.ap()), reads, writes, kind="d")
                return
            S.add("pool", lambda e: e.collective_compute(
                "AllGather", ALU.bypass, replica_groups=PAIRS, ins=[src.ap().opt()], outs=[dst.ap().opt()]),
                reads, writes, kind="cc")

        xT_v = xT_d.rearrange("(c p) t -> p c t", p=128)
        for c in range(KC):
            dma("sp", xres[:, c, :], xT_v[:, c, :], [], [f"x{c}"])
        dma("sp", pvec[:], pvec_d[:, :], [], ["pvec"])
        dma("sp", cst[:], cst_d[:, :], [], ["cst"])
        cp(negtri_b, cst[:, CS_NEGTRI:CS_NEGTRI + 128], ["cst"], ["cb16"], eng="dve")
        S.add("dve", lambda e: e.memset(negones_b, -1.0), [], ["cb16"])
        S.add("dve", lambda e: e.memset(ones_b, 1.0), [], ["cb16"])
        S.add("dve", lambda e: e.memset(small[:, 62:63], -0.5), [], ["smc"])
        cp(cb16[:, 384:896], cst[:, CS_AMASK:CS_AMASK + 512], ["cst"], ["cb16"], eng="dve")
        wsp_stage = carve(0, [128, L * 4 * 128], F32)
        dma("sp", wsp_stage, wspT_d[:, :], [], ["wsp_stage"])
        tt(wmT[:].rearrange("p (a t) -> p a t", t=128), wsp_stage.rearrange("p (a t) -> p a t", t=128),
           cst[:, CS_SGUM:CS_SGUM + 128].unsqueeze(1).broadcast_to([128, L * 4, 128]), ALU.mult,
           ["wsp_stage", "cst"], ["wmT"])
        S.barrier()

        XR = [f"x{c}" for c in range(KC)]

        def rms_norm_tile(src3, ncols, gbase, hT_out3, hres, sq, rstd, lnv, psb, srcres, out_f32=False):
            act(sq, src3, AF.Square, srcres, ["sqA", "sqB"])
            for c in range(KC):
                mm(PSB(psb)[:, 0:ncols], ones_b, sq[:, c, :], c == 0, c == KC - 1, ["sqA", "sqB", "cb16"], [f"ps{psb}"])
            act(lnv, PSB(psb)[:, 0:ncols], AF.Ln, [f"ps{psb}"], ["lnv"], scale=1.0 / D, bias=1e-6)
            act(rstd, lnv, AF.Exp, ["lnv"], ["rstd"], scale=-0.5)
            for c in range(KC):
                stt(hT_out3[:, c, :], src3[:, c, :], pvec[:, gbase + c: gbase + c + 1], rstd, ALU.mult, ALU.mult,
                    srcres + ["rstd", "pvec"], [hres])

        def load_w(q, wb, wres, src_ap):
            dma(q, wb, src_ap, [], [wres])

        for l in layers:
            w_in_v = w_in_d[l].rearrange("(k p) c -> p k c", p=128)
            kt_src, kt_all, v_src, v_all, tl_src, tl_all = [scratch[l][k_] for k_ in
                                                            ("kt_src", "kt_all", "v_src", "v_all", "tl_src", "tl_all")]

            hT = carve(0, [128, KC, T], BF16)
            qT = carve(32768, [128, 4, T], BF16)
            wbuf = [carve(49152 + i * 8192, [128, KC, 512], BF16) for i in range(2)]
            sq = carve(65536, [128, KC, 512], BF16)
            rstd = carve(73728, [128, 512], F32)
            lnv = carve(75776, [128, 512], F32)
            kst = [carve(77824 + i * 4096, [128, T], BF16) for i in range(2)]
            vst = [carve(86016 + i * 4096, [128, 4, 512], BF16) for i in range(2)]
            tl_sb = carve(94208, [128, 128], BF16)
            cc_sb = carve(94464, [128, 128], F32)
            wcnt = [0]

            def next_w(col0, ncol=512, src=None):
                i = wcnt[0] % 2
                wcnt[0] += 1
                srcv = w_in_v if src is None else src
                load_w("pool", wbuf[i][:, :, 0:ncol], f"wbuf{i}", srcv[:, :, col0:col0 + ncol])
                return wbuf[i], f"wbuf{i}"

            wb5 = [wbuf[0], wbuf[1]] + [carve(95232 + i * 8192, [128, KC, 512], BF16) for i in range(3)]
            rb5 = ["wbuf0", "wbuf1", "wb5_2", "wb5_3", "wb5_4"]
            for i_, c0_ in enumerate((3072, 3584, 512, 1024, 0)):
                load_w("pool", wb5[i_], rb5[i_], w_in_v[:, :, c0_:c0_ + 512])
            for tt_ in range(4):
                sl = slice(tt_ * 512, (tt_ + 1) * 512)
                rms_norm_tile(xres[:, :, sl], 512, PV_MIXG + l * 8, hT[:, :, sl], f"hT{tt_}", sq, rstd, lnv, 7, XR)
            HT = [f"hT{i}" for i in range(4)]

            wcc, rcc = wb5[0], rb5[0]
            wcx, rcx = wb5[1], rb5[1]
            wk_, rk_ = wb5[2], rb5[2]
            kt_dst = kt_src.ap().rearrange("(m p) (c k) -> p m c k", p=128, k=128)
            pb = 0
            for tt_ in range(4):
                sl = slice(tt_ * 512, (tt_ + 1) * 512)
                ks = kst[tt_ % 2].rearrange("p (c t) -> p c t", t=512)
                for c in range(4):
                    for k in range(KC):
                        mm(PSB(pb), wk_[:, k, c * 128:(c + 1) * 128], hT[:, k, sl], k == 0, k == KC - 1,
                           [f"hT{tt_}", rk_], [f"ps{pb}"])
                    cp(ks[:, c, :], PSB(pb), [f"ps{pb}"], [f"kst{tt_ % 2}"])
                    pb = (pb + 1) % 4
                for c in range(4):
                    dma("sp", kt_dst[:, tt_ * 4:(tt_ + 1) * 4, c, :], ks[:, c, :].rearrange("p (m k) -> p m k", k=128),
                        [f"kst{tt_ % 2}"], ["kt_src"])
            allgather(kt_src, kt_all, ["kt_src"], ["kt_all"])

            wv_, rv_ = wb5[3], rb5[3]
            v_dst = v_src.ap().rearrange("(m p) c -> p m c", p=128)
            for m in range(NB):
                vs = vst[(m // 4) % 2]
                for k in range(KC):
                    mm(PSB(pb), hT[:, k, m * 128:(m + 1) * 128], wv_[:, k, :], k == 0, k == KC - 1,
                       [f"hT{m // 4}", rv_], [f"ps{pb}"])
                cp(vs[:, m % 4, :], PSB(pb), [f"ps{pb}"], [f"vst{(m // 4) % 2}"])
                pb = (pb + 1) % 4
                if m % 4 == 3:
                    dma("sp", v_dst[:, m - 3:m + 1, :], vs, [f"vst{(m // 4) % 2}"], ["v_src"])
            allgather(v_src, v_all, ["v_src"], ["v_all"])

            hT_tail = [hT[:, k, :].rearrange("p (m t) -> p m t", t=128)[:, :, 126:128] for k in range(KC)]
            for (wb_, rb_, half) in ((wcc, rcc, 0), (wcx, rcx, 1)):
                for c in range(4):
                    for k in range(KC):
                        mm(PSB(6)[:, half * 128 + c * 32: half * 128 + (c + 1) * 32], wb_[:, k, c * 128:(c + 1) * 128],
                           hT_tail[k], k == 0, k == KC - 1, HT + [rb_], ["ps6"])
            cp(cc_sb, PSB(6)[:, 0:128], ["ps6"], ["cc_sb"], eng="act")
            tt(tl_sb, cc_sb, PSB(6)[:, 128:256], ALU.mult, ["cc_sb", "ps6"], ["tl_sb"])
            dma("sp", tl_src[:, :], tl_sb, ["tl_sb"], ["tl_src"])
            allgather(tl_src, tl_all, ["tl_src"], ["tl_all"])

            wq_, rq_ = wb5[4], rb5[4]
            for tt_ in range(4):
                sl = slice(tt_ * 512, (tt_ + 1) * 512)
                for c in range(4):
                    for k in range(KC):
                        mm(PSB(pb), wq_[:, k, c * 128:(c + 1) * 128], hT[:, k, sl], k == 0, k == KC - 1,
                           [f"hT{tt_}", rq_], [f"ps{pb}"])
                    if (c * 4 + tt_) % 2 == 0:
                        act(qT[:, c, sl], PSB(pb), AF.Copy, [f"ps{pb}"], ["qT"], scale=0.125)
                    else:
                        ts(qT[:, c, sl], PSB(pb), 0.125, None, ALU.mult, None, [f"ps{pb}"], ["qT"])
                    pb = (pb + 1) % 4
            S.barrier()

            if stop_after == f"qkv{l}":
                break
            KTr = carve(49152, [128, 2 * NB, 512], BF16)
            Vr = carve(81920, [128, 2 * NB, 512], BF16)
            yaT = carve(0, [128, 4, T], BF16)
            Ebuf = [carve(16384, [128, 1024], F32), carve(118784, [128, 1024], F32)]
            Lsp = [carve(20480 + i * 2048, [128, 1024], BF16) for i in range(3)]
            Abuf = [carve(26624 + i * 2048, [128, 1024], BF16) for i in range(2)]
            Rb = [carve(114688 + i * 2048, [128, 1024], BF16) for i in range(2)]
            yatmp = carve(30720, [128, 4, 128], BF16)
            kt_v = kt_all.ap().rearrange("(b p) x -> p b x", p=128)
            v_v = v_all.ap().rearrange("(b p) x -> p b x", p=128)
            for i in range(4):
                dma("sp", KTr[:, i * 8:(i + 1) * 8, :], kt_v[:, i * 8:(i + 1) * 8, :], ["kt_all"], ["KT"])
            for i in range(4):
                dma("sp", Vr[:, i * 8:(i + 1) * 8, :], v_v[:, i * 8:(i + 1) * 8, :], ["v_all"], ["V"])

            steps = []
            for m in range(NB):
                G = 4 * (m // 2) + (1 if m % 2 == 0 else 3)
                for si, g in enumerate(range(G, -1, -1)):
                    r, ml = gblock(g)
                    steps.append(dict(m=m, si=si, g=g, kb=r * NB + ml, last=(g == 0), par=m % 2))
            NS = len(steps)
            Zb = lambda s: psum[:, (s % 2) * 2:(s % 2) * 2 + 2, :].rearrange("p a b -> p (a b)")
            Zres = lambda s: f"Z{s % 2}"
            Cb = psum[:, 4:6, :].rearrange("p a b -> p (a b)")
            Ob = psum[0:64, 6:8, :].rearrange("p a b -> p (a b)")

            def qk(out1024, st, start_flag, wres):
                m, kb = st["m"], st["kb"]
                for c in range(4):
                    for hh in range(2):
                        hp = hh * 4 + c
                        mm(out1024[:, hp * 128:(hp + 1) * 128],
                           KTr[hh * 64:(hh + 1) * 64, kb, c * 128:(c + 1) * 128],
                           qT[hh * 64:(hh + 1) * 64, c, m * 128:(m + 1) * 128],
                           start_flag, (True if start_flag else c == 3), ["KT", "qT"], [wres])

            def front(s):
                st = steps[s]
                qk(Zb(s), st, True, Zres(s))

            def mid0(s):
                act(Ebuf[s % 2], Zb(s), AF.Exp, [Zres(s)], [f"E{s % 2}"])

            def mid(s):
                st = steps[s]
                si = st["si"]
                lb = Lsp[s % 3]
                act(lb, Ebuf[s % 2], AF.Ln, [f"E{s % 2}"], [f"L{s % 3}"], bias=1.0)
                if si < 2:
                    mk = amask_b[st["par"]][1 - si]
                    tt(lb.rearrange("p (h t) -> p h t", t=128), lb.rearrange("p (h t) -> p h t", t=128),
                       mk.unsqueeze(1).broadcast_to([128, 8, 128]), ALU.mult, [f"L{s % 3}", "cb16"], [f"L{s % 3}"])
                if si == 0:
                    st["carry"] = None
                elif si == 1:
                    st["carry"] = (Lsp[(s - 1) % 3], f"L{(s - 1) % 3}")
                else:
                    prev = steps[s - 1]["carry"]
                    rb = Rb[s % 2]
                    tt(rb, prev[0], Lsp[(s - 1) % 3], ALU.add, [prev[1], f"L{(s - 1) % 3}"], [f"R{s % 2}"])
                    st["carry"] = (rb, f"R{s % 2}")

            def back1(s):
                st = steps[s]
                lb = Lsp[s % 3]
                for hb in range(2):
                    cs = slice(hb * 512, (hb + 1) * 512)
                    mm(Cb[:, cs], negtri_b, lb[:, cs], True, False, [f"L{s % 3}", "cb16"], ["C"])
                    if st["carry"] is not None:
                        mm(Cb[:, cs], negones_b, st["carry"][0][:, cs], False, False, [st["carry"][1], "cb16"], ["C"])
                qk(Cb, st, False, "C")

            def back2(s):
                st = steps[s]
                ab = Abuf[s % 2]
                act(ab, Cb, AF.Exp, ["C"], [f"A{s % 2}"])
                if st["si"] < 2:
                    mk = amask_b[st["par"]][1 - st["si"]]
                    tt(ab.rearrange("p (h t) -> p h t", t=128), ab.rearrange("p (h t) -> p h t", t=128),
                       mk.unsqueeze(1).broadcast_to([128, 8, 128]), ALU.mult, [f"A{s % 2}", "cb16"], [f"A{s % 2}"])

            def back3(s):
                st = steps[s]
                ab = Abuf[s % 2]
                m, kb = st["m"], st["kb"]
                for hp in range(8):
                    hh, c = divmod(hp, 4)
                    h = 2 * c + hh
                    mm(Ob[:, hp * 128:(hp + 1) * 128], Vr[:, kb, h * 64:(h + 1) * 64], ab[:, hp * 128:(hp + 1) * 128],
                       st["si"] == 0 and c == 0, st["last"] and c == 3, [f"A{s % 2}", "V"], ["O"])
                if st["last"]:
                    ov = Ob.rearrange("p (hh c t) -> p hh c t", hh=2, c=4)
                    S.add("dve", lambda e: e.tensor_copy(yaT[0:64, :, m * 128:(m + 1) * 128], ov[:, 0, :, :]),
                          ["O"], ["yaT"])
                    S.add("dve", lambda e: e.tensor_copy(yatmp[0:64], ov[:, 1, :, :]), ["O"], ["yatmp"])
                    dma("sp", yaT[64:128, :, m * 128:(m + 1) * 128], yatmp[0:64], ["yatmp"], ["yaT"])

            ok = lambda i: 0 <= i < NS
            for s in range(-2, NS + 2):
                if ok(s - 1):
                    back1(s - 1)
                if ok(s - 2):
                    back3(s - 2)
                if ok(s + 2):
                    front(s + 2)
                if ok(s + 1):
                    mid0(s + 1)
                if ok(s):
                    mid(s)
                if ok(s - 1):
                    back2(s - 1)
            S.barrier()

            if stop_after == f"attn{l}":
                break

            def emit_wout(t0_):
                w_out_v = w_out_d[l].rearrange("(k p) c -> p k c", p=128)
                pb_ = 0
                for cg in range(2):
                    wo_, ro_ = next_w(cg * 512, src=w_out_v)
                    for oc4 in range(4):
                        oc = cg * 4 + oc4
                        for tt_ in range(2):
                            sl = slice(tt_ * 512, (tt_ + 1) * 512)
                            gsl = slice(t0_ + tt_ * 512, t0_ + (tt_ + 1) * 512)
                            for k in range(KC):
                                mm(PSB(pb_), wo_[:, k, oc4 * 128:(oc4 + 1) * 128], mrgT[:, k, sl], k == 0, k == KC - 1,
                                   ["mrgT", ro_], [f"ps{pb_}"])
                            tt(xres[:, oc, gsl], xres[:, oc, gsl], PSB(pb_), ALU.add, [f"x{oc}", f"ps{pb_}"], [f"x{oc}"])
                            pb_ = (pb_ + 1) % 4

            dma("sp", bcp[:], bc_d[:, l * 1536:(l + 1) * 1536], [], ["bcp"])
            for hf in range(2):
                t0 = hf * 1024
                hTh = carve(16384, [128, KC, 1024], BF16)
                ybT = carve(32768, [128, 4, 1024], BF16)
                ycT = carve(40960, [128, 4, 1024], BF16)
                wbuf = [carve(49152 + i * 8192, [128, KC, 512], BF16) for i in range(2)]
                sq = carve(65536, [128, KC, 512], BF16)
                rstd = carve(73728, [128, 512], F32)
                lnv = carve(75776, [128, 512], F32)
                mrgT = carve(77824, [128, KC, 1024], BF16)
                yext = carve(94208, [128, 4, 8, 130], BF16)
                cbT = carve(102528, [128, 4, 1024], BF16)
                gv4 = carve(110720, [128, 4, 512], F32)
                vn4 = carve(118912, [128, 4, 512], BF16)
                tls = carve(123008, [128, 2, 4, NB, 2], BF16)
                gv, tmpf, ctmp = gv4[:, 0, :], gv4[:, 1, :], gv4[:, 2, :]
                sg = [carve(118912, [128, 512], F32)]
                tfb = [(rstd, "rstd"), (lnv, "lnv")]
                wcnt = [0]
                for tt_ in range(2):
                    sl = slice(tt_ * 512, (tt_ + 1) * 512)
                    gsl = slice(t0 + tt_ * 512, t0 + (tt_ + 1) * 512)
                    rms_norm_tile(xres[:, :, gsl], 512, PV_MIXG + l * 8, hTh[:, :, sl], f"hTh{tt_}", sq, rstd, lnv, 7, XR)
                HTH = ["hTh0", "hTh1"]
                if hf == 1:
                    emit_wout(0)
                pb = 0
                wz, rz = next_w(1536)
                for c in range(4):
                    for tt_ in range(2):
                        sl = slice(tt_ * 512, (tt_ + 1) * 512)
                        for k in range(KC):
                            mm(PSB(pb), wz[:, k, c * 128:(c + 1) * 128], hTh[:, k, sl], k == 0, k == KC - 1,
                               [f"hTh{tt_}", rz], [f"ps{pb}"])
                        act(ybT[:, c, sl], PSB(pb), AF.Gelu, [f"ps{pb}"], ["ybT"])
                        pb = (pb + 1) % 4
                if hf == 0:
                    dma("sp", tls.rearrange("p r c m t -> p r (c m t)"),
                        tl_all.ap().rearrange("(r p) x -> p r x", p=128), ["tl_all"], ["tls"])
                wcb, rcb = next_w(2560)
                for c in range(4):
                    for tt_ in range(2):
                        sl = slice(tt_ * 512, (tt_ + 1) * 512)
                        for k in range(KC):
                            mm(PSB(pb), wcb[:, k, c * 128:(c + 1) * 128], hTh[:, k, sl], k == 0, k == KC - 1,
                               [f"hTh{tt_}", rcb], [f"ps{pb}"])
                        cp(cbT[:, c, sl], PSB(pb), [f"ps{pb}"], ["cbT"])
                        pb = (pb + 1) % 4
                wcc, rcc = next_w(3072)
                wcx, rcx = next_w(3584)
                for c in range(4):
                    for tt_ in range(2):
                        sl = slice(tt_ * 512, (tt_ + 1) * 512)
                        p1 = pb
                        p2 = (pb + 1) % 4
                        pb = (pb + 2) % 4
                        for k in range(KC):
                            mm(PSB(p1), wcc[:, k, c * 128:(c + 1) * 128], hTh[:, k, sl], k == 0, k == KC - 1,
                               [f"hTh{tt_}", rcc], [f"ps{p1}"])
                        for k in range(KC):
                            mm(PSB(p2), wcx[:, k, c * 128:(c + 1) * 128], hTh[:, k, sl], k == 0, k == KC - 1,
                               [f"hTh{tt_}", rcx], [f"ps{p2}"])
                        cp(ctmp, PSB(p1), [f"ps{p1}"], ["gv4_2"], eng="act")
                        tt(yext[:, c, tt_ * 4:(tt_ + 1) * 4, 2:130], ctmp.rearrange("p (m t) -> p m t", t=128),
                           PSB(p2).rearrange("p (m t) -> p m t", t=128), ALU.mult, ["gv4_2", f"ps{p2}"], ["yext"])
                wz2, rz2 = next_w(2048)
                gw = [carve(49152 + i * 8192, [128, KC, 3, 128], BF16) for i in range(2)]
                bw = [carve(65536 + i * 4096, [128, 3, 4, 128], BF16) for i in range(2)]

                def load_gate(dc_):
                    dq_ = (dc_ + 1) % 2
                    gsrc = bass_gate_ap(w_in_d, l, dc_)
                    bsrc = w_br_d[l].rearrange("n (k p) d -> p n k d", p=128)
                    for n_ in range(3):
                        dma("pool", gw[dq_][:, :, n_, :], gsrc[:, :, n_, :], [], [f"wbuf{dq_}"])
                        dma("pool", bw[dq_][:, n_, :, :], bsrc[:, n_, :, dc_ * 128:(dc_ + 1) * 128], [],
                            [("sqA" if dq_ == 0 else "sqB")])

                load_gate(0)
                smc = lambda o, n=4: small[:, o:o + n]
                for grp in range(2):
                    for b_ in range(4):
                        mb = grp * 4 + b_
                        bs = slice(mb * 128, (mb + 1) * 128)
                        for k in range(KC):
                            mm(PSB(b_), hTh[:, k, bs], wz2[:, k, :], k == 0, k == KC - 1, [f"hTh{grp}", rz2], [f"ps{b_}"])
                        act(gv4[:, b_, :], PSB(b_), AF.Gelu, [f"ps{b_}"], [f"gv4_{b_}", "sm_sum"], accum=small[:, b_:b_ + 1])
                    for b_ in range(4):
                        act(vn4[:, b_, :], gv4[:, b_, :], AF.Square, [f"gv4_{b_}"], ["vn4", "sm_sq"], accum=small[:, 4 + b_:5 + b_])
                    ts(smc(8), smc(0), -1.0 / 512, None, ALU.mult, None, ["sm_sum"], ["sm_nm"])
                    tt(smc(12), smc(8), smc(8), ALU.mult, ["sm_nm"], ["sm_m2"])
                    ts(smc(16), smc(4), 1.0 / 512, 1e-5, ALU.mult, ALU.add, ["sm_sq"], ["sm_ve"])
                    tt(smc(20), smc(16), smc(12), ALU.subtract, ["sm_ve", "sm_m2"], ["sm_var"])
                    S.add("pool", lambda e: e.tensor_tensor(smc(24), smc(20), small[:, 62:63].broadcast_to([128, 4]), ALU.pow),
                          ["sm_var", "smc"], ["sm_rstd"])
                    for b_ in range(4):
                        ts(gv4[:, b_, :], gv4[:, b_, :], small[:, 8 + b_:9 + b_], small[:, 24 + b_:25 + b_], ALU.add, ALU.mult,
                           [f"gv4_{b_}", "sm_nm", "sm_rstd"], [f"gv4_{b_}"])
                    G4 = [f"gv4_{i}" for i in range(4)]
                    tt(gv4, gv4, bcp[:, BC_LNG: BC_LNG + 512].unsqueeze(1).broadcast_to([128, 4, 512]), ALU.mult,
                       G4 + ["bcp"], G4)
                    tt(vn4, gv4, bcp[:, BC_LNB: BC_LNB + 512].unsqueeze(1).broadcast_to([128, 4, 512]), ALU.add,
                       G4 + ["bcp"], ["vn4"])
                    for b_ in range(4):
                        for gi in range(4):
                            mm(PSB(4 + b_)[:, gi * 128:(gi + 1) * 128], vn4[:, b_, gi * 128:(gi + 1) * 128],
                               wmT[:, (l * 4 + gi) * 128:(l * 4 + gi + 1) * 128], True, True, ["vn4", "wmT"], [f"ps{4 + b_}"])
                    for b_ in range(4):
                        mb = grp * 4 + b_
                        bs = slice(mb * 128, (mb + 1) * 128)
                        tf_, T_ = tfb[b_ % 2]
                        tt(tf_, PSB(4 + b_), bcp[:, BC_BSP: BC_BSP + 512], ALU.add, [f"ps{4 + b_}", "bcp"], [T_])
                        tt(ybT[:, :, bs], tf_.rearrange("p (g t) -> p g t", t=128), ybT[:, :, bs], ALU.mult,
                           [T_, "ybT"], ["ybT"])
                selA = pvec[:, PV_SEL:PV_SEL + 1]
                selB = pvec[:, PV_SEL + 1:PV_SEL + 2]
                halo_t = small[:, 28:60].bitcast(BF16).rearrange("p (c j t) -> p c j t", c=4, t=2)
                jb = hf * 4
                for par in range(2):
                    if par == 0:
                        candB = tls[:, 0, :, 2 * jb:2 * jb + 8:2, :]
                        if hf == 0:
                            jlo = 1
                            candA = tls[:, 0, :, 1:6:2, :]
                        else:
                            jlo = 0
                            candA = tls[:, 0, :, 2 * jb - 1:2 * jb + 6:2, :]
                    else:
                        jlo = 0
                        candA = tls[:, 1, :, 2 * jb + 1:2 * jb + 8:2, :]
                        candB = tls[:, 1, :, 2 * jb:2 * jb + 8:2, :]
                    for c in range(4):
                        dst = yext[:, c, par:8:2, 0:2]
                        ht = halo_t[:, c, 0:4, :]
                        ts(ht, candB[:, c], selB, None, ALU.mult, None, ["tls", "pvec"], ["halo_t"])
                        if jlo == 1:
                            cp(dst[:, 0:1, :], ht[:, 0:1, :], ["halo_t"], ["yext"], eng="dve")
                            stt(dst[:, 1:4, :], candA[:, c], selA, ht[:, 1:4, :], ALU.mult, ALU.add,
                                ["tls", "pvec", "halo_t"], ["yext"])
                        else:
                            stt(dst, candA[:, c], selA, ht, ALU.mult, ALU.add, ["tls", "pvec", "halo_t"], ["yext"])
                for c in range(4):
                    for tt_ in range(2):
                        sl = slice(tt_ * 512, (tt_ + 1) * 512)
                        ye = yext[:, c, tt_ * 4:(tt_ + 1) * 4, :]
                        acc = ctmp.rearrange("p (m t) -> p m t", t=128)
                        wcol = lambda k: pvec[:, PV_CONV + (l * 3 + k) * 4 + c: PV_CONV + (l * 3 + k) * 4 + c + 1]
                        ts(acc, ye[:, :, 0:128], wcol(0), None, ALU.mult, None, ["yext", "pvec"], ["gv4_2"])
                        stt(acc, ye[:, :, 1:129], wcol(1), acc, ALU.mult, ALU.add, ["yext", "pvec", "gv4_2"], ["gv4_2"])
                        stt(acc, ye[:, :, 2:130], wcol(2), acc, ALU.mult, ALU.add, ["yext", "pvec", "gv4_2"], ["gv4_2"])
                        tt(ycT[:, c, sl], ctmp, cbT[:, c, sl], ALU.mult, ["gv4_2", "cbT"], ["ycT"])
                gw = [carve(49152 + i * 8192, [128, KC, 3, 128], BF16) for i in range(2)]
                bw = [carve(65536 + i * 4096, [128, 3, 4, 128], BF16) for i in range(2)]
                brs = [(yaT, "yaT", t0), (ybT, "ybT", 0), (ycT, "ycT", 0)]
                gcnt = 0
                for dc in range(KC):
                    dq = (dc + 1) % 2
                    g_ = gw[dq]
                    b_ = bw[dq]
                    if dc > 0:
                        load_gate(dc)
                    for tt_ in range(2):
                        sl = slice(tt_ * 512, (tt_ + 1) * 512)
                        for n in range(3):
                            pg = (gcnt % 4) * 2
                            pbq = pg + 1
                            gcnt += 1
                            for k in range(KC):
                                mm(PSB(pg), g_[:, k, n, :], hTh[:, k, sl], k == 0, k == KC - 1,
                                   [f"hTh{tt_}", f"wbuf{dq}"], [f"ps{pg}"])
                            src, sres, o0 = brs[n]
                            for k in range(4):
                                mm(PSB(pbq), b_[:, n, k, :], src[:, k, o0 + tt_ * 512: o0 + (tt_ + 1) * 512],
                                   k == 0, k == 3, [sres, ("sqA" if dq == 0 else "sqB")], [f"ps{pbq}"])
                            act(sg[0], PSB(pg), AF.Sigmoid, [f"ps{pg}"], ["vn4"])
                            if n == 0:
                                tt(gv, sg[0], PSB(pbq), ALU.mult, ["vn4", f"ps{pbq}"], ["gv4_0"])
                            elif n == 1:
                                tt(tmpf, sg[0], PSB(pbq), ALU.mult, ["vn4", f"ps{pbq}"], ["gv4_1"])
                                tt(gv, gv, tmpf, ALU.add, ["gv4_0", "gv4_1"], ["gv4_0"])
                            else:
                                tt(tmpf, sg[0], PSB(pbq), ALU.mult, ["vn4", f"ps{pbq}"], ["gv4_1"])
                                tt(mrgT[:, dc, sl], gv, tmpf, ALU.add, ["gv4_0", "gv4_1"], ["mrgT"])
                if hf == 1:
                    emit_wout(1024)

            S.barrier()
            if stop_after == f"mix{l}":
                break

            hT = carve(0, [128, KC, T], BF16)
            qx = carve(32768, [128, KC, T], BF16)
            wbuf = [carve(65536 + i * 8192, [128, KC, 512], BF16) for i in range(2)]
            sq = carve(81920, [128, KC, 512], BF16)
            rstd = carve(90112, [128, 512], F32)
            lnv = carve(92160, [128, 512], F32)
            memf = carve(94208, [128, KC, 256], F32)
            mTn = carve(102400, [128, KC, 256], BF16)
            kTx = carve(106496, [128, KC, 256], BF16)
            vx = carve(110592, [128, 2, D], BF16)
            pT = [carve(114688, [128, 2, 512], BF16), carve(118784, [128, 2, 512], BF16)]
            recb = [carve(116736, [128, 512], F32), carve(120832, [128, 512], F32)]
            wcnt = [0]
            wk_v = wk_d[l].rearrange("(k p) c -> p k c", p=128)
            wv_v = wv_d[l].rearrange("(k p) c -> p k c", p=128)
            wq_v = wq_d[l].rearrange("(k p) c -> p k c", p=128)
            wo_v = wo_d[l].rearrange("(k p) c -> p k c", p=128)
            dma("sp", memf, memT_d.rearrange("(c p) t -> p c t", p=128), [], ["memf"])
            rms_norm_tile(memf, 256, PV_MEMG + l * 8, mTn, "mTn", sq[:, :, 0:256], rstd[:, 0:256], lnv[:, 0:256], 6, ["memf"])
            pbx = [0]

            def xnorm(tt_):
                sl = slice(tt_ * 512, (tt_ + 1) * 512)
                rms_norm_tile(xres[:, :, sl], 512, PV_XAG + l * 8, hT[:, :, sl], f"hT{tt_}", sq, rstd, lnv, 7, XR)

            def memk(cg):
                w_, r_ = next_w(cg * 512, src=wk_v)
                for dc4 in range(4):
                    dc = cg * 4 + dc4
                    pb = pbx[0]
                    for k in range(KC):
                        mm(PSB(pb)[:, 0:256], w_[:, k, dc4 * 128:(dc4 + 1) * 128], mTn[:, k, :], k == 0, k == KC - 1,
                           ["mTn", r_], [f"ps{pb}"])
                    cp(kTx[:, dc, :], PSB(pb)[:, 0:256], [f"ps{pb}"], ["kTx"])
                    pbx[0] = (pb + 1) % 4

            def memv(cg):
                w_, r_ = next_w(cg * 512, src=wv_v)
                for kc in range(2):
                    pb = pbx[0]
                    for k in range(KC):
                        mm(PSB(pb), mTn[:, k, kc * 128:(kc + 1) * 128], w_[:, k, :], k == 0, k == KC - 1,
                           ["mTn", r_], [f"ps{pb}"])
                    cp(vx[:, kc, cg * 512:(cg + 1) * 512], PSB(pb), [f"ps{pb}"], ["vx"])
                    pbx[0] = (pb + 1) % 4

            xnorm(0)
            memk(0)
            xnorm(1)
            memk(1)
            xnorm(2)
            memv(0)
            xnorm(3)
            memv(1)
            pb = pbx[0]
            for cg in range(2):
                w_, r_ = next_w(cg * 512, src=wq_v)
                for tt_ in range(4):
                    sl = slice(tt_ * 512, (tt_ + 1) * 512)
                    for dc4 in range(4):
                        dc = cg * 4 + dc4
                        for k in range(KC):
                            mm(PSB(pb), w_[:, k, dc4 * 128:(dc4 + 1) * 128], hT[:, k, sl], k == 0, k == KC - 1,
                               [f"hT{tt_}", r_], [f"ps{pb}"])
                        if (dc * 4 + tt_) % 2 == 0:
                            act(qx[:, dc, sl], PSB(pb), AF.Copy, [f"ps{pb}"], ["qx"], scale=0.0625)
                        else:
                            ts(qx[:, dc, sl], PSB(pb), 0.0625, None, ALU.mult, None, [f"ps{pb}"], ["qx"])
                        pb = (pb + 1) % 4
            oT = hT
            it = 0
            for tt_ in range(4):
                sl = slice(tt_ * 512, (tt_ + 1) * 512)
                for hx in range(4):
                    q_ = it % 2
                    it += 1
                    pT_, rec_ = pT[q_], recb[q_]
                    P_, R_ = f"pT{q_}", f"rec{q_}"
                    bd = 2 + q_
                    bo = 4 + 2 * q_
                    for kc in range(2):
                        for dd in range(2):
                            mm(PSB(kc), kTx[:, hx * 2 + dd, kc * 128:(kc + 1) * 128], qx[:, hx * 2 + dd, sl],
                               dd == 0, dd == 1, ["kTx", "qx"], [f"ps{kc}"])
                    act(pT_, psum[:, 0:2, :], AF.Exp, ["ps0", "ps1"], [P_])
                    for kc in range(2):
                        mm(PSB(bd), ones_b, pT_[:, kc, :], kc == 0, kc == 1, [P_, "cb16"], [f"ps{bd}"])
                    for dd in range(2):
                        for kc in range(2):
                            mm(PSB(bo + dd), vx[:, kc, hx * 256 + dd * 128: hx * 256 + (dd + 1) * 128], pT_[:, kc, :],
                               kc == 0, kc == 1, [P_, "vx"], [f"ps{bo + dd}"])
                    act(rec_, PSB(bd), AF.Ln, [f"ps{bd}"], [R_])
                    act(rec_, rec_, AF.Exp, [R_], [R_], scale=-1.0)
                    for dd in range(2):
                        tt(oT[:, hx * 2 + dd, sl], PSB(bo + dd), rec_, ALU.mult, [f"ps{bo + dd}", R_], [f"hT{tt_}"])
            pb = 0
            for cg in range(2):
                w_, r_ = next_w(cg * 512, src=wo_v)
                for oc4 in range(4):
                    oc = cg * 4 + oc4
                    for tt_ in range(4):
                        sl = slice(tt_ * 512, (tt_ + 1) * 512)
                        for k in range(KC):
                            mm(PSB(pb), w_[:, k, oc4 * 128:(oc4 + 1) * 128], oT[:, k, sl], k == 0, k == KC - 1,
                               [f"hT{tt_}", r_], [f"ps{pb}"])
                        tt(xres[:, oc, sl], xres[:, oc, sl], PSB(pb), ALU.add, [f"x{oc}", f"ps{pb}"], [f"x{oc}"])
                        pb = (pb + 1) % 4
            S.barrier()
            if stop_after == f"xa{l}":
                break

            wg_v = wg_d[l].rearrange("(k p) c -> p k c", p=128)
            wu_v = wu_d[l].rearrange("(k p) c -> p k c", p=128)
            wd_v = wd_d[l].rearrange("(k p) c -> p k c", p=128)
            for hf in range(2):
                t0 = hf * 1024
                hTh = carve(0, [128, KC, 1024], BF16)
                h1T = carve(16384, [128, NJ, 1024], BF16)
                wgb = [carve(61440 + i * 8192, [128, KC, 512], BF16) for i in range(2)]
                wub = [carve(77824 + i * 8192, [128, KC, 512], BF16) for i in range(2)]
                sq = carve(94208, [128, KC, 512], BF16)
                rstd = carve(102400, [128, 512], F32)
                lnv = carve(104448, [128, 512], F32)
                sgf = carve(106496, [128, 512], F32)
                tf = carve(108544, [128, 512], F32)
                def ffn_norm(t0_):
                    for tt_ in range(2):
                        sl = slice(tt_ * 512, (tt_ + 1) * 512)
                        gsl = slice(t0_ + tt_ * 512, t0_ + (tt_ + 1) * 512)
                        rms_norm_tile(xres[:, :, gsl], 512, PV_FFNG + l * 8, hTh[:, :, sl], f"hTh{tt_}", sq, rstd, lnv, 7, XR)

                if hf == 0:
                    ffn_norm(0)
                ngrp = 6
                pcount = 0
                for gi in range(ngrp):
                    c0 = gi * 512
                    ncol = min(512, FH - c0)
                    i = gi % 2
                    dma("pool", wgb[i][:, :, 0:ncol], wg_v[:, :, c0:c0 + ncol], [], [f"wgb{i}"])
                    dma("pool", wub[i][:, :, 0:ncol], wu_v[:, :, c0:c0 + ncol], [], [f"wub{i}"])
                    for jj in range(ncol // 128):
                        j = gi * 4 + jj
                        for tt_ in range(2):
                            sl = slice(tt_ * 512, (tt_ + 1) * 512)
                            pg = (pcount % 4) * 2
                            pu = pg + 1
                            pcount += 1
                            for k in range(KC):
                                mm(PSB(pg), wgb[i][:, k, jj * 128:(jj + 1) * 128], hTh[:, k, sl], k == 0, k == KC - 1,
                                   [f"hTh{tt_}", f"wgb{i}"], [f"ps{pg}"])
                            for k in range(KC):
                                mm(PSB(pu), wub[i][:, k, jj * 128:(jj + 1) * 128], hTh[:, k, sl], k == 0, k == KC - 1,
                                   [f"hTh{tt_}", f"wub{i}"], [f"ps{pu}"])
                            act(sgf, PSB(pg), AF.Sigmoid, [f"ps{pg}"], ["sgf"])
                            tt(tf, sgf, PSB(pg), ALU.mult, ["sgf", f"ps{pg}"], ["tf"])
                            tt(h1T[:, j, sl], tf, PSB(pu), ALU.mult, ["tf", f"ps{pu}"], ["h1T"])
                if hf == 0:
                    ffn_norm(1024)
                wdb = [carve(110592, [128, NJ, 256], BF16), carve(61440, [128, NJ, 256], BF16)]
                wdn = [["wdb0"], ["wgb0", "wgb1"]]
                pb = 0
                for cg in range(4):
                    i = cg % 2
                    dma("pool", wdb[i], wd_v[:, :, cg * 256:(cg + 1) * 256], [], wdn[i])
                    for oc2 in range(2):
                        oc = cg * 2 + oc2
                        for tt_ in range(2):
                            sl = slice(tt_ * 512, (tt_ + 1) * 512)
                            gsl = slice(t0 + tt_ * 512, t0 + (tt_ + 1) * 512)
                            for j in range(NJ):
                                mm(PSB(pb), wdb[i][:, j, oc2 * 128:(oc2 + 1) * 128], h1T[:, j, sl], j == 0, j == NJ - 1,
                                   ["h1T"] + wdn[i], [f"ps{pb}"])
                            tt(xres[:, oc, gsl], xres[:, oc, gsl], PSB(pb), ALU.add, [f"x{oc}", f"ps{pb}"], [f"x{oc}"])
                            pb = (pb + 1) % 4
            S.barrier()
            if stop_after == f"ffn{l}":
                break

        yT_v = yT_d.rearrange("(c p) t -> p c t", p=128)
        if stop_after is None and final:
            sq = carve(0, [128, KC, 512], BF16)
            rstd = carve(8192, [128, 512], F32)
            lnv = carve(10240, [128, 512], F32)
            outb = [carve(16384 + i * 16384, [128, KC, 512], F32) for i in range(2)]
            for tt_ in range(4):
                sl = slice(tt_ * 512, (tt_ + 1) * 512)
                ob = outb[tt_ % 2]
                rms_norm_tile(xres[:, :, sl], 512, PV_FING, ob, f"outb{tt_ % 2}", sq, rstd, lnv, 7, XR)
                dma("sp", yT_v[:, :, sl], ob, [f"outb{tt_ % 2}"], ["yT"])
        else:
            for c in range(KC):
                dma("sp", yT_v[:, c, :], xres[:, c, :], [f"x{c}"], ["yT"])
        fw = S.final_wait("sp")

        @block.tensor
        def _(e):
            S.emit("pe", e)

        @block.scalar
        def _(e):
            S.emit("act", e)

        @block.vector
        def _(e):
            S.emit("dve", e)

        @block.gpsimd
        def _(e):
            S.emit("pool", e)

        @block.sync
        def _(e):
            S.emit("sp", e)
            for s, v in fw:
                e.wait_ge(sems[s], v)

    return nc


def bass_gate_ap(w_in_d, l, dc):
    v = w_in_d[l].rearrange("(k p) c -> p k c", p=128)[:, :, 4096:7168]
    return v.rearrange("p k (n c) -> p k n c", n=3)[:, :, :, dc * 128:(dc + 1) * 128]


_CACHE = {}


def _owned_blocks(role):
    out = []
    for j in range(8):
        out += ([4 * j, 4 * j + 3] if role == 0 else [4 * j + 1, 4 * j + 2])
    return out


def _prep_inputs(inp):
    f = np.float32
    x = np.asarray(inp["x"], f)
    mem = np.asarray(inp["mem"], f)
    gv = lambda k: np.asarray(inp[k], f)

    def pcols(a):
        a = a.reshape(-1, 8, 128)
        return np.ascontiguousarray(a.transpose(2, 0, 1).reshape(128, -1))

    pvec_base = np.zeros((128, NPV), f)
    pvec_base[:, PV_MIXG:PV_MIXG + 16] = pcols(gv("norm_mix_g"))
    pvec_base[:, PV_XAG:PV_XAG + 16] = pcols(gv("norm_xa_g"))
    pvec_base[:, PV_MEMG:PV_MEMG + 16] = pcols(gv("mem_norm_g"))
    pvec_base[:, PV_FFNG:PV_FFNG + 16] = pcols(gv("norm_ffn_g"))
    pvec_base[:, PV_FING:PV_FING + 8] = pcols(gv("final_g")[None])
    cw = gv("conv_w").reshape(L * 3, 4, 128)
    pvec_base[:, PV_CONV:PV_CONV + 24] = cw.transpose(2, 0, 1).reshape(128, 24)

    bc = np.zeros((128, NBC), f)
    for l_ in range(L):
        o = l_ * 1536
        bc[:, o:o + 512] = np.broadcast_to(gv("sgu_ln_g")[l_].reshape(1, -1), (128, 512))
        bc[:, o + 512:o + 1024] = np.broadcast_to(gv("sgu_ln_b")[l_].reshape(1, -1), (128, 512))
        bc[:, o + 1024:o + 1536] = np.broadcast_to(gv("b_spatial")[l_].reshape(1, -1), (128, 512))
    wspT = np.ascontiguousarray(gv("w_spatial").transpose(3, 0, 1, 2).reshape(128, L * 4 * 128))

    j = np.arange(128)
    negtri = -(j[:, None] >= j[None, :]).astype(f)
    sgum = ((j[None, :] // 64) >= (j[:, None] // 64)).astype(f)
    diag = (j[:, None] < j[None, :]).astype(f)
    full = np.ones((128, 128), f)
    none = np.zeros((128, 128), f)
    amasks = {0: [[diag, none], [full, diag]], 1: [[full, diag], [diag, none]]}

    shared = {k: np.ascontiguousarray(gv(k)) for k in
              ["w_in", "w_branch", "w_out", "w_q_xa", "w_k_xa", "w_v_xa", "w_o_xa",
               "w_gate_ffn", "w_up_ffn", "w_down_ffn"]}
    in_maps = []
    for core in range(8):
        b, role = divmod(core, 2)
        blocks = _owned_blocks(role)
        xb = x[b].reshape(32, 128, D)[blocks].reshape(T, D)
        cst = np.zeros((128, NCST), f)
        cst[:, CS_NEGTRI:CS_NEGTRI + 128] = negtri
        cst[:, CS_SGUM:CS_SGUM + 128] = sgum
        for par in range(2):
            for w in range(2):
                o = CS_AMASK + (par * 2 + w) * 128
                cst[:, o:o + 128] = amasks[role][par][w]
        pv = pvec_base.copy()
        pv[:, PV_SEL] = 1.0 if role == 0 else 0.0
        pv[:, PV_SEL + 1] = 0.0 if role == 0 else 1.0
        m = dict(shared)
        m.update({"xT": np.ascontiguousarray(xb.T), "memT": np.ascontiguousarray(mem[b].T),
                  "pvec": pv, "cst": cst, "bc": bc, "wspT": wspT})
        in_maps.append(m)
    return in_maps


def _assemble(results):
    out = np.zeros((4, 4096, D), np.float32)
    for core in range(8):
        b, role = divmod(core, 2)
        blocks = _owned_blocks(role)
        y = np.asarray(results[core]["yT"]).T.reshape(NB, 128, D)
        out[b].reshape(32, 128, D)[blocks] = y
    return out


FUSED = True


def kernel(**inputs):
    in_maps = _prep_inputs(inputs)
    if FUSED:
        if "nc" not in _CACHE:
            _CACHE["nc"] = build_program()
        res = run_bass_kernel_spmd(_CACHE["nc"], in_maps, core_ids=list(range(8)))
        return _assemble(res.results)
    if "nc0" not in _CACHE:
        _CACHE["nc0"] = build_program(layers=(0,), final=False)
        _CACHE["nc1"] = build_program(layers=(1,), final=True)
    res0 = run_bass_kernel_spmd(_CACHE["nc0"], in_maps, core_ids=list(range(8)))
    for c in range(8):
        in_maps[c]["xT"] = np.ascontiguousarray(np.asarray(res0.results[c]["yT"], np.float32))
    res1 = run_bass_kernel_spmd(_CACHE["nc1"], in_maps, core_ids=list(range(8)))
    return _assemble(res1.results)
```

```python
import numpy as np
import concourse.bass as bass
import concourse.mybir as mybir
from concourse.bass_utils import run_bass_kernel_spmd

F32 = mybir.dt.float32
BF16 = mybir.dt.bfloat16
AF = mybir.ActivationFunctionType
ALU = mybir.AluOpType

L = 2
D = 1024
T = 2048
NB = 16
KC = 8
FH = 2816
NJ = 22
INC = 7168
PAIRS = [[0, 1], [2, 3], [4, 5], [6, 7]]

PV_MIXG = 0
PV_XAG = 16
PV_MEMG = 32
PV_FFNG = 48
PV_FING = 64
PV_CONV = 72
PV_SEL = 96
NPV = 98
CS_NEGTRI = 0
CS_SGUM = 128
CS_AMASK = 256
NCST = 768
BC_LNG = 0
BC_LNB = 512
BC_BSP = 1024
NBC = 3072


def gblock(g):
    j, i = divmod(g, 4)
    return [(0, 2 * j), (1, 2 * j), (1, 2 * j + 1), (0, 2 * j + 1)][i]


class Sched:
    def __init__(self, nc, sems):
        self.nc = nc
        self.sems = sems
        self.engs = ["pe", "act", "dve", "pool", "sp"]
        self.ops = {e: [] for e in self.engs}
        self.count = {e: 0 for e in self.engs}
        self.waited = {e: {} for e in self.engs}
        self.res = {}
        self.dq = {"sp": [f"D_sp{i}" for i in range(8)], "pool": [f"D_pl{i}" for i in range(8)]}
        self.dnext = {"sp": 0, "pool": 0}
        self.dcnt = {}
        self.pending_barrier = {e: {} for e in self.engs}
        self.cc_n = 0

    def _r(self, k):
        if k not in self.res:
            self.res[k] = {"w": None, "r": {}}
        return self.res[k]

    def add(self, eng, fn, reads=(), writes=(), kind="c"):
        deps = {}

        def need(tok):
            if tok is None:
                return
            s, v = tok
            if deps.get(s, 0) < v:
                deps[s] = v

        for k in reads:
            need(self._r(k)["w"])
        for k in writes:
            r = self._r(k)
            need(r["w"])
            for s, v in r["r"].items():
                need((s, v))
        for s, v in self.pending_barrier[eng].items():
            need((s, v))
        self.pending_barrier[eng] = {}
        if kind == "d":
            q = self.dq[eng]
            dsem = q[self.dnext[eng] % len(q)]
            self.dnext[eng] += 1
            prev = self.dcnt.get(dsem, 0)
            if prev:
                need((dsem, prev))
            self.dcnt[dsem] = prev + 16
            tok = (dsem, prev + 16)
            inc = (dsem, 16)
        elif kind == "cc":
            name = f"CC{self.cc_n}"
            self.cc_n += 1
            tok = (name, 1)
            inc = (name, None)
        else:
            self.count[eng] += 1
            tok = (f"S_{eng}", self.count[eng])
            inc = (f"S_{eng}", 1)
        waits = []
        for s, v in deps.items():
            if eng == "pe" and s == "S_pe":
                continue
            if self.waited[eng].get(s, 0) >= v:
                continue
            self.waited[eng][s] = v
            waits.append((s, v))
        self.ops[eng].append((waits, fn, inc))
        for k in reads:
            r = self._r(k)
            if r["r"].get(tok[0], 0) < tok[1]:
                r["r"][tok[0]] = tok[1]
        for k in writes:
            r = self._r(k)
            r["w"] = tok
            r["r"] = {}
        return tok

    def barrier(self):
        snap = {}
        for e in ["pe", "act", "dve", "pool"]:
            if self.count[e]:
                snap[f"S_{e}"] = self.count[e]
        for s, v in self.dcnt.items():
            snap[s] = v
        for i in range(self.cc_n):
            snap[f"CC{i}"] = 1
        for e in self.engs:
            self.pending_barrier[e] = dict(snap)

    def emit(self, eng, handle):
        for waits, fn, inc in self.ops[eng]:
            for s, v in waits:
                handle.wait_ge(self.sems[s], v)
            ins = fn(handle)
            if inc[1] is None:
                ins.then_inc(self.sems[inc[0]])
            else:
                ins.then_inc(self.sems[inc[0]], inc[1])

    def final_wait(self, eng):
        self.barrier()
        deps = self.pending_barrier[eng]
        waits = [(s, v) for s, v in deps.items() if self.waited[eng].get(s, 0) < v]
        return waits


def build_program(stop_after=None, layers=(0, 1), final=True):
    nc = bass.Bass("TRN2", target_bir_lowering=False)

    def din(name, shape):
        return nc.dram_tensor(name, list(shape), F32, kind="ExternalInput").ap()

    xT_d = din("xT", [D, T])
    memT_d = din("memT", [D, 256])
    pvec_d = din("pvec", [128, NPV])
    cst_d = din("cst", [128, NCST])
    bc_d = din("bc", [128, NBC])
    wspT_d = din("wspT", [128, L * 4 * 128])
    w_in_d = din("w_in", [L, D, INC])
    w_br_d = din("w_branch", [L, 3, 512, D])
    w_out_d = din("w_out", [L, D, D])
    wq_d = din("w_q_xa", [L, D, D])
    wk_d = din("w_k_xa", [L, D, D])
    wv_d = din("w_v_xa", [L, D, D])
    wo_d = din("w_o_xa", [L, D, D])
    wg_d = din("w_gate_ffn", [L, D, FH])
    wu_d = din("w_up_ffn", [L, D, FH])
    wd_d = din("w_down_ffn", [L, FH, D])
    yT_d = nc.dram_tensor("yT", [D, T], F32, kind="ExternalOutput").ap()

    scratch = {}
    for l_ in range(L):
        scratch[l_] = dict(
            kt_src=nc.dram_tensor(f"kt_src{l_}", [NB * 128, 512], BF16),
            kt_all=nc.dram_tensor(f"kt_all{l_}", [2 * NB * 128, 512], BF16),
            v_src=nc.dram_tensor(f"v_src{l_}", [NB * 128, 512], BF16),
            v_all=nc.dram_tensor(f"v_all{l_}", [2 * NB * 128, 512], BF16),
            tl_src=nc.dram_tensor(f"tl_src{l_}", [128, 128], BF16),
            tl_all=nc.dram_tensor(f"tl_all{l_}", [256, 128], BF16))

    sem_names = ["S_pe", "S_act", "S_dve", "S_pool"] + [f"D_sp{i}" for i in range(8)] + \
                [f"D_pl{i}" for i in range(8)] + [f"CC{i}" for i in range(3 * L)]

    import contextlib
    with contextlib.ExitStack() as es:
        xres_t = es.enter_context(nc.sbuf_tensor("xres", [128, KC, T], F32))
        ARENA_B = 122 * 1024
        arena = es.enter_context(nc.sbuf_tensor("arena", [128, ARENA_B // 2], BF16))
        pvec = es.enter_context(nc.sbuf_tensor("pvec_sb", [128, NPV], F32))
        cst = es.enter_context(nc.sbuf_tensor("cst_sb", [128, NCST], F32))
        bcp = es.enter_context(nc.sbuf_tensor("bcp", [128, 1536], F32))
        cb16 = es.enter_context(nc.sbuf_tensor("cb16", [128, 1280], BF16))
        wmT = es.enter_context(nc.sbuf_tensor("wmT_sb", [128, L * 4 * 128], BF16))
        small = es.enter_context(nc.sbuf_tensor("small", [128, 64], F32))
        psum = es.enter_context(nc.psum_tensor("ps", [128, 8, 512], F32))
        sems = {n: es.enter_context(nc.semaphore(n)) for n in sem_names}
        block = es.enter_context(nc.Block())

        S = Sched(nc, sems)
        xres = xres_t

        negtri_b = cb16[:, 0:128]
        negones_b = cb16[:, 128:256]
        ones_b = cb16[:, 256:384]
        amask_b = [[cb16[:, 384 + (p * 2 + w) * 128: 384 + (p * 2 + w + 1) * 128] for w in range(2)] for p in range(2)]
        sgum_b = cb16[:, 896:1024]

        def carve(off, shape, dt):
            es_ = 2 if dt == BF16 else 4
            n = int(np.prod(shape[1:]))
            assert off % 4 == 0 and off + n * es_ <= ARENA_B, (off, shape)
            ap = arena[:, off // 2: off // 2 + n * es_ // 2]
            if dt == F32:
                ap = ap.bitcast(F32)
            if len(shape) == 3:
                ap = ap.rearrange("p (a b) -> p a b", b=shape[2])
            elif len(shape) == 4:
                ap = ap.rearrange("p (a b c) -> p a b c", b=shape[2], c=shape[3])
            elif len(shape) == 5:
                ap = ap.rearrange("p (a b c d) -> p a b c d", b=shape[2], c=shape[3], d=shape[4])
            return ap

        PSB = lambda b: psum[:, b, :]
        evac_flip = [0]

        def mm(out, lhsT, rhs, start, stop, reads, writes):
            S.add("pe", lambda e: e.matmul(out, lhsT, rhs, start=start, stop=stop), reads, writes)

        def act(out, in_, func, reads, writes, scale=1.0, bias=0.0, accum=None):
            kw = {"scale": scale}
            if bias != 0.0:
                kw["bias"] = bias
            if accum is not None:
                kw["accum_out"] = accum
            S.add("act", lambda e: e.activation(out, in_, func, **kw), reads, writes)

        def tt(out, in0, in1, op, reads, writes, eng="dve"):
            S.add(eng, lambda e: e.tensor_tensor(out, in0, in1, op), reads, writes)

        def ts(out, in0, s1, s2, op0, op1, reads, writes, eng="dve"):
            if op1 is None:
                S.add(eng, lambda e: e.tensor_scalar(out, in0, s1, None, op0), reads, writes)
            else:
                S.add(eng, lambda e: e.tensor_scalar(out, in0, s1, s2, op0, op1), reads, writes)

        def stt(out, in0, scalar, in1, op0, op1, reads, writes):
            S.add("dve", lambda e: e.scalar_tensor_tensor(out, in0, scalar, in1, op0, op1), reads, writes)

        def cp(out, in_, reads, writes, eng=None):
            if eng is None:
                eng = "act" if evac_flip[0] % 2 == 0 else "dve"
                evac_flip[0] += 1
            if eng == "act":
                S.add("act", lambda e: e.activation(out, in_, AF.Copy), reads, writes)
            else:
                S.add(eng, lambda e: e.tensor_copy(out, in_), reads, writes)

        def dma(q, out, in_, reads, writes):
            S.add(q, lambda e: e.dma_start(out=out, in_=in_), reads, writes, kind="d")

        def allgather(src, dst, reads, writes):
            import os
            if os.environ.get("NO_CC"):
                S.add("pool", lambda e: e.dma_start(out=dst.ap()[0:src.shape[0], :], in_=src.ap()), reads, writes, kind="d")
                return
            S.add("pool", lambda e: e.collective_compute(
                "AllGather", ALU.bypass, replica_groups=PAIRS, ins=[src.ap().opt()], outs=[dst.ap().opt()]),
                reads, writes, kind="cc")

        xT_v = xT_d.rearrange("(c p) t -> p c t", p=128)
        for c in range(KC):
            dma("sp", xres[:, c, :], xT_v[:, c, :], [], [f"x{c}"])
        dma("sp", pvec[:], pvec_d[:, :], [], ["pvec"])
        dma("sp", cst[:], cst_d[:, :], [], ["cst"])
        cp(negtri_b, cst[:, CS_NEGTRI:CS_NEGTRI + 128], ["cst"], ["cb16"], eng="dve")
        S.add("dve", lambda e: e.memset(negones_b, -1.0), [], ["cb16"])
        S.add("dve", lambda e: e.memset(ones_b, 1.0), [], ["cb16"])
        S.add("dve", lambda e: e.memset(small[:, 62:63], -0.5), [], ["smc"])
        cp(cb16[:, 384:896], cst[:, CS_AMASK:CS_AMASK + 512], ["cst"], ["cb16"], eng="dve")
        wsp_stage = carve(0, [128, L * 4 * 128], F32)
        dma("sp", wsp_stage, wspT_d[:, :], [], ["wsp_stage"])
        tt(wmT[:].rearrange("p (a t) -> p a t", t=128), wsp_stage.rearrange("p (a t) -> p a t", t=128),
           cst[:, CS_SGUM:CS_SGUM + 128].unsqueeze(1).broadcast_to([128, L * 4, 128]), ALU.mult,
           ["wsp_stage", "cst"], ["wmT"])
        S.barrier()

        XR = [f"x{c}" for c in range(KC)]

        def rms_norm_tile(src3, ncols, gbase, hT_out3, hres, sq, rstd, lnv, psb, srcres, out_f32=False):
            act(sq, src3, AF.Square, srcres, ["sqA", "sqB"])
            for c in range(KC):
                mm(PSB(psb)[:, 0:ncols], ones_b, sq[:, c, :], c == 0, c == KC - 1, ["sqA", "sqB", "cb16"], [f"ps{psb}"])
            act(lnv, PSB(psb)[:, 0:ncols], AF.Ln, [f"ps{psb}"], ["lnv"], scale=1.0 / D, bias=1e-6)
            act(rstd, lnv, AF.Exp, ["lnv"], ["rstd"], scale=-0.5)
            for c in range(KC):
                stt(hT_out3[:, c, :], src3[:, c, :], pvec[:, gbase + c: gbase + c + 1], rstd, ALU.mult, ALU.mult,
                    srcres + ["rstd", "pvec"], [hres])

        def load_w(q, wb, wres, src_ap):
            dma(q, wb, src_ap, [], [wres])

        for l in layers:
            w_in_v = w_in_d[l].rearrange("(k p) c -> p k c", p=128)
            kt_src, kt_all, v_src, v_all, tl_src, tl_all = [scratch[l][k_] for k_ in
                                                            ("kt_src", "kt_all", "v_src", "v_all", "tl_src", "tl_all")]

            hT = carve(0, [128, KC, T], BF16)
            qT = carve(32768, [128, 4, T], BF16)
            wbuf = [carve(49152 + i * 8192, [128, KC, 512], BF16) for i in range(2)]
            sq = carve(65536, [128, KC, 512], BF16)
            rstd = carve(73728, [128, 512], F32)
            lnv = carve(75776, [128, 512], F32)
            kst = [carve(77824 + i * 4096, [128, T], BF16) for i in range(2)]
            vst = [carve(86016 + i * 4096, [128, 4, 512], BF16) for i in range(2)]
            tl_sb = carve(94208, [128, 128], BF16)
            cc_sb = carve(94464, [128, 128], F32)
            wcnt = [0]

            def next_w(col0, ncol=512, src=None):
                i = wcnt[0] % 2
                wcnt[0] += 1
                srcv = w_in_v if src is None else src
                load_w("pool", wbuf[i][:, :, 0:ncol], f"wbuf{i}", srcv[:, :, col0:col0 + ncol])
                return wbuf[i], f"wbuf{i}"

            wb5 = [wbuf[0], wbuf[1]] + [carve(95232 + i * 8192, [128, KC, 512], BF16) for i in range(3)]
            rb5 = ["wbuf0", "wbuf1", "wb5_2", "wb5_3", "wb5_4"]
            for i_, c0_ in enumerate((3072, 3584, 512, 1024, 0)):
                load_w("pool", wb5[i_], rb5[i_], w_in_v[:, :, c0_:c0_ + 512])
            for tt_ in range(4):
                sl = slice(tt_ * 512, (tt_ + 1) * 512)
                rms_norm_tile(xres[:, :, sl], 512, PV_MIXG + l * 8, hT[:, :, sl], f"hT{tt_}", sq, rstd, lnv, 7, XR)
            HT = [f"hT{i}" for i in range(4)]

            wcc, rcc = wb5[0], rb5[0]
            wcx, rcx = wb5[1], rb5[1]
            wk_, rk_ = wb5[2], rb5[2]
            kt_dst = kt_src.ap().rearrange("(m p) (c k) -> p m c k", p=128, k=128)
            pb = 0
            for tt_ in range(4):
                sl = slice(tt_ * 512, (tt_ + 1) * 512)
                ks = kst[tt_ % 2].rearrange("p (c t) -> p c t", t=512)
                for c in range(4):
                    for k in range(KC):
                        mm(PSB(pb), wk_[:, k, c * 128:(c + 1) * 128], hT[:, k, sl], k == 0, k == KC - 1,
                           [f"hT{tt_}", rk_], [f"ps{pb}"])
                    cp(ks[:, c, :], PSB(pb), [f"ps{pb}"], [f"kst{tt_ % 2}"])
                    pb = (pb + 1) % 4
                for c in range(4):
                    dma("sp", kt_dst[:, tt_ * 4:(tt_ + 1) * 4, c, :], ks[:, c, :].rearrange("p (m k) -> p m k", k=128),
                        [f"kst{tt_ % 2}"], ["kt_src"])
            allgather(kt_src, kt_all, ["kt_src"], ["kt_all"])

            wv_, rv_ = wb5[3], rb5[3]
            v_dst = v_src.ap().rearrange("(m p) c -> p m c", p=128)
            for m in range(NB):
                vs = vst[(m // 4) % 2]
                for k in range(KC):
                    mm(PSB(pb), hT[:, k, m * 128:(m + 1) * 128], wv_[:, k, :], k == 0, k == KC - 1,
                       [f"hT{m // 4}", rv_], [f"ps{pb}"])
                cp(vs[:, m % 4, :], PSB(pb), [f"ps{pb}"], [f"vst{(m // 4) % 2}"])
                pb = (pb + 1) % 4
                if m % 4 == 3:
                    dma("sp", v_dst[:, m - 3:m + 1, :], vs, [f"vst{(m // 4) % 2}"], ["v_src"])
            allgather(v_src, v_all, ["v_src"], ["v_all"])

            hT_tail = [hT[:, k, :].rearrange("p (m t) -> p m t", t=128)[:, :, 126:128] for k in range(KC)]
            for (wb_, rb_, half) in ((wcc, rcc, 0), (wcx, rcx, 1)):
                for c in range(4):
                    for k in range(KC):
                        mm(PSB(6)[:, half * 128 + c * 32: half * 128 + (c + 1) * 32], wb_[:, k, c * 128:(c + 1) * 128],
                           hT_tail[k], k == 0, k == KC - 1, HT + [rb_], ["ps6"])
            cp(cc_sb, PSB(6)[:, 0:128], ["ps6"], ["cc_sb"], eng="act")
            tt(tl_sb, cc_sb, PSB(6)[:, 128:256], ALU.mult, ["cc_sb", "ps6"], ["tl_sb"])
            dma("sp", tl_src[:, :], tl_sb, ["tl_sb"], ["tl_src"])
            allgather(tl_src, tl_all, ["tl_src"], ["tl_all"])

            wq_, rq_ = wb5[4], rb5[4]
            for tt_ in range(4):
                sl = slice(tt_ * 512, (tt_ + 1) * 512)
                for c in range(4):
                    for k in range(KC):
                        mm(PSB(pb), wq_[:, k, c * 128:(c + 1) * 128], hT[:, k, sl], k == 0, k == KC - 1,
                           [f"hT{tt_}", rq_], [f"ps{pb}"])
                    if (c * 4 + tt_) % 2 == 0:
                        act(qT[:, c, sl], PSB(pb), AF.Copy, [f"ps{pb}"], ["qT"], scale=0.125)
                    else:
                        ts(qT[:, c, sl], PSB(pb), 0.125, None, ALU.mult, None, [f"ps{pb}"], ["qT"])
                    pb = (pb + 1) % 4
            S.barrier()

            if stop_after == f"qkv{l}":
                break
            KTr = carve(49152, [128, 2 * NB, 512], BF16)
            Vr = carve(81920, [128, 2 * NB, 512], BF16)
            yaT = carve(0, [128, 4, T], BF16)
            Ebuf = [carve(16384, [128, 1024], F32), carve(118784, [128, 1024], F32)]
            Lsp = [carve(20480 + i * 2048, [128, 1024], BF16) for i in range(3)]
            Abuf = [carve(26624 + i * 2048, [128, 1024], BF16) for i in range(2)]
            Rb = [carve(114688 + i * 2048, [128, 1024], BF16) for i in range(2)]
            yatmp = carve(30720, [128, 4, 128], BF16)
            kt_v = kt_all.ap().rearrange("(b p) x -> p b x", p=128)
            v_v = v_all.ap().rearrange("(b p) x -> p b x", p=128)
            for i in range(4):
                dma("sp", KTr[:, i * 8:(i + 1) * 8, :], kt_v[:, i * 8:(i + 1) * 8, :], ["kt_all"], ["KT"])
            for i in range(4):
                dma("sp", Vr[:, i * 8:(i + 1) * 8, :], v_v[:, i * 8:(i + 1) * 8, :], ["v_all"], ["V"])

            steps = []
            for m in range(NB):
                G = 4 * (m // 2) + (1 if m % 2 == 0 else 3)
                for si, g in enumerate(range(G, -1, -1)):
                    r, ml = gblock(g)
                    steps.append(dict(m=m, si=si, g=g, kb=r * NB + ml, last=(g == 0), par=m % 2))
            NS = len(steps)
            Zb = lambda s: psum[:, (s % 2) * 2:(s % 2) * 2 + 2, :].rearrange("p a b -> p (a b)")
            Zres = lambda s: f"Z{s % 2}"
            Cb = psum[:, 4:6, :].rearrange("p a b -> p (a b)")
            Ob = psum[0:64, 6:8, :].rearrange("p a b -> p (a b)")

            def qk(out1024, st, start_flag, wres):
                m, kb = st["m"], st["kb"]
                for c in range(4):
                    for hh in range(2):
                        hp = hh * 4 + c
                        mm(out1024[:, hp * 128:(hp + 1) * 128],
                           KTr[hh * 64:(hh + 1) * 64, kb, c * 128:(c + 1) * 128],
                           qT[hh * 64:(hh + 1) * 64, c, m * 128:(m + 1) * 128],
                           start_flag, (True if start_flag else c == 3), ["KT", "qT"], [wres])

            def front(s):
                st = steps[s]
                qk(Zb(s), st, True, Zres(s))

            def mid0(s):
                act(Ebuf[s % 2], Zb(s), AF.Exp, [Zres(s)], [f"E{s % 2}"])

            def mid(s):
                st = steps[s]
                si = st["si"]
                lb = Lsp[s % 3]
                act(lb, Ebuf[s % 2], AF.Ln, [f"E{s % 2}"], [f"L{s % 3}"], bias=1.0)
                if si < 2:
                    mk = amask_b[st["par"]][1 - si]
                    tt(lb.rearrange("p (h t) -> p h t", t=128), lb.rearrange("p (h t) -> p h t", t=128),
                       mk.unsqueeze(1).broadcast_to([128, 8, 128]), ALU.mult, [f"L{s % 3}", "cb16"], [f"L{s % 3}"])
                if si == 0:
                    st["carry"] = None
                elif si == 1:
                    st["carry"] = (Lsp[(s - 1) % 3], f"L{(s - 1) % 3}")
                else:
                    prev = steps[s - 1]["carry"]
                    rb = Rb[s % 2]
                    tt(rb, prev[0], Lsp[(s - 1) % 3], ALU.add, [prev[1], f"L{(s - 1) % 3}"], [f"R{s % 2}"])
                    st["carry"] = (rb, f"R{s % 2}")

            def back1(s):
                st = steps[s]
                lb = Lsp[s % 3]
                for hb in range(2):
                    cs = slice(hb * 512, (hb + 1) * 512)
                    mm(Cb[:, cs], negtri_b, lb[:, cs], True, False, [f"L{s % 3}", "cb16"], ["C"])
                    if st["carry"] is not None:
                        mm(Cb[:, cs], negones_b, st["carry"][0][:, cs], False, False, [st["carry"][1], "cb16"], ["C"])
                qk(Cb, st, False, "C")

            def back2(s):
                st = steps[s]
                ab = Abuf[s % 2]
                act(ab, Cb, AF.Exp, ["C"], [f"A{s % 2}"])
                if st["si"] < 2:
                    mk = amask_b[st["par"]][1 - st["si"]]
                    tt(ab.rearrange("p (h t) -> p h t", t=128), ab.rearrange("p (h t) -> p h t", t=128),
                       mk.unsqueeze(1).broadcast_to([128, 8, 128]), ALU.mult, [f"A{s % 2}", "cb16"], [f"A{s % 2}"])

            def back3(s):
                st = steps[s]
                ab = Abuf[s % 2]
                m, kb = st["m"], st["kb"]
                for hp in range(8):
                    hh, c = divmod(hp, 4)
                    h = 2 * c + hh
                    mm(Ob[:, hp * 128:(hp + 1) * 128], Vr[:, kb, h * 64:(h + 1) * 64], ab[:, hp * 128:(hp + 1) * 128],
                       st["si"] == 0 and c == 0, st["last"] and c == 3, [f"A{s % 2}", "V"], ["O"])
                if st["last"]:
                    ov = Ob.rearrange("p (hh c t) -> p hh c t", hh=2, c=4)
                    S.add("dve", lambda e: e.tensor_copy(yaT[0:64, :, m * 128:(m + 1) * 128], ov[:, 0, :, :]),
                          ["O"], ["yaT"])
                    S.add("dve", lambda e: e.tensor_copy(yatmp[0:64], ov[:, 1, :, :]), ["O"], ["yatmp"])
                    dma("sp", yaT[64:128, :, m * 128:(m + 1) * 128], yatmp[0:64], ["yatmp"], ["yaT"])

            ok = lambda i: 0 <= i < NS
            for s in range(-2, NS + 2):
                if ok(s - 1):
                    back1(s - 1)
                if ok(s - 2):
                    back3(s - 2)
                if ok(s + 2):
                    front(s + 2)
                if ok(s + 1):
                    mid0(s + 1)
                if ok(s):
                    mid(s)
                if ok(s - 1):
                    back2(s - 1)
            S.barrier()

            if stop_after == f"attn{l}":
                break

            def emit_wout(t0_):
                w_out_v = w_out_d[l].rearrange("(k p) c -> p k c", p=128)
                pb_ = 0
                for cg in range(2):
                    wo_, ro_ = next_w(cg * 512, src=w_out_v)
                    for oc4 in range(4):
                        oc = cg * 4 + oc4
                        for tt_ in range(2):
                            sl = slice(tt_ * 512, (tt_ + 1) * 512)
                            gsl = slice(t0_ + tt_ * 512, t0_ + (tt_ + 1) * 512)
                            for k in range(KC):
                                mm(PSB(pb_), wo_[:, k, oc4 * 128:(oc4 + 1) * 128], mrgT[:, k, sl], k == 0, k == KC - 1,
                                   ["mrgT", ro_], [f"ps{pb_}"])
                            tt(xres[:, oc, gsl], xres[:, oc, gsl], PSB(pb_), ALU.add, [f"x{oc}", f"ps{pb_}"], [f"x{oc}"])
                            pb_ = (pb_ + 1) % 4

            dma("sp", bcp[:], bc_d[:, l * 1536:(l + 1) * 1536], [], ["bcp"])
            for hf in range(2):
                t0 = hf * 1024
                hTh = carve(16384, [128, KC, 1024], BF16)
                ybT = carve(32768, [128, 4, 1024], BF16)
                ycT = carve(40960, [128, 4, 1024], BF16)
                wbuf = [carve(49152 + i * 8192, [128, KC, 512], BF16) for i in range(2)]
                sq = carve(65536, [128, KC, 512], BF16)
                rstd = carve(73728, [128, 512], F32)
                lnv = carve(75776, [128, 512], F32)
                mrgT = carve(77824, [128, KC, 1024], BF16)
                yext = carve(94208, [128, 4, 8, 130], BF16)
                cbT = carve(102528, [128, 4, 1024], BF16)
                gv4 = carve(110720, [128, 4, 512], F32)
                vn4 = carve(118912, [128, 4, 512], BF16)
                tls = carve(123008, [128, 2, 4, NB, 2], BF16)
                gv, tmpf, ctmp = gv4[:, 0, :], gv4[:, 1, :], gv4[:, 2, :]
                sg = [carve(118912, [128, 512], F32)]
                tfb = [(rstd, "rstd"), (lnv, "lnv")]
                wcnt = [0]
                for tt_ in range(2):
                    sl = slice(tt_ * 512, (tt_ + 1) * 512)
                    gsl = slice(t0 + tt_ * 512, t0 + (tt_ + 1) * 512)
                    rms_norm_tile(xres[:, :, gsl], 512, PV_MIXG + l * 8, hTh[:, :, sl], f"hTh{tt_}", sq, rstd, lnv, 7, XR)
                HTH = ["hTh0", "hTh1"]
                if hf == 1:
                    emit_wout(0)
                pb = 0
                wz, rz = next_w(1536)
                for c in range(4):
                    for tt_ in range(2):
                        sl = slice(tt_ * 512, (tt_ + 1) * 512)
                        for k in range(KC):
                            mm(PSB(pb), wz[:, k, c * 128:(c + 1) * 128], hTh[:, k, sl], k == 0, k == KC - 1,
                               [f"hTh{tt_}", rz], [f"ps{pb}"])
                        act(ybT[:, c, sl], PSB(pb), AF.Gelu, [f"ps{pb}"], ["ybT"])
                        pb = (pb + 1) % 4
                if hf == 0:
                    dma("sp", tls.rearrange("p r c m t -> p r (c m t)"),
                        tl_all.ap().rearrange("(r p) x -> p r x", p=128), ["tl_all"], ["tls"])
                wcb, rcb = next_w(2560)
                for c in range(4):
                    for tt_ in range(2):
                        sl = slice(tt_ * 512, (tt_ + 1) * 512)
                        for k in range(KC):
                            mm(PSB(pb), wcb[:, k, c * 128:(c + 1) * 128], hTh[:, k, sl], k == 0, k == KC - 1,
                               [f"hTh{tt_}", rcb], [f"ps{pb}"])
                        cp(cbT[:, c, sl], PSB(pb), [f"ps{pb}"], ["cbT"])
                        pb = (pb + 1) % 4
                wcc, rcc = next_w(3072)
                wcx, rcx = next_w(3584)
                for c in range(4):
                    for tt_ in range(2):
                        sl = slice(tt_ * 512, (tt_ + 1) * 512)
                        p1 = pb
                        p2 = (pb + 1) % 4
                        pb = (pb + 2) % 4
                        for k in range(KC):
                            mm(PSB(p1), wcc[:, k, c * 128:(c + 1) * 128], hTh[:, k, sl], k == 0, k == KC - 1,
                               [f"hTh{tt_}", rcc], [f"ps{p1}"])
                        for k in range(KC):
                            mm(PSB(p2), wcx[:, k, c * 128:(c + 1) * 128], hTh[:, k, sl], k == 0, k == KC - 1,
                               [f"hTh{tt_}", rcx], [f"ps{p2}"])
                        cp(ctmp, PSB(p1), [f"ps{p1}"], ["gv4_2"], eng="act")
                        tt(yext[:, c, tt_ * 4:(tt_ + 1) * 4, 2:130], ctmp.rearrange("p (m t) -> p m t", t=128),
                           PSB(p2).rearrange("p (m t) -> p m t", t=128), ALU.mult, ["gv4_2", f"ps{p2}"], ["yext"])
                wz2, rz2 = next_w(2048)
                bwf = carve(65536, [128, 4, D], BF16)
                sgb = carve(123520, [128, 512], BF16)
                brs = [(yaT, "yaT", t0), (ybT, "ybT", 0), (ycT, "ycT", 0)]
                gcnt = [0]
                w_br_v = w_br_d[l].rearrange("n (k p) d -> p n k d", p=128)

                def load_pass_w(n_, hc):
                    dma("pool", bwf[:, :, hc * 512:(hc + 1) * 512], w_br_v[:, n_, :, hc * 512:(hc + 1) * 512],
                        [], ["sqA" if hc == 0 else "sqB"])

                def load_gate_grp(n_, cg_, wi_):
                    c0_ = 4096 + n_ * 1024 + cg_ * 512
                    load_w("pool", wbuf[wi_], f"wbuf{wi_}", w_in_v[:, :, c0_:c0_ + 512])

                def merge_pass(n_, cg_, wi_, extra=None):
                    src, sres, o0 = brs[n_]
                    it_ = 0
                    for dc4 in range(4):
                        dc = cg_ * 4 + dc4
                        for tt_ in range(2):
                            sl = slice(tt_ * 512, (tt_ + 1) * 512)
                            pg = (gcnt[0] % 4) * 2
                            pbq = pg + 1
                            gcnt[0] += 1
                            for k in range(KC):
                                mm(PSB(pg), wbuf[wi_][:, k, dc4 * 128:(dc4 + 1) * 128], hTh[:, k, sl], k == 0, k == KC - 1,
                                   [f"hTh{tt_}", f"wbuf{wi_}"], [f"ps{pg}"])
                            for k in range(4):
                                mm(PSB(pbq), bwf[:, k, dc * 128:(dc + 1) * 128], src[:, k, o0 + tt_ * 512: o0 + (tt_ + 1) * 512],
                                   k == 0, k == 3, [sres, "sqA" if cg_ == 0 else "sqB"], [f"ps{pbq}"])
                            act(sgb, PSB(pg), AF.Sigmoid, [f"ps{pg}"], ["sgb"])
                            if n_ == 0:
                                tt(mrgT[:, dc, sl], sgb, PSB(pbq), ALU.mult, ["sgb", f"ps{pbq}"], ["mrgT"])
                            else:
                                tt(tmpf, sgb, PSB(pbq), ALU.mult, ["sgb", f"ps{pbq}"], ["gv4_1"])
                                tt(mrgT[:, dc, sl], mrgT[:, dc, sl], tmpf, ALU.add, ["mrgT", "gv4_1"], ["mrgT"])
                            it_ += 1
                            if extra and it_ % 2 == 0:
                                extra.pop(0)()

                load_pass_w(0, 0)
                load_gate_grp(0, 0, 1)
                load_pass_w(0, 1)
                smc = lambda o, n=4: small[:, o:o + n]
                for grp in range(2):
                    for b_ in range(4):
                        mb = grp * 4 + b_
                        bs = slice(mb * 128, (mb + 1) * 128)
                        for k in range(KC):
                            mm(PSB(b_), hTh[:, k, bs], wz2[:, k, :], k == 0, k == KC - 1, [f"hTh{grp}", rz2], [f"ps{b_}"])
                        act(gv4[:, b_, :], PSB(b_), AF.Gelu, [f"ps{b_}"], [f"gv4_{b_}", "sm_sum"], accum=small[:, b_:b_ + 1])
                    for b_ in range(4):
                        act(vn4[:, b_, :], gv4[:, b_, :], AF.Square, [f"gv4_{b_}"], ["vn4", "sm_sq"], accum=small[:, 4 + b_:5 + b_])
                    ts(smc(8), smc(0), -1.0 / 512, None, ALU.mult, None, ["sm_sum"], ["sm_nm"])
                    tt(smc(12), smc(8), smc(8), ALU.mult, ["sm_nm"], ["sm_m2"])
                    ts(smc(16), smc(4), 1.0 / 512, 1e-5, ALU.mult, ALU.add, ["sm_sq"], ["sm_ve"])
                    tt(smc(20), smc(16), smc(12), ALU.subtract, ["sm_ve", "sm_m2"], ["sm_var"])
                    S.add("pool", lambda e: e.tensor_tensor(smc(24), smc(20), small[:, 62:63].broadcast_to([128, 4]), ALU.pow),
                          ["sm_var", "smc"], ["sm_rstd"])
                    for b_ in range(4):
                        ts(gv4[:, b_, :], gv4[:, b_, :], small[:, 8 + b_:9 + b_], small[:, 24 + b_:25 + b_], ALU.add, ALU.mult,
                           [f"gv4_{b_}", "sm_nm", "sm_rstd"], [f"gv4_{b_}"])
                    G4 = [f"gv4_{i}" for i in range(4)]
                    tt(gv4, gv4, bcp[:, BC_LNG: BC_LNG + 512].unsqueeze(1).broadcast_to([128, 4, 512]), ALU.mult,
                       G4 + ["bcp"], G4)
                    tt(vn4, gv4, bcp[:, BC_LNB: BC_LNB + 512].unsqueeze(1).broadcast_to([128, 4, 512]), ALU.add,
                       G4 + ["bcp"], ["vn4"])
                    merge_pass(0, grp, 1)
                    if grp == 0:
                        load_gate_grp(0, 1, 1)
                    for b_ in range(4):
                        for gi in range(4):
                            mm(PSB(4 + b_)[:, gi * 128:(gi + 1) * 128], vn4[:, b_, gi * 128:(gi + 1) * 128],
                               wmT[:, (l * 4 + gi) * 128:(l * 4 + gi + 1) * 128], True, True, ["vn4", "wmT"], [f"ps{4 + b_}"])
                    for b_ in range(4):
                        mb = grp * 4 + b_
                        bs = slice(mb * 128, (mb + 1) * 128)
                        tf_, T_ = tfb[b_ % 2]
                        tt(tf_, PSB(4 + b_), bcp[:, BC_BSP: BC_BSP + 512], ALU.add, [f"ps{4 + b_}", "bcp"], [T_])
                        tt(ybT[:, :, bs], tf_.rearrange("p (g t) -> p g t", t=128), ybT[:, :, bs], ALU.mult,
                           [T_, "ybT"], ["ybT"])
                selA = pvec[:, PV_SEL:PV_SEL + 1]
                selB = pvec[:, PV_SEL + 1:PV_SEL + 2]
                halo_t = small[:, 28:60].bitcast(BF16).rearrange("p (c j t) -> p c j t", c=4, t=2)
                jb = hf * 4
                for par in range(2):
                    if par == 0:
                        candB = tls[:, 0, :, 2 * jb:2 * jb + 8:2, :]
                        if hf == 0:
                            jlo = 1
                            candA = tls[:, 0, :, 1:6:2, :]
                        else:
                            jlo = 0
                            candA = tls[:, 0, :, 2 * jb - 1:2 * jb + 6:2, :]
                    else:
                        jlo = 0
                        candA = tls[:, 1, :, 2 * jb + 1:2 * jb + 8:2, :]
                        candB = tls[:, 1, :, 2 * jb:2 * jb + 8:2, :]
                    for c in range(4):
                        dst = yext[:, c, par:8:2, 0:2]
                        ht = halo_t[:, c, 0:4, :]
                        ts(ht, candB[:, c], selB, None, ALU.mult, None, ["tls", "pvec"], ["halo_t"])
                        if jlo == 1:
                            cp(dst[:, 0:1, :], ht[:, 0:1, :], ["halo_t"], ["yext"], eng="dve")
                            stt(dst[:, 1:4, :], candA[:, c], selA, ht[:, 1:4, :], ALU.mult, ALU.add,
                                ["tls", "pvec", "halo_t"], ["yext"])
                        else:
                            stt(dst, candA[:, c], selA, ht, ALU.mult, ALU.add, ["tls", "pvec", "halo_t"], ["yext"])
                def taps_job(c, tt_):
                    def job():
                        sl = slice(tt_ * 512, (tt_ + 1) * 512)
                        ye = yext[:, c, tt_ * 4:(tt_ + 1) * 4, :]
                        acc = ctmp.rearrange("p (m t) -> p m t", t=128)
                        wcol = lambda k: pvec[:, PV_CONV + (l * 3 + k) * 4 + c: PV_CONV + (l * 3 + k) * 4 + c + 1]
                        ts(acc, ye[:, :, 0:128], wcol(0), None, ALU.mult, None, ["yext", "pvec"], ["gv4_2"])
                        stt(acc, ye[:, :, 1:129], wcol(1), acc, ALU.mult, ALU.add, ["yext", "pvec", "gv4_2"], ["gv4_2"])
                        stt(acc, ye[:, :, 2:130], wcol(2), acc, ALU.mult, ALU.add, ["yext", "pvec", "gv4_2"], ["gv4_2"])
                        tt(ycT[:, c, sl], ctmp, cbT[:, c, sl], ALU.mult, ["gv4_2", "cbT"], ["ycT"])
                    return job

                jobs = [taps_job(c, tt_) for c in range(4) for tt_ in range(2)]
                load_pass_w(1, 0)
                load_gate_grp(1, 0, 0)
                load_pass_w(1, 1)
                load_gate_grp(1, 1, 1)
                merge_pass(1, 0, 0, extra=jobs)
                load_pass_w(2, 0)
                load_gate_grp(2, 0, 0)
                merge_pass(1, 1, 1, extra=jobs)
                while jobs:
                    jobs.pop(0)()
                load_pass_w(2, 1)
                load_gate_grp(2, 1, 1)
                merge_pass(2, 0, 0)
                merge_pass(2, 1, 1)
                if hf == 1:
                    emit_wout(1024)

            S.barrier()
            if stop_after == f"mix{l}":
                break

            hT = carve(0, [128, KC, T], BF16)
            qx = carve(32768, [128, KC, T], BF16)
            wbuf = [carve(65536 + i * 8192, [128, KC, 512], BF16) for i in range(2)]
            sq = carve(81920, [128, KC, 512], BF16)
            rstd = carve(90112, [128, 512], F32)
            lnv = carve(92160, [128, 512], F32)
            memf = carve(94208, [128, KC, 256], F32)
            mTn = carve(102400, [128, KC, 256], BF16)
            kTx = carve(106496, [128, KC, 256], BF16)
            vx = carve(110592, [128, 2, D], BF16)
            pT = [carve(114688, [128, 2, 512], BF16), carve(118784, [128, 2, 512], BF16)]
            recb = [carve(116736, [128, 512], F32), carve(120832, [128, 512], F32)]
            wcnt = [0]
            wk_v = wk_d[l].rearrange("(k p) c -> p k c", p=128)
            wv_v = wv_d[l].rearrange("(k p) c -> p k c", p=128)
            wq_v = wq_d[l].rearrange("(k p) c -> p k c", p=128)
            wo_v = wo_d[l].rearrange("(k p) c -> p k c", p=128)
            dma("sp", memf, memT_d.rearrange("(c p) t -> p c t", p=128), [], ["memf"])
            rms_norm_tile(memf, 256, PV_MEMG + l * 8, mTn, "mTn", sq[:, :, 0:256], rstd[:, 0:256], lnv[:, 0:256], 6, ["memf"])
            pbx = [0]

            def xnorm(tt_):
                sl = slice(tt_ * 512, (tt_ + 1) * 512)
                rms_norm_tile(xres[:, :, sl], 512, PV_XAG + l * 8, hT[:, :, sl], f"hT{tt_}", sq, rstd, lnv, 7, XR)

            def memk(cg):
                w_, r_ = next_w(cg * 512, src=wk_v)
                for dc4 in range(4):
                    dc = cg * 4 + dc4
                    pb = pbx[0]
                    for k in range(KC):
                        mm(PSB(pb)[:, 0:256], w_[:, k, dc4 * 128:(dc4 + 1) * 128], mTn[:, k, :], k == 0, k == KC - 1,
                           ["mTn", r_], [f"ps{pb}"])
                    cp(kTx[:, dc, :], PSB(pb)[:, 0:256], [f"ps{pb}"], ["kTx"])
                    pbx[0] = (pb + 1) % 4

            def memv(cg):
                w_, r_ = next_w(cg * 512, src=wv_v)
                for kc in range(2):
                    pb = pbx[0]
                    for k in range(KC):
                        mm(PSB(pb), mTn[:, k, kc * 128:(kc + 1) * 128], w_[:, k, :], k == 0, k == KC - 1,
                           ["mTn", r_], [f"ps{pb}"])
                    cp(vx[:, kc, cg * 512:(cg + 1) * 512], PSB(pb), [f"ps{pb}"], ["vx"])
                    pbx[0] = (pb + 1) % 4

            xnorm(0)
            memk(0)
            xnorm(1)
            memk(1)
            xnorm(2)
            memv(0)
            xnorm(3)
            memv(1)
            pb = pbx[0]
            for cg in range(2):
                w_, r_ = next_w(cg * 512, src=wq_v)
                for tt_ in range(4):
                    sl = slice(tt_ * 512, (tt_ + 1) * 512)
                    for dc4 in range(4):
                        dc = cg * 4 + dc4
                        for k in range(KC):
                            mm(PSB(pb), w_[:, k, dc4 * 128:(dc4 + 1) * 128], hT[:, k, sl], k == 0, k == KC - 1,
                               [f"hT{tt_}", r_], [f"ps{pb}"])
                        if (dc * 4 + tt_) % 2 == 0:
                            act(qx[:, dc, sl], PSB(pb), AF.Copy, [f"ps{pb}"], ["qx"], scale=0.0625)
                        else:
                            ts(qx[:, dc, sl], PSB(pb), 0.0625, None, ALU.mult, None, [f"ps{pb}"], ["qx"])
                        pb = (pb + 1) % 4
            oT = hT
            it = 0
            for tt_ in range(4):
                sl = slice(tt_ * 512, (tt_ + 1) * 512)
                for hx in range(4):
                    q_ = it % 2
                    it += 1
                    pT_, rec_ = pT[q_], recb[q_]
                    P_, R_ = f"pT{q_}", f"rec{q_}"
                    bd = 2 + q_
                    bo = 4 + 2 * q_
                    for kc in range(2):
                        for dd in range(2):
                            mm(PSB(kc), kTx[:, hx * 2 + dd, kc * 128:(kc + 1) * 128], qx[:, hx * 2 + dd, sl],
                               dd == 0, dd == 1, ["kTx", "qx"], [f"ps{kc}"])
                    act(pT_, psum[:, 0:2, :], AF.Exp, ["ps0", "ps1"], [P_])
                    for kc in range(2):
                        mm(PSB(bd), ones_b, pT_[:, kc, :], kc == 0, kc == 1, [P_, "cb16"], [f"ps{bd}"])
                    for dd in range(2):
                        for kc in range(2):
                            mm(PSB(bo + dd), vx[:, kc, hx * 256 + dd * 128: hx * 256 + (dd + 1) * 128], pT_[:, kc, :],
                               kc == 0, kc == 1, [P_, "vx"], [f"ps{bo + dd}"])
                    act(rec_, PSB(bd), AF.Ln, [f"ps{bd}"], [R_])
                    act(rec_, rec_, AF.Exp, [R_], [R_], scale=-1.0)
                    for dd in range(2):
                        tt(oT[:, hx * 2 + dd, sl], PSB(bo + dd), rec_, ALU.mult, [f"ps{bo + dd}", R_], [f"hT{tt_}"])
            pb = 0
            for cg in range(2):
                w_, r_ = next_w(cg * 512, src=wo_v)
                for oc4 in range(4):
                    oc = cg * 4 + oc4
                    for tt_ in range(4):
                        sl = slice(tt_ * 512, (tt_ + 1) * 512)
                        for k in range(KC):
                            mm(PSB(pb), w_[:, k, oc4 * 128:(oc4 + 1) * 128], oT[:, k, sl], k == 0, k == KC - 1,
                               [f"hT{tt_}", r_], [f"ps{pb}"])
                        tt(xres[:, oc, sl], xres[:, oc, sl], PSB(pb), ALU.add, [f"x{oc}", f"ps{pb}"], [f"x{oc}"])
                        pb = (pb + 1) % 4
            S.barrier()
            if stop_after == f"xa{l}":
                break

            wg_v = wg_d[l].rearrange("(k p) c -> p k c", p=128)
            wu_v = wu_d[l].rearrange("(k p) c -> p k c", p=128)
            wd_v = wd_d[l].rearrange("(k p) c -> p k c", p=128)
            for hf in range(2):
                t0 = hf * 1024
                hTh = carve(0, [128, KC, 1024], BF16)
                h1T = carve(16384, [128, NJ, 1024], BF16)
                wgb = [carve(61440 + i * 8192, [128, KC, 512], BF16) for i in range(2)]
                wub = [carve(77824 + i * 8192, [128, KC, 512], BF16) for i in range(2)]
                sq = carve(94208, [128, KC, 512], BF16)
                rstd = carve(102400, [128, 512], F32)
                lnv = carve(104448, [128, 512], F32)
                sgf = carve(106496, [128, 512], F32)
                tf = carve(108544, [128, 512], F32)
                def ffn_norm(t0_):
                    for tt_ in range(2):
                        sl = slice(tt_ * 512, (tt_ + 1) * 512)
                        gsl = slice(t0_ + tt_ * 512, t0_ + (tt_ + 1) * 512)
                        rms_norm_tile(xres[:, :, gsl], 512, PV_FFNG + l * 8, hTh[:, :, sl], f"hTh{tt_}", sq, rstd, lnv, 7, XR)

                if hf == 0:
                    ffn_norm(0)
                ngrp = 6
                pcount = 0
                for gi in range(ngrp):
                    c0 = gi * 512
                    ncol = min(512, FH - c0)
                    i = gi % 2
                    dma("pool", wgb[i][:, :, 0:ncol], wg_v[:, :, c0:c0 + ncol], [], [f"wgb{i}"])
                    dma("pool", wub[i][:, :, 0:ncol], wu_v[:, :, c0:c0 + ncol], [], [f"wub{i}"])
                    for jj in range(ncol // 128):
                        j = gi * 4 + jj
                        for tt_ in range(2):
                            sl = slice(tt_ * 512, (tt_ + 1) * 512)
                            pg = (pcount % 4) * 2
                            pu = pg + 1
                            pcount += 1
                            for k in range(KC):
                                mm(PSB(pg), wgb[i][:, k, jj * 128:(jj + 1) * 128], hTh[:, k, sl], k == 0, k == KC - 1,
                                   [f"hTh{tt_}", f"wgb{i}"], [f"ps{pg}"])
                            for k in range(KC):
                                mm(PSB(pu), wub[i][:, k, jj * 128:(jj + 1) * 128], hTh[:, k, sl], k == 0, k == KC - 1,
                                   [f"hTh{tt_}", f"wub{i}"], [f"ps{pu}"])
                            act(sgf, PSB(pg), AF.Sigmoid, [f"ps{pg}"], ["sgf"])
                            tt(tf, sgf, PSB(pg), ALU.mult, ["sgf", f"ps{pg}"], ["tf"])
                            tt(h1T[:, j, sl], tf, PSB(pu), ALU.mult, ["tf", f"ps{pu}"], ["h1T"])
                if hf == 0:
                    ffn_norm(1024)
                wdb = [carve(110592, [128, NJ, 256], BF16), carve(61440, [128, NJ, 256], BF16)]
                wdn = [["wdb0"], ["wgb0", "wgb1"]]
                pb = 0
                for cg in range(4):
                    i = cg % 2
                    dma("pool", wdb[i], wd_v[:, :, cg * 256:(cg + 1) * 256], [], wdn[i])
                    for oc2 in range(2):
                        oc = cg * 2 + oc2
                        for tt_ in range(2):
                            sl = slice(tt_ * 512, (tt_ + 1) * 512)
                            gsl = slice(t0 + tt_ * 512, t0 + (tt_ + 1) * 512)
                            for j in range(NJ):
                                mm(PSB(pb), wdb[i][:, j, oc2 * 128:(oc2 + 1) * 128], h1T[:, j, sl], j == 0, j == NJ - 1,
                                   ["h1T"] + wdn[i], [f"ps{pb}"])
                            tt(xres[:, oc, gsl], xres[:, oc, gsl], PSB(pb), ALU.add, [f"x{oc}", f"ps{pb}"], [f"x{oc}"])
                            pb = (pb + 1) % 4
            S.barrier()
            if stop_after == f"ffn{l}":
                break

        yT_v = yT_d.rearrange("(c p) t -> p c t", p=128)
        if stop_after is None and final:
            sq = carve(0, [128, KC, 512], BF16)
            rstd = carve(8192, [128, 512], F32)
            lnv = carve(10240, [128, 512], F32)
            outb = [carve(16384 + i * 16384, [128, KC, 512], F32) for i in range(2)]
            for tt_ in range(4):
                sl = slice(tt_ * 512, (tt_ + 1) * 512)
                ob = outb[tt_ % 2]
                rms_norm_tile(xres[:, :, sl], 512, PV_FING, ob, f"outb{tt_ % 2}", sq, rstd, lnv, 7, XR)
                dma("sp", yT_v[:, :, sl], ob, [f"outb{tt_ % 2}"], ["yT"])
        else:
            for c in range(KC):
                dma("sp", yT_v[:, c, :], xres[:, c, :], [f"x{c}"], ["yT"])
        fw = S.final_wait("sp")

        @block.tensor
        def _(e):
            S.emit("pe", e)

        @block.scalar
        def _(e):
            S.emit("act", e)

        @block.vector
        def _(e):
            S.emit("dve", e)

        @block.gpsimd
        def _(e):
            S.emit("pool", e)

        @block.sync
        def _(e):
            S.emit("sp", e)
            for s, v in fw:
                e.wait_ge(sems[s], v)

    return nc


def bass_gate_ap(w_in_d, l, dc):
    v = w_in_d[l].rearrange("(k p) c -> p k c", p=128)[:, :, 4096:7168]
    return v.rearrange("p k (n c) -> p k n c", n=3)[:, :, :, dc * 128:(dc + 1) * 128]


_CACHE = {}


def _owned_blocks(role):
    out = []
    for j in range(8):
        out += ([4 * j, 4 * j + 3] if role == 0 else [4 * j + 1, 4 * j + 2])
    return out


def _prep_inputs(inp):
    f = np.float32
    x = np.asarray(inp["x"], f)
    mem = np.asarray(inp["mem"], f)
    gv = lambda k: np.asarray(inp[k], f)

    def pcols(a):
        a = a.reshape(-1, 8, 128)
        return np.ascontiguousarray(a.transpose(2, 0, 1).reshape(128, -1))

    pvec_base = np.zeros((128, NPV), f)
    pvec_base[:, PV_MIXG:PV_MIXG + 16] = pcols(gv("norm_mix_g"))
    pvec_base[:, PV_XAG:PV_XAG + 16] = pcols(gv("norm_xa_g"))
    pvec_base[:, PV_MEMG:PV_MEMG + 16] = pcols(gv("mem_norm_g"))
    pvec_base[:, PV_FFNG:PV_FFNG + 16] = pcols(gv("norm_ffn_g"))
    pvec_base[:, PV_FING:PV_FING + 8] = pcols(gv("final_g")[None])
    cw = gv("conv_w").reshape(L * 3, 4, 128)
    pvec_base[:, PV_CONV:PV_CONV + 24] = cw.transpose(2, 0, 1).reshape(128, 24)

    bc = np.zeros((128, NBC), f)
    for l_ in range(L):
        o = l_ * 1536
        bc[:, o:o + 512] = np.broadcast_to(gv("sgu_ln_g")[l_].reshape(1, -1), (128, 512))
        bc[:, o + 512:o + 1024] = np.broadcast_to(gv("sgu_ln_b")[l_].reshape(1, -1), (128, 512))
        bc[:, o + 1024:o + 1536] = np.broadcast_to(gv("b_spatial")[l_].reshape(1, -1), (128, 512))
    wspT = np.ascontiguousarray(gv("w_spatial").transpose(3, 0, 1, 2).reshape(128, L * 4 * 128))

    j = np.arange(128)
    negtri = -(j[:, None] >= j[None, :]).astype(f)
    sgum = ((j[None, :] // 64) >= (j[:, None] // 64)).astype(f)
    diag = (j[:, None] < j[None, :]).astype(f)
    full = np.ones((128, 128), f)
    none = np.zeros((128, 128), f)
    amasks = {0: [[diag, none], [full, diag]], 1: [[full, diag], [diag, none]]}

    shared = {k: np.ascontiguousarray(gv(k)) for k in
              ["w_in", "w_branch", "w_out", "w_q_xa", "w_k_xa", "w_v_xa", "w_o_xa",
               "w_gate_ffn", "w_up_ffn", "w_down_ffn"]}
    in_maps = []
    for core in range(8):
        b, role = divmod(core, 2)
        blocks = _owned_blocks(role)
        xb = x[b].reshape(32, 128, D)[blocks].reshape(T, D)
        cst = np.zeros((128, NCST), f)
        cst[:, CS_NEGTRI:CS_NEGTRI + 128] = negtri
        cst[:, CS_SGUM:CS_SGUM + 128] = sgum
        for par in range(2):
            for w in range(2):
                o = CS_AMASK + (par * 2 + w) * 128
                cst[:, o:o + 128] = amasks[role][par][w]
        pv = pvec_base.copy()
        pv[:, PV_SEL] = 1.0 if role == 0 else 0.0
        pv[:, PV_SEL + 1] = 0.0 if role == 0 else 1.0
        m = dict(shared)
        m.update({"xT": np.ascontiguousarray(xb.T), "memT": np.ascontiguousarray(mem[b].T),
                  "pvec": pv, "cst": cst, "bc": bc, "wspT": wspT})
        in_maps.append(m)
    return in_maps


def _assemble(results):
    out = np.zeros((4, 4096, D), np.float32)
    for core in range(8):
        b, role = divmod(core, 2)
        blocks = _owned_blocks(role)
        y = np.asarray(results[core]["yT"]).T.reshape(NB, 128, D)
        out[b].reshape(32, 128, D)[blocks] = y
    return out


FUSED = True


def kernel(**inputs):
    in_maps = _prep_inputs(inputs)
    if FUSED:
        if "nc" not in _CACHE:
            _CACHE["nc"] = build_program()
        res = run_bass_kernel_spmd(_CACHE["nc"], in_maps, core_ids=list(range(8)))
        return _assemble(res.results)
    if "nc0" not in _CACHE:
        _CACHE["nc0"] = build_program(layers=(0,), final=False)
        _CACHE["nc1"] = build_program(layers=(1,), final=True)
    res0 = run_bass_kernel_spmd(_CACHE["nc0"], in_maps, core_ids=list(range(8)))
    for c in range(8):
        in_maps[c]["xT"] = np.ascontiguousarray(np.asarray(res0.results[c]["yT"], np.float32))
    res1 = run_bass_kernel_spmd(_CACHE["nc1"], in_maps, core_ids=list(range(8)))
    return _assemble(res1.results)
```

```python
import numpy as np
import concourse.bass as bass
import concourse.mybir as mybir
from concourse.bass_utils import run_bass_kernel_spmd

F32 = mybir.dt.float32
BF16 = mybir.dt.bfloat16
AF = mybir.ActivationFunctionType
ALU = mybir.AluOpType

L = 2
D = 1024
T = 2048
NB = 16
KC = 8
FH = 2816
NJ = 22
INC = 7168
PAIRS = [[0, 1], [2, 3], [4, 5], [6, 7]]

PV_MIXG = 0
PV_XAG = 16
PV_MEMG = 32
PV_FFNG = 48
PV_FING = 64
PV_CONV = 72
PV_SEL = 96
NPV = 98
CS_NEGTRI = 0
CS_SGUM = 128
CS_AMASK = 256
NCST = 768
BC_LNG = 0
BC_LNB = 512
BC_BSP = 1024
NBC = 3072


def gblock(g):
    j, i = divmod(g, 4)
    return [(0, 2 * j), (1, 2 * j), (1, 2 * j + 1), (0, 2 * j + 1)][i]


class Sched:
    def __init__(self, nc, sems):
        self.nc = nc
        self.sems = sems
        self.engs = ["pe", "act", "dve", "pool", "sp"]
        self.ops = {e: [] for e in self.engs}
        self.count = {e: 0 for e in self.engs}
        self.waited = {e: {} for e in self.engs}
        self.res = {}
        self.dq = {"sp": [f"D_sp{i}" for i in range(8)], "pool": [f"D_pl{i}" for i in range(8)]}
        self.dnext = {"sp": 0, "pool": 0}
        self.dcnt = {}
        self.pending_barrier = {e: {} for e in self.engs}
        self.cc_n = 0

    def _r(self, k):
        if k not in self.res:
            self.res[k] = {"w": None, "r": {}}
        return self.res[k]

    def add(self, eng, fn, reads=(), writes=(), kind="c"):
        deps = {}

        def need(tok):
            if tok is None:
                return
            s, v = tok
            if deps.get(s, 0) < v:
                deps[s] = v

        for k in reads:
            need(self._r(k)["w"])
        for k in writes:
            r = self._r(k)
            need(r["w"])
            for s, v in r["r"].items():
                need((s, v))
        for s, v in self.pending_barrier[eng].items():
            need((s, v))
        self.pending_barrier[eng] = {}
        if kind == "d":
            q = self.dq[eng]
            dsem = q[self.dnext[eng] % len(q)]
            self.dnext[eng] += 1
            prev = self.dcnt.get(dsem, 0)
            if prev:
                need((dsem, prev))
            self.dcnt[dsem] = prev + 16
            tok = (dsem, prev + 16)
            inc = (dsem, 16)
        elif kind == "cc":
            name = f"CC{self.cc_n}"
            self.cc_n += 1
            tok = (name, 1)
            inc = (name, None)
        else:
            self.count[eng] += 1
            tok = (f"S_{eng}", self.count[eng])
            inc = (f"S_{eng}", 1)
        waits = []
        for s, v in deps.items():
            if eng == "pe" and s == "S_pe":
                continue
            if self.waited[eng].get(s, 0) >= v:
                continue
            self.waited[eng][s] = v
            waits.append((s, v))
        self.ops[eng].append((waits, fn, inc))
        for k in reads:
            r = self._r(k)
            if r["r"].get(tok[0], 0) < tok[1]:
                r["r"][tok[0]] = tok[1]
        for k in writes:
            r = self._r(k)
            r["w"] = tok
            r["r"] = {}
        return tok

    def barrier(self):
        snap = {}
        for e in ["pe", "act", "dve", "pool"]:
            if self.count[e]:
                snap[f"S_{e}"] = self.count[e]
        for s, v in self.dcnt.items():
            snap[s] = v
        for i in range(self.cc_n):
            snap[f"CC{i}"] = 1
        for e in self.engs:
            self.pending_barrier[e] = dict(snap)

    def emit(self, eng, handle):
        for waits, fn, inc in self.ops[eng]:
            for s, v in waits:
                handle.wait_ge(self.sems[s], v)
            ins = fn(handle)
            if inc[1] is None:
                ins.then_inc(self.sems[inc[0]])
            else:
                ins.then_inc(self.sems[inc[0]], inc[1])

    def final_wait(self, eng):
        self.barrier()
        deps = self.pending_barrier[eng]
        waits = [(s, v) for s, v in deps.items() if self.waited[eng].get(s, 0) < v]
        return waits


def build_program(stop_after=None, layers=(0, 1), final=True):
    nc = bass.Bass("TRN2", target_bir_lowering=False)

    def din(name, shape):
        return nc.dram_tensor(name, list(shape), F32, kind="ExternalInput").ap()

    xT_d = din("xT", [D, T])
    memT_d = din("memT", [D, 256])
    pvec_d = din("pvec", [128, NPV])
    cst_d = din("cst", [128, NCST])
    bc_d = din("bc", [128, NBC])
    wspT_d = din("wspT", [128, L * 4 * 128])
    w_in_d = din("w_in", [L, D, INC])
    w_br_d = din("w_branch", [L, 3, 512, D])
    w_out_d = din("w_out", [L, D, D])
    wq_d = din("w_q_xa", [L, D, D])
    wk_d = din("w_k_xa", [L, D, D])
    wv_d = din("w_v_xa", [L, D, D])
    wo_d = din("w_o_xa", [L, D, D])
    wg_d = din("w_gate_ffn", [L, D, FH])
    wu_d = din("w_up_ffn", [L, D, FH])
    wd_d = din("w_down_ffn", [L, FH, D])
    yT_d = nc.dram_tensor("yT", [D, T], F32, kind="ExternalOutput").ap()

    scratch = {}
    for l_ in range(L):
        scratch[l_] = dict(
            kt_src=nc.dram_tensor(f"kt_src{l_}", [NB * 128, 512], BF16),
            kt_all=nc.dram_tensor(f"kt_all{l_}", [2 * NB * 128, 512], BF16),
            v_src=nc.dram_tensor(f"v_src{l_}", [NB * 128, 512], BF16),
            v_all=nc.dram_tensor(f"v_all{l_}", [2 * NB * 128, 512], BF16),
            tl_src=nc.dram_tensor(f"tl_src{l_}", [128, 128], BF16),
            tl_all=nc.dram_tensor(f"tl_all{l_}", [256, 128], BF16))

    sem_names = ["S_pe", "S_act", "S_dve", "S_pool"] + [f"D_sp{i}" for i in range(8)] + \
                [f"D_pl{i}" for i in range(8)] + [f"CC{i}" for i in range(3 * L)]

    import contextlib
    with contextlib.ExitStack() as es:
        xres_t = es.enter_context(nc.sbuf_tensor("xres", [128, KC, T], F32))
        ARENA_B = 122 * 1024
        arena = es.enter_context(nc.sbuf_tensor("arena", [128, ARENA_B // 2], BF16))
        pvec = es.enter_context(nc.sbuf_tensor("pvec_sb", [128, NPV], F32))
        cst = es.enter_context(nc.sbuf_tensor("cst_sb", [128, NCST], F32))
        bcp = es.enter_context(nc.sbuf_tensor("bcp", [128, 1536], F32))
        cb16 = es.enter_context(nc.sbuf_tensor("cb16", [128, 1280], BF16))
        wmT = es.enter_context(nc.sbuf_tensor("wmT_sb", [128, L * 4 * 128], BF16))
        small = es.enter_context(nc.sbuf_tensor("small", [128, 64], F32))
        psum = es.enter_context(nc.psum_tensor("ps", [128, 8, 512], F32))
        sems = {n: es.enter_context(nc.semaphore(n)) for n in sem_names}
        block = es.enter_context(nc.Block())

        S = Sched(nc, sems)
        xres = xres_t

        negtri_b = cb16[:, 0:128]
        negones_b = cb16[:, 128:256]
        ones_b = cb16[:, 256:384]
        amask_b = [[cb16[:, 384 + (p * 2 + w) * 128: 384 + (p * 2 + w + 1) * 128] for w in range(2)] for p in range(2)]
        sgum_b = cb16[:, 896:1024]

        def carve(off, shape, dt):
            es_ = 2 if dt == BF16 else 4
            n = int(np.prod(shape[1:]))
            assert off % 4 == 0 and off + n * es_ <= ARENA_B, (off, shape)
            ap = arena[:, off // 2: off // 2 + n * es_ // 2]
            if dt == F32:
                ap = ap.bitcast(F32)
            if len(shape) == 3:
                ap = ap.rearrange("p (a b) -> p a b", b=shape[2])
            elif len(shape) == 4:
                ap = ap.rearrange("p (a b c) -> p a b c", b=shape[2], c=shape[3])
            elif len(shape) == 5:
                ap = ap.rearrange("p (a b c d) -> p a b c d", b=shape[2], c=shape[3], d=shape[4])
            return ap

        PSB = lambda b: psum[:, b, :]
        evac_flip = [0]

        def mm(out, lhsT, rhs, start, stop, reads, writes):
            S.add("pe", lambda e: e.matmul(out, lhsT, rhs, start=start, stop=stop), reads, writes)

        def act(out, in_, func, reads, writes, scale=1.0, bias=0.0, accum=None):
            kw = {"scale": scale}
            if bias != 0.0:
                kw["bias"] = bias
            if accum is not None:
                kw["accum_out"] = accum
            S.add("act", lambda e: e.activation(out, in_, func, **kw), reads, writes)

        def tt(out, in0, in1, op, reads, writes, eng="dve"):
            S.add(eng, lambda e: e.tensor_tensor(out, in0, in1, op), reads, writes)

        def ts(out, in0, s1, s2, op0, op1, reads, writes, eng="dve"):
            if op1 is None:
                S.add(eng, lambda e: e.tensor_scalar(out, in0, s1, None, op0), reads, writes)
            else:
                S.add(eng, lambda e: e.tensor_scalar(out, in0, s1, s2, op0, op1), reads, writes)

        def stt(out, in0, scalar, in1, op0, op1, reads, writes):
            S.add("dve", lambda e: e.scalar_tensor_tensor(out, in0, scalar, in1, op0, op1), reads, writes)

        def cp(out, in_, reads, writes, eng=None):
            if eng is None:
                eng = "act" if evac_flip[0] % 2 == 0 else "dve"
                evac_flip[0] += 1
            if eng == "act":
                S.add("act", lambda e: e.activation(out, in_, AF.Copy), reads, writes)
            else:
                S.add(eng, lambda e: e.tensor_copy(out, in_), reads, writes)

        def dma(q, out, in_, reads, writes):
            S.add(q, lambda e: e.dma_start(out=out, in_=in_), reads, writes, kind="d")

        def allgather(src, dst, reads, writes):
            import os
            if os.environ.get("NO_CC"):
                S.add("pool", lambda e: e.dma_start(out=dst.ap()[0:src.shape[0], :], in_=src.ap()), reads, writes, kind="d")
                return
            S.add("pool", lambda e: e.collective_compute(
                "AllGather", ALU.bypass, replica_groups=PAIRS, ins=[src.ap().opt()], outs=[dst.ap().opt()]),
                reads, writes, kind="cc")

        xT_v = xT_d.rearrange("(c p) t -> p c t", p=128)
        for c in range(KC):
            dma("sp", xres[:, c, :], xT_v[:, c, :], [], [f"x{c}"])
        dma("sp", pvec[:], pvec_d[:, :], [], ["pvec"])
        dma("sp", cst[:], cst_d[:, :], [], ["cst"])
        cp(negtri_b, cst[:, CS_NEGTRI:CS_NEGTRI + 128], ["cst"], ["cb16"], eng="dve")
        S.add("dve", lambda e: e.memset(negones_b, -1.0), [], ["cb16"])
        S.add("dve", lambda e: e.memset(ones_b, 1.0), [], ["cb16"])
        S.add("dve", lambda e: e.memset(small[:, 62:63], -0.5), [], ["smc"])
        cp(cb16[:, 384:896], cst[:, CS_AMASK:CS_AMASK + 512], ["cst"], ["cb16"], eng="dve")
        wsp_stage = carve(120832, [128, L * 4 * 128], F32)
        dma("sp", wsp_stage, wspT_d[:, :], [], ["wsp_stage"])
        tt(wmT[:].rearrange("p (a t) -> p a t", t=128), wsp_stage.rearrange("p (a t) -> p a t", t=128),
           cst[:, CS_SGUM:CS_SGUM + 128].unsqueeze(1).broadcast_to([128, L * 4, 128]), ALU.mult,
           ["wsp_stage", "cst"], ["wmT"])

        XR = [f"x{c}" for c in range(KC)]

        def rms_norm_tile(src3, ncols, gbase, hT_out3, hres, sq, rstd, lnv, psb, srcres, out_f32=False):
            act(sq, src3, AF.Square, srcres, ["sqA", "sqB"])
            for c in range(KC):
                mm(PSB(psb)[:, 0:ncols], ones_b, sq[:, c, :], c == 0, c == KC - 1, ["sqA", "sqB", "cb16"], [f"ps{psb}"])
            act(lnv, PSB(psb)[:, 0:ncols], AF.Ln, [f"ps{psb}"], ["lnv"], scale=1.0 / D, bias=1e-6)
            act(rstd, lnv, AF.Exp, ["lnv"], ["rstd"], scale=-0.5)
            for c in range(KC):
                stt(hT_out3[:, c, :], src3[:, c, :], pvec[:, gbase + c: gbase + c + 1], rstd, ALU.mult, ALU.mult,
                    srcres + ["rstd", "pvec"], [hres])

        def load_w(q, wb, wres, src_ap):
            dma(q, wb, src_ap, [], [wres])

        for l in layers:
            w_in_v = w_in_d[l].rearrange("(k p) c -> p k c", p=128)
            kt_src, kt_all, v_src, v_all, tl_src, tl_all = [scratch[l][k_] for k_ in
                                                            ("kt_src", "kt_all", "v_src", "v_all", "tl_src", "tl_all")]

            hT = carve(0, [128, KC, T], BF16)
            qT = carve(32768, [128, 4, T], BF16)
            wbuf = [carve(49152 + i * 8192, [128, KC, 512], BF16) for i in range(2)]
            sq = carve(65536, [128, KC, 512], BF16)
            rstd = carve(73728, [128, 512], F32)
            lnv = carve(75776, [128, 512], F32)
            kst = [carve(77824 + i * 4096, [128, T], BF16) for i in range(2)]
            vst = [carve(86016 + i * 4096, [128, 4, 512], BF16) for i in range(2)]
            tl_sb = carve(94208, [128, 128], BF16)
            cc_sb = carve(94464, [128, 128], F32)
            wcnt = [0]

            def next_w(col0, ncol=512, src=None):
                i = wcnt[0] % 2
                wcnt[0] += 1
                srcv = w_in_v if src is None else src
                load_w("pool", wbuf[i][:, :, 0:ncol], f"wbuf{i}", srcv[:, :, col0:col0 + ncol])
                return wbuf[i], f"wbuf{i}"

            wb5 = [wbuf[0], wbuf[1]] + [carve(95232 + i * 8192, [128, KC, 512], BF16) for i in range(3)]
            rb5 = ["wbuf0", "wbuf1", "wb5_2", "wb5_3", "wb5_4"]
            for i_, c0_ in enumerate((3072, 3584, 512, 1024, 0)):
                load_w("pool", wb5[i_], rb5[i_], w_in_v[:, :, c0_:c0_ + 512])
            for tt_ in range(4):
                sl = slice(tt_ * 512, (tt_ + 1) * 512)
                rms_norm_tile(xres[:, :, sl], 512, PV_MIXG + l * 8, hT[:, :, sl], f"hT{tt_}", sq, rstd, lnv, 7, XR)
            HT = [f"hT{i}" for i in range(4)]

            wcc, rcc = wb5[0], rb5[0]
            wcx, rcx = wb5[1], rb5[1]
            wk_, rk_ = wb5[2], rb5[2]
            kt_dst = kt_src.ap().rearrange("(m p) (c k) -> p m c k", p=128, k=128)
            pb = 0
            for tt_ in range(4):
                sl = slice(tt_ * 512, (tt_ + 1) * 512)
                ks = kst[tt_ % 2].rearrange("p (c t) -> p c t", t=512)
                for c in range(4):
                    for k in range(KC):
                        mm(PSB(pb), wk_[:, k, c * 128:(c + 1) * 128], hT[:, k, sl], k == 0, k == KC - 1,
                           [f"hT{tt_}", rk_], [f"ps{pb}"])
                    cp(ks[:, c, :], PSB(pb), [f"ps{pb}"], [f"kst{tt_ % 2}"])
                    pb = (pb + 1) % 4
                for c in range(4):
                    dma("sp", kt_dst[:, tt_ * 4:(tt_ + 1) * 4, c, :], ks[:, c, :].rearrange("p (m k) -> p m k", k=128),
                        [f"kst{tt_ % 2}"], ["kt_src"])
            allgather(kt_src, kt_all, ["kt_src"], ["kt_all"])

            wv_, rv_ = wb5[3], rb5[3]
            v_dst = v_src.ap().rearrange("(m p) c -> p m c", p=128)
            for m in range(NB):
                vs = vst[(m // 4) % 2]
                for k in range(KC):
                    mm(PSB(pb), hT[:, k, m * 128:(m + 1) * 128], wv_[:, k, :], k == 0, k == KC - 1,
                       [f"hT{m // 4}", rv_], [f"ps{pb}"])
                cp(vs[:, m % 4, :], PSB(pb), [f"ps{pb}"], [f"vst{(m // 4) % 2}"])
                pb = (pb + 1) % 4
                if m % 4 == 3:
                    dma("sp", v_dst[:, m - 3:m + 1, :], vs, [f"vst{(m // 4) % 2}"], ["v_src"])
            allgather(v_src, v_all, ["v_src"], ["v_all"])

            hT_tail = [hT[:, k, :].rearrange("p (m t) -> p m t", t=128)[:, :, 126:128] for k in range(KC)]
            for (wb_, rb_, half) in ((wcc, rcc, 0), (wcx, rcx, 1)):
                for c in range(4):
                    for k in range(KC):
                        mm(PSB(6)[:, half * 128 + c * 32: half * 128 + (c + 1) * 32], wb_[:, k, c * 128:(c + 1) * 128],
                           hT_tail[k], k == 0, k == KC - 1, HT + [rb_], ["ps6"])
            cp(cc_sb, PSB(6)[:, 0:128], ["ps6"], ["cc_sb"], eng="act")
            tt(tl_sb, cc_sb, PSB(6)[:, 128:256], ALU.mult, ["cc_sb", "ps6"], ["tl_sb"])
            dma("sp", tl_src[:, :], tl_sb, ["tl_sb"], ["tl_src"])
            allgather(tl_src, tl_all, ["tl_src"], ["tl_all"])

            wq_, rq_ = wb5[4], rb5[4]
            for tt_ in range(4):
                sl = slice(tt_ * 512, (tt_ + 1) * 512)
                for c in range(4):
                    for k in range(KC):
                        mm(PSB(pb), wq_[:, k, c * 128:(c + 1) * 128], hT[:, k, sl], k == 0, k == KC - 1,
                           [f"hT{tt_}", rq_], [f"ps{pb}"])
                    if (c * 4 + tt_) % 2 == 0:
                        act(qT[:, c, sl], PSB(pb), AF.Copy, [f"ps{pb}"], ["qT"], scale=0.125)
                    else:
                        ts(qT[:, c, sl], PSB(pb), 0.125, None, ALU.mult, None, [f"ps{pb}"], ["qT"])
                    pb = (pb + 1) % 4
            S.barrier()

            if stop_after == f"qkv{l}":
                break
            KTr = carve(49152, [128, 2 * NB, 512], BF16)
            Vr = carve(81920, [128, 2 * NB, 512], BF16)
            yaT = carve(0, [128, 4, T], BF16)
            Ebuf = [carve(16384, [128, 1024], F32), carve(118784, [128, 1024], F32)]
            Lsp = [carve(20480 + i * 2048, [128, 1024], BF16) for i in range(3)]
            Abuf = [carve(26624 + i * 2048, [128, 1024], BF16) for i in range(2)]
            Rb = [carve(114688 + i * 2048, [128, 1024], BF16) for i in range(2)]
            yatmp = carve(30720, [128, 4, 128], BF16)
            kt_v = kt_all.ap().rearrange("(b p) x -> p b x", p=128)
            v_v = v_all.ap().rearrange("(b p) x -> p b x", p=128)
            for i in range(4):
                dma("sp", KTr[:, i * 8:(i + 1) * 8, :], kt_v[:, i * 8:(i + 1) * 8, :], ["kt_all"], ["KT"])
            for i in range(4):
                dma("sp", Vr[:, i * 8:(i + 1) * 8, :], v_v[:, i * 8:(i + 1) * 8, :], ["v_all"], ["V"])

            steps = []
            for m in range(NB):
                G = 4 * (m // 2) + (1 if m % 2 == 0 else 3)
                for si, g in enumerate(range(G, -1, -1)):
                    r, ml = gblock(g)
                    steps.append(dict(m=m, si=si, g=g, kb=r * NB + ml, last=(g == 0), par=m % 2))
            NS = len(steps)
            Zb = lambda s: psum[:, (s % 2) * 2:(s % 2) * 2 + 2, :].rearrange("p a b -> p (a b)")
            Zres = lambda s: f"Z{s % 2}"
            Cb = psum[:, 4:6, :].rearrange("p a b -> p (a b)")
            Ob = psum[0:64, 6:8, :].rearrange("p a b -> p (a b)")

            def qk(out1024, st, start_flag, wres):
                m, kb = st["m"], st["kb"]
                for c in range(4):
                    for hh in range(2):
                        hp = hh * 4 + c
                        mm(out1024[:, hp * 128:(hp + 1) * 128],
                           KTr[hh * 64:(hh + 1) * 64, kb, c * 128:(c + 1) * 128],
                           qT[hh * 64:(hh + 1) * 64, c, m * 128:(m + 1) * 128],
                           start_flag, (True if start_flag else c == 3), ["KT", "qT"], [wres])

            def front(s):
                st = steps[s]
                qk(Zb(s), st, True, Zres(s))

            def mid0(s):
                act(Ebuf[s % 2], Zb(s), AF.Exp, [Zres(s)], [f"E{s % 2}"])

            def mid(s):
                st = steps[s]
                si = st["si"]
                lb = Lsp[s % 3]
                act(lb, Ebuf[s % 2], AF.Ln, [f"E{s % 2}"], [f"L{s % 3}"], bias=1.0)
                if si < 2:
                    mk = amask_b[st["par"]][1 - si]
                    tt(lb.rearrange("p (h t) -> p h t", t=128), lb.rearrange("p (h t) -> p h t", t=128),
                       mk.unsqueeze(1).broadcast_to([128, 8, 128]), ALU.mult, [f"L{s % 3}", "cb16"], [f"L{s % 3}"])
                if si == 0:
                    st["carry"] = None
                elif si == 1:
                    st["carry"] = (Lsp[(s - 1) % 3], f"L{(s - 1) % 3}")
                else:
                    prev = steps[s - 1]["carry"]
                    rb = Rb[s % 2]
                    tt(rb, prev[0], Lsp[(s - 1) % 3], ALU.add, [prev[1], f"L{(s - 1) % 3}"], [f"R{s % 2}"])
                    st["carry"] = (rb, f"R{s % 2}")

            def back1(s):
                st = steps[s]
                lb = Lsp[s % 3]
                for hb in range(2):
                    cs = slice(hb * 512, (hb + 1) * 512)
                    mm(Cb[:, cs], negtri_b, lb[:, cs], True, False, [f"L{s % 3}", "cb16"], ["C"])
                    if st["carry"] is not None:
                        mm(Cb[:, cs], negones_b, st["carry"][0][:, cs], False, False, [st["carry"][1], "cb16"], ["C"])
                qk(Cb, st, False, "C")

            def back2(s):
                st = steps[s]
                ab = Abuf[s % 2]
                act(ab, Cb, AF.Exp, ["C"], [f"A{s % 2}"])
                if st["si"] < 2:
                    mk = amask_b[st["par"]][1 - st["si"]]
                    tt(ab.rearrange("p (h t) -> p h t", t=128), ab.rearrange("p (h t) -> p h t", t=128),
                       mk.unsqueeze(1).broadcast_to([128, 8, 128]), ALU.mult, [f"A{s % 2}", "cb16"], [f"A{s % 2}"])

            def back3(s):
                st = steps[s]
                ab = Abuf[s % 2]
                m, kb = st["m"], st["kb"]
                for hp in range(8):
                    hh, c = divmod(hp, 4)
                    h = 2 * c + hh
                    mm(Ob[:, hp * 128:(hp + 1) * 128], Vr[:, kb, h * 64:(h + 1) * 64], ab[:, hp * 128:(hp + 1) * 128],
                       st["si"] == 0 and c == 0, st["last"] and c == 3, [f"A{s % 2}", "V"], ["O"])
                if st["last"]:
                    ov = Ob.rearrange("p (hh c t) -> p hh c t", hh=2, c=4)
                    S.add("dve", lambda e: e.tensor_copy(yaT[0:64, :, m * 128:(m + 1) * 128], ov[:, 0, :, :]),
                          ["O"], ["yaT"])
                    S.add("dve", lambda e: e.tensor_copy(yatmp[0:64], ov[:, 1, :, :]), ["O"], ["yatmp"])
                    dma("sp", yaT[64:128, :, m * 128:(m + 1) * 128], yatmp[0:64], ["yatmp"], ["yaT"])

            ok = lambda i: 0 <= i < NS
            for s in range(-2, NS + 2):
                if ok(s - 1):
                    back1(s - 1)
                if ok(s - 2):
                    back3(s - 2)
                if ok(s + 2):
                    front(s + 2)
                if ok(s + 1):
                    mid0(s + 1)
                if ok(s):
                    mid(s)
                if ok(s - 1):
                    back2(s - 1)
            S.barrier()

            if stop_after == f"attn{l}":
                break

            def emit_wout(t0_):
                w_out_v = w_out_d[l].rearrange("(k p) c -> p k c", p=128)
                pb_ = 0
                for cg in range(2):
                    wo_, ro_ = next_w(cg * 512, src=w_out_v)
                    for oc4 in range(4):
                        oc = cg * 4 + oc4
                        for tt_ in range(2):
                            sl = slice(tt_ * 512, (tt_ + 1) * 512)
                            gsl = slice(t0_ + tt_ * 512, t0_ + (tt_ + 1) * 512)
                            for k in range(KC):
                                mm(PSB(pb_), wo_[:, k, oc4 * 128:(oc4 + 1) * 128], mrgT[:, k, sl], k == 0, k == KC - 1,
                                   ["mrgT", ro_], [f"ps{pb_}"])
                            tt(xres[:, oc, gsl], xres[:, oc, gsl], PSB(pb_), ALU.add, [f"x{oc}", f"ps{pb_}"], [f"x{oc}"])
                            pb_ = (pb_ + 1) % 4

            dma("sp", bcp[:], bc_d[:, l * 1536:(l + 1) * 1536], [], ["bcp"])
            for hf in range(2):
                t0 = hf * 1024
                hTh = carve(16384, [128, KC, 1024], BF16)
                ybT = carve(32768, [128, 4, 1024], BF16)
                ycT = carve(40960, [128, 4, 1024], BF16)
                wbuf = [carve(49152 + i * 8192, [128, KC, 512], BF16) for i in range(2)]
                sq = carve(65536, [128, KC, 512], BF16)
                rstd = carve(73728, [128, 512], F32)
                lnv = carve(75776, [128, 512], F32)
                mrgT = carve(77824, [128, KC, 1024], BF16)
                yext = carve(94208, [128, 4, 8, 130], BF16)
                cbT = carve(102528, [128, 4, 1024], BF16)
                gv4 = carve(110720, [128, 4, 512], F32)
                vn4 = carve(118912, [128, 4, 512], BF16)
                tls = carve(123008, [128, 2, 4, NB, 2], BF16)
                gv, tmpf, ctmp = gv4[:, 0, :], gv4[:, 1, :], gv4[:, 2, :]
                sg = [carve(118912, [128, 512], F32)]
                tfb = [(rstd, "rstd"), (lnv, "lnv")]
                wcnt = [0]
                for tt_ in range(2):
                    sl = slice(tt_ * 512, (tt_ + 1) * 512)
                    gsl = slice(t0 + tt_ * 512, t0 + (tt_ + 1) * 512)
                    rms_norm_tile(xres[:, :, gsl], 512, PV_MIXG + l * 8, hTh[:, :, sl], f"hTh{tt_}", sq, rstd, lnv, 7, XR)
                HTH = ["hTh0", "hTh1"]
                if hf == 1:
                    emit_wout(0)
                pb = 0
                wz, rz = next_w(1536)
                for c in range(4):
                    for tt_ in range(2):
                        sl = slice(tt_ * 512, (tt_ + 1) * 512)
                        for k in range(KC):
                            mm(PSB(pb), wz[:, k, c * 128:(c + 1) * 128], hTh[:, k, sl], k == 0, k == KC - 1,
                               [f"hTh{tt_}", rz], [f"ps{pb}"])
                        act(ybT[:, c, sl], PSB(pb), AF.Gelu, [f"ps{pb}"], ["ybT"])
                        pb = (pb + 1) % 4
                if hf == 0:
                    dma("sp", tls.rearrange("p r c m t -> p r (c m t)"),
                        tl_all.ap().rearrange("(r p) x -> p r x", p=128), ["tl_all"], ["tls"])
                wcb, rcb = next_w(2560)
                for c in range(4):
                    for tt_ in range(2):
                        sl = slice(tt_ * 512, (tt_ + 1) * 512)
                        for k in range(KC):
                            mm(PSB(pb), wcb[:, k, c * 128:(c + 1) * 128], hTh[:, k, sl], k == 0, k == KC - 1,
                               [f"hTh{tt_}", rcb], [f"ps{pb}"])
                        cp(cbT[:, c, sl], PSB(pb), [f"ps{pb}"], ["cbT"])
                        pb = (pb + 1) % 4
                wcc, rcc = next_w(3072)
                wcx, rcx = next_w(3584)
                for c in range(4):
                    for tt_ in range(2):
                        sl = slice(tt_ * 512, (tt_ + 1) * 512)
                        p1 = pb
                        p2 = (pb + 1) % 4
                        pb = (pb + 2) % 4
                        for k in range(KC):
                            mm(PSB(p1), wcc[:, k, c * 128:(c + 1) * 128], hTh[:, k, sl], k == 0, k == KC - 1,
                               [f"hTh{tt_}", rcc], [f"ps{p1}"])
                        for k in range(KC):
                            mm(PSB(p2), wcx[:, k, c * 128:(c + 1) * 128], hTh[:, k, sl], k == 0, k == KC - 1,
                               [f"hTh{tt_}", rcx], [f"ps{p2}"])
                        cp(ctmp, PSB(p1), [f"ps{p1}"], ["gv4_2"], eng="act")
                        tt(yext[:, c, tt_ * 4:(tt_ + 1) * 4, 2:130], ctmp.rearrange("p (m t) -> p m t", t=128),
                           PSB(p2).rearrange("p (m t) -> p m t", t=128), ALU.mult, ["gv4_2", f"ps{p2}"], ["yext"])
                wz2, rz2 = next_w(2048)
                bwf = carve(65536, [128, 4, D], BF16)
                sgb = carve(123520, [128, 512], BF16)
                brs = [(yaT, "yaT", t0), (ybT, "ybT", 0), (ycT, "ycT", 0)]
                gcnt = [0]
                w_br_v = w_br_d[l].rearrange("n (k p) d -> p n k d", p=128)

                def load_pass_w(n_, hc):
                    dma("pool", bwf[:, :, hc * 512:(hc + 1) * 512], w_br_v[:, n_, :, hc * 512:(hc + 1) * 512],
                        [], ["sqA" if hc == 0 else "sqB"])

                def load_gate_grp(n_, cg_, wi_):
                    c0_ = 4096 + n_ * 1024 + cg_ * 512
                    load_w("pool", wbuf[wi_], f"wbuf{wi_}", w_in_v[:, :, c0_:c0_ + 512])

                def merge_pass(n_, cg_, wi_, extra=None):
                    src, sres, o0 = brs[n_]
                    it_ = 0
                    for dc4 in range(4):
                        dc = cg_ * 4 + dc4
                        for tt_ in range(2):
                            sl = slice(tt_ * 512, (tt_ + 1) * 512)
                            pg = (gcnt[0] % 4) * 2
                            pbq = pg + 1
                            gcnt[0] += 1
                            for k in range(KC):
                                mm(PSB(pg), wbuf[wi_][:, k, dc4 * 128:(dc4 + 1) * 128], hTh[:, k, sl], k == 0, k == KC - 1,
                                   [f"hTh{tt_}", f"wbuf{wi_}"], [f"ps{pg}"])
                            for k in range(4):
                                mm(PSB(pbq), bwf[:, k, dc * 128:(dc + 1) * 128], src[:, k, o0 + tt_ * 512: o0 + (tt_ + 1) * 512],
                                   k == 0, k == 3, [sres, "sqA" if cg_ == 0 else "sqB"], [f"ps{pbq}"])
                            act(sgb, PSB(pg), AF.Sigmoid, [f"ps{pg}"], ["sgb"])
                            if n_ == 0:
                                tt(mrgT[:, dc, sl], sgb, PSB(pbq), ALU.mult, ["sgb", f"ps{pbq}"], ["mrgT"])
                            else:
                                tt(tmpf, sgb, PSB(pbq), ALU.mult, ["sgb", f"ps{pbq}"], ["gv4_1"])
                                tt(mrgT[:, dc, sl], mrgT[:, dc, sl], tmpf, ALU.add, ["mrgT", "gv4_1"], ["mrgT"])
                            it_ += 1
                            if extra and it_ % 2 == 0:
                                extra.pop(0)()

                load_pass_w(0, 0)
                load_gate_grp(0, 0, 1)
                load_pass_w(0, 1)
                smc = lambda o, n=4: small[:, o:o + n]
                for grp in range(2):
                    for b_ in range(4):
                        mb = grp * 4 + b_
                        bs = slice(mb * 128, (mb + 1) * 128)
                        for k in range(KC):
                            mm(PSB(b_), hTh[:, k, bs], wz2[:, k, :], k == 0, k == KC - 1, [f"hTh{grp}", rz2], [f"ps{b_}"])
                        act(gv4[:, b_, :], PSB(b_), AF.Gelu, [f"ps{b_}"], [f"gv4_{b_}", "sm_sum"], accum=small[:, b_:b_ + 1])
                    for b_ in range(4):
                        act(vn4[:, b_, :], gv4[:, b_, :], AF.Square, [f"gv4_{b_}"], ["vn4", "sm_sq"], accum=small[:, 4 + b_:5 + b_])
                    ts(smc(8), smc(0), -1.0 / 512, None, ALU.mult, None, ["sm_sum"], ["sm_nm"])
                    tt(smc(12), smc(8), smc(8), ALU.mult, ["sm_nm"], ["sm_m2"])
                    ts(smc(16), smc(4), 1.0 / 512, 1e-5, ALU.mult, ALU.add, ["sm_sq"], ["sm_ve"])
                    tt(smc(20), smc(16), smc(12), ALU.subtract, ["sm_ve", "sm_m2"], ["sm_var"])
                    S.add("pool", lambda e: e.tensor_tensor(smc(24), smc(20), small[:, 62:63].broadcast_to([128, 4]), ALU.pow),
                          ["sm_var", "smc"], ["sm_rstd"])
                    for b_ in range(4):
                        ts(gv4[:, b_, :], gv4[:, b_, :], small[:, 8 + b_:9 + b_], small[:, 24 + b_:25 + b_], ALU.add, ALU.mult,
                           [f"gv4_{b_}", "sm_nm", "sm_rstd"], [f"gv4_{b_}"])
                    G4 = [f"gv4_{i}" for i in range(4)]
                    tt(gv4, gv4, bcp[:, BC_LNG: BC_LNG + 512].unsqueeze(1).broadcast_to([128, 4, 512]), ALU.mult,
                       G4 + ["bcp"], G4)
                    tt(vn4, gv4, bcp[:, BC_LNB: BC_LNB + 512].unsqueeze(1).broadcast_to([128, 4, 512]), ALU.add,
                       G4 + ["bcp"], ["vn4"])
                    merge_pass(0, grp, 1)
                    if grp == 0:
                        load_gate_grp(0, 1, 1)
                    for b_ in range(4):
                        for gi in range(4):
                            mm(PSB(4 + b_)[:, gi * 128:(gi + 1) * 128], vn4[:, b_, gi * 128:(gi + 1) * 128],
                               wmT[:, (l * 4 + gi) * 128:(l * 4 + gi + 1) * 128], True, True, ["vn4", "wmT"], [f"ps{4 + b_}"])
                    for b_ in range(4):
                        mb = grp * 4 + b_
                        bs = slice(mb * 128, (mb + 1) * 128)
                        tf_, T_ = tfb[b_ % 2]
                        tt(tf_, PSB(4 + b_), bcp[:, BC_BSP: BC_BSP + 512], ALU.add, [f"ps{4 + b_}", "bcp"], [T_])
                        tt(ybT[:, :, bs], tf_.rearrange("p (g t) -> p g t", t=128), ybT[:, :, bs], ALU.mult,
                           [T_, "ybT"], ["ybT"])
                selA = pvec[:, PV_SEL:PV_SEL + 1]
                selB = pvec[:, PV_SEL + 1:PV_SEL + 2]
                halo_t = small[:, 28:60].bitcast(BF16).rearrange("p (c j t) -> p c j t", c=4, t=2)
                jb = hf * 4
                for par in range(2):
                    if par == 0:
                        candB = tls[:, 0, :, 2 * jb:2 * jb + 8:2, :]
                        if hf == 0:
                            jlo = 1
                            candA = tls[:, 0, :, 1:6:2, :]
                        else:
                            jlo = 0
                            candA = tls[:, 0, :, 2 * jb - 1:2 * jb + 6:2, :]
                    else:
                        jlo = 0
                        candA = tls[:, 1, :, 2 * jb + 1:2 * jb + 8:2, :]
                        candB = tls[:, 1, :, 2 * jb:2 * jb + 8:2, :]
                    for c in range(4):
                        dst = yext[:, c, par:8:2, 0:2]
                        ht = halo_t[:, c, 0:4, :]
                        ts(ht, candB[:, c], selB, None, ALU.mult, None, ["tls", "pvec"], ["halo_t"])
                        if jlo == 1:
                            cp(dst[:, 0:1, :], ht[:, 0:1, :], ["halo_t"], ["yext"], eng="dve")
                            stt(dst[:, 1:4, :], candA[:, c], selA, ht[:, 1:4, :], ALU.mult, ALU.add,
                                ["tls", "pvec", "halo_t"], ["yext"])
                        else:
                            stt(dst, candA[:, c], selA, ht, ALU.mult, ALU.add, ["tls", "pvec", "halo_t"], ["yext"])
                def taps_job(c, tt_):
                    def job():
                        sl = slice(tt_ * 512, (tt_ + 1) * 512)
                        ye = yext[:, c, tt_ * 4:(tt_ + 1) * 4, :]
                        acc = ctmp.rearrange("p (m t) -> p m t", t=128)
                        wcol = lambda k: pvec[:, PV_CONV + (l * 3 + k) * 4 + c: PV_CONV + (l * 3 + k) * 4 + c + 1]
                        ts(acc, ye[:, :, 0:128], wcol(0), None, ALU.mult, None, ["yext", "pvec"], ["gv4_2"])
                        stt(acc, ye[:, :, 1:129], wcol(1), acc, ALU.mult, ALU.add, ["yext", "pvec", "gv4_2"], ["gv4_2"])
                        stt(acc, ye[:, :, 2:130], wcol(2), acc, ALU.mult, ALU.add, ["yext", "pvec", "gv4_2"], ["gv4_2"])
                        tt(ycT[:, c, sl], ctmp, cbT[:, c, sl], ALU.mult, ["gv4_2", "cbT"], ["ycT"])
                    return job

                jobs = [taps_job(c, tt_) for c in range(4) for tt_ in range(2)]
                load_pass_w(1, 0)
                load_gate_grp(1, 0, 0)
                load_pass_w(1, 1)
                load_gate_grp(1, 1, 1)
                merge_pass(1, 0, 0, extra=jobs)
                load_pass_w(2, 0)
                load_gate_grp(2, 0, 0)
                merge_pass(1, 1, 1, extra=jobs)
                while jobs:
                    jobs.pop(0)()
                load_pass_w(2, 1)
                load_gate_grp(2, 1, 1)
                merge_pass(2, 0, 0)
                merge_pass(2, 1, 1)
                if hf == 1:
                    emit_wout(1024)

            S.barrier()
            if stop_after == f"mix{l}":
                break

            hT = carve(0, [128, KC, T], BF16)
            qx = carve(32768, [128, KC, T], BF16)
            wbuf = [carve(65536 + i * 8192, [128, KC, 512], BF16) for i in range(2)]
            sq = carve(81920, [128, KC, 512], BF16)
            rstd = carve(90112, [128, 512], F32)
            lnv = carve(92160, [128, 512], F32)
            memf = carve(94208, [128, KC, 256], F32)
            mTn = carve(102400, [128, KC, 256], BF16)
            kTx = carve(106496, [128, KC, 256], BF16)
            vx = carve(110592, [128, 2, D], BF16)
            pT = [carve(114688, [128, 2, 512], BF16), carve(118784, [128, 2, 512], BF16)]
            recb = [carve(116736, [128, 512], F32), carve(120832, [128, 512], F32)]
            wcnt = [0]
            wk_v = wk_d[l].rearrange("(k p) c -> p k c", p=128)
            wv_v = wv_d[l].rearrange("(k p) c -> p k c", p=128)
            wq_v = wq_d[l].rearrange("(k p) c -> p k c", p=128)
            wo_v = wo_d[l].rearrange("(k p) c -> p k c", p=128)
            dma("sp", memf, memT_d.rearrange("(c p) t -> p c t", p=128), [], ["memf"])
            rms_norm_tile(memf, 256, PV_MEMG + l * 8, mTn, "mTn", sq[:, :, 0:256], rstd[:, 0:256], lnv[:, 0:256], 6, ["memf"])
            pbx = [0]

            def xnorm(tt_):
                sl = slice(tt_ * 512, (tt_ + 1) * 512)
                rms_norm_tile(xres[:, :, sl], 512, PV_XAG + l * 8, hT[:, :, sl], f"hT{tt_}", sq, rstd, lnv, 7, XR)

            def memk(cg):
                w_, r_ = next_w(cg * 512, src=wk_v)
                for dc4 in range(4):
                    dc = cg * 4 + dc4
                    pb = pbx[0]
                    for k in range(KC):
                        mm(PSB(pb)[:, 0:256], w_[:, k, dc4 * 128:(dc4 + 1) * 128], mTn[:, k, :], k == 0, k == KC - 1,
                           ["mTn", r_], [f"ps{pb}"])
                    cp(kTx[:, dc, :], PSB(pb)[:, 0:256], [f"ps{pb}"], ["kTx"])
                    pbx[0] = (pb + 1) % 4

            def memv(cg):
                w_, r_ = next_w(cg * 512, src=wv_v)
                for kc in range(2):
                    pb = pbx[0]
                    for k in range(KC):
                        mm(PSB(pb), mTn[:, k, kc * 128:(kc + 1) * 128], w_[:, k, :], k == 0, k == KC - 1,
                           ["mTn", r_], [f"ps{pb}"])
                    cp(vx[:, kc, cg * 512:(cg + 1) * 512], PSB(pb), [f"ps{pb}"], ["vx"])
                    pbx[0] = (pb + 1) % 4

            xnorm(0)
            memk(0)
            xnorm(1)
            memk(1)
            xnorm(2)
            memv(0)
            xnorm(3)
            memv(1)
            pb = pbx[0]
            for cg in range(2):
                w_, r_ = next_w(cg * 512, src=wq_v)
                for tt_ in range(4):
                    sl = slice(tt_ * 512, (tt_ + 1) * 512)
                    for dc4 in range(4):
                        dc = cg * 4 + dc4
                        for k in range(KC):
                            mm(PSB(pb), w_[:, k, dc4 * 128:(dc4 + 1) * 128], hT[:, k, sl], k == 0, k == KC - 1,
                               [f"hT{tt_}", r_], [f"ps{pb}"])
                        if (dc * 4 + tt_) % 2 == 0:
                            act(qx[:, dc, sl], PSB(pb), AF.Copy, [f"ps{pb}"], ["qx"], scale=0.0625)
                        else:
                            ts(qx[:, dc, sl], PSB(pb), 0.0625, None, ALU.mult, None, [f"ps{pb}"], ["qx"])
                        pb = (pb + 1) % 4
            oT = hT
            it = 0
            for tt_ in range(4):
                sl = slice(tt_ * 512, (tt_ + 1) * 512)
                for hx in range(4):
                    q_ = it % 2
                    it += 1
                    pT_, rec_ = pT[q_], recb[q_]
                    P_, R_ = f"pT{q_}", f"rec{q_}"
                    bd = 2 + q_
                    bo = 4 + 2 * q_
                    for kc in range(2):
                        for dd in range(2):
                            mm(PSB(kc), kTx[:, hx * 2 + dd, kc * 128:(kc + 1) * 128], qx[:, hx * 2 + dd, sl],
                               dd == 0, dd == 1, ["kTx", "qx"], [f"ps{kc}"])
                    act(pT_, psum[:, 0:2, :], AF.Exp, ["ps0", "ps1"], [P_])
                    for kc in range(2):
                        mm(PSB(bd), ones_b, pT_[:, kc, :], kc == 0, kc == 1, [P_, "cb16"], [f"ps{bd}"])
                    for dd in range(2):
                        for kc in range(2):
                            mm(PSB(bo + dd), vx[:, kc, hx * 256 + dd * 128: hx * 256 + (dd + 1) * 128], pT_[:, kc, :],
                               kc == 0, kc == 1, [P_, "vx"], [f"ps{bo + dd}"])
                    act(rec_, PSB(bd), AF.Ln, [f"ps{bd}"], [R_])
                    act(rec_, rec_, AF.Exp, [R_], [R_], scale=-1.0)
                    for dd in range(2):
                        tt(oT[:, hx * 2 + dd, sl], PSB(bo + dd), rec_, ALU.mult, [f"ps{bo + dd}", R_], [f"hT{tt_}"])
            pb = 0
            for cg in range(2):
                w_, r_ = next_w(cg * 512, src=wo_v)
                for oc4 in range(4):
                    oc = cg * 4 + oc4
                    for tt_ in range(4):
                        sl = slice(tt_ * 512, (tt_ + 1) * 512)
                        for k in range(KC):
                            mm(PSB(pb), w_[:, k, oc4 * 128:(oc4 + 1) * 128], oT[:, k, sl], k == 0, k == KC - 1,
                               [f"hT{tt_}", r_], [f"ps{pb}"])
                        tt(xres[:, oc, sl], xres[:, oc, sl], PSB(pb), ALU.add, [f"x{oc}", f"ps{pb}"], [f"x{oc}"])
                        pb = (pb + 1) % 4
            S.barrier()
            if stop_after == f"xa{l}":
                break

            wg_v = wg_d[l].rearrange("(k p) c -> p k c", p=128)
            wu_v = wu_d[l].rearrange("(k p) c -> p k c", p=128)
            wd_v = wd_d[l].rearrange("(k p) c -> p k c", p=128)
            for hf in range(2):
                t0 = hf * 1024
                hTh = carve(0, [128, KC, 1024], BF16)
                h1T = carve(16384, [128, NJ, 1024], BF16)
                wgb = [carve(61440 + i * 8192, [128, KC, 512], BF16) for i in range(2)]
                wub = [carve(77824 + i * 8192, [128, KC, 512], BF16) for i in range(2)]
                sq = carve(94208, [128, KC, 512], BF16)
                rstd = carve(102400, [128, 512], F32)
                lnv = carve(104448, [128, 512], F32)
                sgf = carve(106496, [128, 512], F32)
                tf = carve(108544, [128, 512], F32)
                def ffn_norm(t0_):
                    for tt_ in range(2):
                        sl = slice(tt_ * 512, (tt_ + 1) * 512)
                        gsl = slice(t0_ + tt_ * 512, t0_ + (tt_ + 1) * 512)
                        rms_norm_tile(xres[:, :, gsl], 512, PV_FFNG + l * 8, hTh[:, :, sl], f"hTh{tt_}", sq, rstd, lnv, 7, XR)

                if hf == 0:
                    ffn_norm(0)
                ngrp = 6
                pcount = 0
                for gi in range(ngrp):
                    c0 = gi * 512
                    ncol = min(512, FH - c0)
                    i = gi % 2
                    dma("pool", wgb[i][:, :, 0:ncol], wg_v[:, :, c0:c0 + ncol], [], [f"wgb{i}"])
                    dma("pool", wub[i][:, :, 0:ncol], wu_v[:, :, c0:c0 + ncol], [], [f"wub{i}"])
                    njj = ncol // 128
                    order = ([(jj, t_) for t_ in range(2) for jj in range(njj)] if (gi == 0 and hf == 0)
                             else [(jj, t_) for jj in range(njj) for t_ in range(2)])
                    for jj, tt_ in order:
                        j = gi * 4 + jj
                        if True:
                            sl = slice(tt_ * 512, (tt_ + 1) * 512)
                            pg = (pcount % 4) * 2
                            pu = pg + 1
                            pcount += 1
                            for k in range(KC):
                                mm(PSB(pg), wgb[i][:, k, jj * 128:(jj + 1) * 128], hTh[:, k, sl], k == 0, k == KC - 1,
                                   [f"hTh{tt_}", f"wgb{i}"], [f"ps{pg}"])
                            for k in range(KC):
                                mm(PSB(pu), wub[i][:, k, jj * 128:(jj + 1) * 128], hTh[:, k, sl], k == 0, k == KC - 1,
                                   [f"hTh{tt_}", f"wub{i}"], [f"ps{pu}"])
                            act(sgf, PSB(pg), AF.Sigmoid, [f"ps{pg}"], ["sgf"])
                            tt(tf, sgf, PSB(pg), ALU.mult, ["sgf", f"ps{pg}"], ["tf"])
                            tt(h1T[:, j, sl], tf, PSB(pu), ALU.mult, ["tf", f"ps{pu}"], ["h1T"])
                if hf == 0:
                    ffn_norm(1024)
                wdb = [carve(110592, [128, NJ, 256], BF16), carve(61440, [128, NJ, 256], BF16)]
                wdn = [["wdb0"], ["wgb0", "wgb1"]]
                pb = 0
                for cg in range(4):
                    i = cg % 2
                    dma("pool", wdb[i], wd_v[:, :, cg * 256:(cg + 1) * 256], [], wdn[i])
                    for oc2 in range(2):
                        oc = cg * 2 + oc2
                        for tt_ in range(2):
                            sl = slice(tt_ * 512, (tt_ + 1) * 512)
                            gsl = slice(t0 + tt_ * 512, t0 + (tt_ + 1) * 512)
                            for j in range(NJ):
                                mm(PSB(pb), wdb[i][:, j, oc2 * 128:(oc2 + 1) * 128], h1T[:, j, sl], j == 0, j == NJ - 1,
                                   ["h1T"] + wdn[i], [f"ps{pb}"])
                            tt(xres[:, oc, gsl], xres[:, oc, gsl], PSB(pb), ALU.add, [f"x{oc}", f"ps{pb}"], [f"x{oc}"])
                            pb = (pb + 1) % 4
            S.barrier()
            if stop_after == f"ffn{l}":
                break

        yT_v = yT_d.rearrange("(c p) t -> p c t", p=128)
        if stop_after is None and final:
            sq = carve(0, [128, KC, 512], BF16)
            rstd = carve(8192, [128, 512], F32)
            lnv = carve(10240, [128, 512], F32)
            outb = [carve(16384 + i * 16384, [128, KC, 512], F32) for i in range(2)]
            for tt_ in range(4):
                sl = slice(tt_ * 512, (tt_ + 1) * 512)
                ob = outb[tt_ % 2]
                rms_norm_tile(xres[:, :, sl], 512, PV_FING, ob, f"outb{tt_ % 2}", sq, rstd, lnv, 7, XR)
                dma("sp", yT_v[:, :, sl], ob, [f"outb{tt_ % 2}"], ["yT"])
        else:
            for c in range(KC):
                dma("sp", yT_v[:, c, :], xres[:, c, :], [f"x{c}"], ["yT"])
        fw = S.final_wait("sp")

        @block.tensor
        def _(e):
            S.emit("pe", e)

        @block.scalar
        def _(e):
            S.emit("act", e)

        @block.vector
        def _(e):
            S.emit("dve", e)

        @block.gpsimd
        def _(e):
            S.emit("pool", e)

        @block.sync
        def _(e):
            S.emit("sp", e)
            for s, v in fw:
                e.wait_ge(sems[s], v)

    return nc


def bass_gate_ap(w_in_d, l, dc):
    v = w_in_d[l].rearrange("(k p) c -> p k c", p=128)[:, :, 4096:7168]
    return v.rearrange("p k (n c) -> p k n c", n=3)[:, :, :, dc * 128:(dc + 1) * 128]


_CACHE = {}


def _owned_blocks(role):
    out = []
    for j in range(8):
        out += ([4 * j, 4 * j + 3] if role == 0 else [4 * j + 1, 4 * j + 2])
    return out


def _prep_inputs(inp):
    f = np.float32
    x = np.asarray(inp["x"], f)
    mem = np.asarray(inp["mem"], f)
    gv = lambda k: np.asarray(inp[k], f)

    def pcols(a):
        a = a.reshape(-1, 8, 128)
        return np.ascontiguousarray(a.transpose(2, 0, 1).reshape(128, -1))

    pvec_base = np.zeros((128, NPV), f)
    pvec_base[:, PV_MIXG:PV_MIXG + 16] = pcols(gv("norm_mix_g"))
    pvec_base[:, PV_XAG:PV_XAG + 16] = pcols(gv("norm_xa_g"))
    pvec_base[:, PV_MEMG:PV_MEMG + 16] = pcols(gv("mem_norm_g"))
    pvec_base[:, PV_FFNG:PV_FFNG + 16] = pcols(gv("norm_ffn_g"))
    pvec_base[:, PV_FING:PV_FING + 8] = pcols(gv("final_g")[None])
    cw = gv("conv_w").reshape(L * 3, 4, 128)
    pvec_base[:, PV_CONV:PV_CONV + 24] = cw.transpose(2, 0, 1).reshape(128, 24)

    bc = np.zeros((128, NBC), f)
    for l_ in range(L):
        o = l_ * 1536
        bc[:, o:o + 512] = np.broadcast_to(gv("sgu_ln_g")[l_].reshape(1, -1), (128, 512))
        bc[:, o + 512:o + 1024] = np.broadcast_to(gv("sgu_ln_b")[l_].reshape(1, -1), (128, 512))
        bc[:, o + 1024:o + 1536] = np.broadcast_to(gv("b_spatial")[l_].reshape(1, -1), (128, 512))
    wspT = np.ascontiguousarray(gv("w_spatial").transpose(3, 0, 1, 2).reshape(128, L * 4 * 128))

    j = np.arange(128)
    negtri = -(j[:, None] >= j[None, :]).astype(f)
    sgum = ((j[None, :] // 64) >= (j[:, None] // 64)).astype(f)
    diag = (j[:, None] < j[None, :]).astype(f)
    full = np.ones((128, 128), f)
    none = np.zeros((128, 128), f)
    amasks = {0: [[diag, none], [full, diag]], 1: [[full, diag], [diag, none]]}

    shared = {k: np.ascontiguousarray(gv(k)) for k in
              ["w_in", "w_branch", "w_out", "w_q_xa", "w_k_xa", "w_v_xa", "w_o_xa",
               "w_gate_ffn", "w_up_ffn", "w_down_ffn"]}
    in_maps = []
    for core in range(8):
        b, role = divmod(core, 2)
        blocks = _owned_blocks(role)
        xb = x[b].reshape(32, 128, D)[blocks].reshape(T, D)
        cst = np.zeros((128, NCST), f)
        cst[:, CS_NEGTRI:CS_NEGTRI + 128] = negtri
        cst[:, CS_SGUM:CS_SGUM + 128] = sgum
        for par in range(2):
            for w in range(2):
                o = CS_AMASK + (par * 2 + w) * 128
                cst[:, o:o + 128] = amasks[role][par][w]
        pv = pvec_base.copy()
        pv[:, PV_SEL] = 1.0 if role == 0 else 0.0
        pv[:, PV_SEL + 1] = 0.0 if role == 0 else 1.0
        m = dict(shared)
        m.update({"xT": np.ascontiguousarray(xb.T), "memT": np.ascontiguousarray(mem[b].T),
                  "pvec": pv, "cst": cst, "bc": bc, "wspT": wspT})
        in_maps.append(m)
    return in_maps


def _assemble(results):
    out = np.zeros((4, 4096, D), np.float32)
    for core in range(8):
        b, role = divmod(core, 2)
        blocks = _owned_blocks(role)
        y = np.asarray(results[core]["yT"]).T.reshape(NB, 128, D)
        out[b].reshape(32, 128, D)[blocks] = y
    return out


FUSED = True


def kernel(**inputs):
    in_maps = _prep_inputs(inputs)
    if FUSED:
        if "nc" not in _CACHE:
            _CACHE["nc"] = build_program()
        res = run_bass_kernel_spmd(_CACHE["nc"], in_maps, core_ids=list(range(8)))
        return _assemble(res.results)
    if "nc0" not in _CACHE:
        _CACHE["nc0"] = build_program(layers=(0,), final=False)
        _CACHE["nc1"] = build_program(layers=(1,), final=True)
    res0 = run_bass_kernel_spmd(_CACHE["nc0"], in_maps, core_ids=list(range(8)))
    for c in range(8):
        in_maps[c]["xT"] = np.ascontiguousarray(np.asarray(res0.results[c]["yT"], np.float32))
    res1 = run_bass_kernel_spmd(_CACHE["nc1"], in_maps, core_ids=list(range(8)))
    return _assemble(res1.results)
```
